# Optimizing a Trainium2 kernel written in Bass

```python
import jax, jax.numpy as jnp
from jax import lax
import numpy as np

D_MODEL = 1024
BATCH = 16
SEQ = 4096
DEPTH = 4

GRID_W = 64
CTX_LEN = 256
N_MIXERS = 2
N_GLA_LAYERS = (DEPTH + 1) // 2
N_MLA_LAYERS = DEPTH // 2
EPS = 1e-6

GLA_HEADS = 4
GLA_DK = D_MODEL // 2 // GLA_HEADS
GLA_DV = D_MODEL // GLA_HEADS
GLA_GATE_RANK = 16
GLA_GATE_NORMALIZER = 16.0
GLA_CHUNK = 64
GLA_IN = 2 * GLA_HEADS * GLA_DK + 2 * GLA_HEADS * GLA_DV

MLA_HEADS = 8
MLA_NOPE = 128
MLA_ROPE = 64
MLA_QK = MLA_NOPE + MLA_ROPE
MLA_V = 128
MLA_Q_RANK = 384
MLA_KV_RANK = 256
MLA_DOWN = MLA_Q_RANK + MLA_KV_RANK + MLA_ROPE
ROPE_THETA = 10000.0
ROPE_FREQS = MLA_ROPE // 4
Q_BLOCK = 128

D_FF = 2816
CONV_W = 3

kernel_name = 'hybrid_gla_mla_convffn_prefix_dit'


def rmsnorm(x, g):
    xf = x.astype(jnp.float32)
    y = xf * lax.rsqrt(jnp.mean(xf * xf, axis=-1, keepdims=True) + EPS)
    return (y * g.astype(jnp.float32)).astype(x.dtype)


def ada_modulation(cond, w, b):
    m = jax.nn.silu(cond) @ w + b
    return jnp.split(m, 6, axis=-1)


def modulate(x, g, shift, scale):
    return rmsnorm(x, g) * (1 + scale[..., None, :]) + shift[..., None, :]


def axial_rope_tables(length):
    rows = length // GRID_W
    t = jnp.arange(rows * GRID_W)
    row = (t // GRID_W).astype(jnp.float32)
    col = (t % GRID_W).astype(jnp.float32)
    inv = ROPE_THETA ** (-jnp.arange(ROPE_FREQS, dtype=jnp.float32) / ROPE_FREQS)
    ang = jnp.stack([row[:, None] * inv, col[:, None] * inv], axis=1)
    return jnp.cos(ang)[:, None], jnp.sin(ang)[:, None]


def apply_axial_rope(x, cos, sin):
    xr = x.astype(jnp.float32).reshape(*x.shape[:-1], 2, 2, ROPE_FREQS)
    x1, x2 = xr[..., 0, :], xr[..., 1, :]
    out = jnp.stack([x1 * cos - x2 * sin, x1 * sin + x2 * cos], axis=-2)
    return out.reshape(x.shape).astype(x.dtype)


def gla_project(h, w_in):
    B, L, _ = h.shape
    q, k, v, r = jnp.split(h @ w_in, [GLA_HEADS * GLA_DK, 2 * GLA_HEADS * GLA_DK,
                                      2 * GLA_HEADS * GLA_DK + GLA_HEADS * GLA_DV], axis=-1)
    q = q.reshape(B, L, GLA_HEADS, GLA_DK) * (GLA_DK ** -0.5)
    k = k.reshape(B, L, GLA_HEADS, GLA_DK)
    v = v.reshape(B, L, GLA_HEADS, GLA_DV)
    return q, k, v, r


def gla_log_decay(h, w1, w2, b):
    z = ((h @ w1) @ w2 + b).astype(jnp.float32)
    return (jax.nn.log_sigmoid(z) / GLA_GATE_NORMALIZER).reshape(*h.shape[:-1], GLA_HEADS, GLA_DK)


def gla_chunked(q, k, v, g, s0):
    B, L, H, _ = q.shape
    n = L // GLA_CHUNK
    def chunks(a):
        return a.reshape(B, n, GLA_CHUNK, H, a.shape[-1]).astype(jnp.float32)
    qc, kc, vc, gc = chunks(q), chunks(k), chunks(v), chunks(g)
    b = jnp.cumsum(gc, axis=2)
    b_last = b[:, :, -1:]
    q_dec = qc * jnp.exp(b)
    k_intra = kc * jnp.exp(-b)
    k_state = kc * jnp.exp(b_last - b)
    mask = jnp.tril(jnp.ones((GLA_CHUNK, GLA_CHUNK), dtype=bool))
    scores = jnp.where(mask, jnp.einsum('bnthd,bnshd->bnhts', q_dec, k_intra), 0.0)
    o_intra = jnp.einsum('bnhts,bnshv->bnthv', scores, vc)

    def step(S, xs):
        qd, ks, vv, bl = xs
        o = jnp.einsum('bthd,bhdv->bthv', qd, S)
        S = S * jnp.exp(bl)[..., None] + jnp.einsum('bthd,bthv->bhdv', ks, vv)
        return S, o

    xs = (jnp.moveaxis(q_dec, 1, 0), jnp.moveaxis(k_state, 1, 0),
          jnp.moveaxis(vc, 1, 0), jnp.moveaxis(b_last[:, :, 0], 1, 0))
    s_final, o_inter = lax.scan(step, s0.astype(jnp.float32), xs)
    o = o_intra + jnp.moveaxis(o_inter, 0, 1)
    return o.reshape(B, L, H, v.shape[-1]).astype(v.dtype), s_final


def gla_bidirectional(q, k, v, g_fwd, g_bwd, s_fwd0, s_bwd0):
    o_f, s_f = gla_chunked(q, k, v, g_fwd, s_fwd0)
    flip = lambda a: jnp.flip(a, axis=1)
    o_b, s_b = gla_chunked(flip(q), flip(k), flip(v), flip(g_bwd), s_bwd0)
    return o_f + flip(o_b), s_f, s_b


def gla_output(o, r, out_norm, w_out):
    B, L = o.shape[:2]
    o = rmsnorm(o, out_norm).reshape(B, L, GLA_HEADS * GLA_DV)
    return (o * jax.nn.silu(r)) @ w_out


def gla_mixer(h, hc, w_in, gate_w1, gate_w2, gate_b, out_norm, w_out, need_ctx_out):
    B = h.shape[0]
    s0 = jnp.zeros((B, GLA_HEADS, GLA_DK, GLA_DV), jnp.float32)
    qc, kc, vc, rc = gla_project(hc, w_in)
    gfc = gla_log_decay(hc, gate_w1[0], gate_w2[0], gate_b[0])
    gbc = gla_log_decay(hc, gate_w1[1], gate_w2[1], gate_b[1])
    oc, s_fwd, s_bwd = gla_bidirectional(qc, kc, vc, gfc, gbc, s0, s0)
    q, k, v, r = gla_project(h, w_in)
    gf = gla_log_decay(h, gate_w1[0], gate_w2[0], gate_b[0])
    gb = gla_log_decay(h, gate_w1[1], gate_w2[1], gate_b[1])
    o, _, _ = gla_bidirectional(q, k, v, gf, gb, s_fwd, s_bwd)
    out = gla_output(o, r, out_norm, w_out)
    out_c = gla_output(oc, rc, out_norm, w_out) if need_ctx_out else None
    return out, out_c


def split_norm(t, g):
    return jnp.concatenate([rmsnorm(t[..., :MLA_NOPE], g[:MLA_NOPE]),
                            rmsnorm(t[..., MLA_NOPE:], g[MLA_NOPE:])], axis=-1)


def mla_qkv(h, w_down, q_lora_norm, kv_lora_norm, w_uq, w_ukv, q_norm, k_norm, rope, want_q):
    B, L, _ = h.shape
    c_q, c_kv, k_pe = jnp.split(h @ w_down, [MLA_Q_RANK, MLA_Q_RANK + MLA_KV_RANK], axis=-1)
    kv = (rmsnorm(c_kv, kv_lora_norm) @ w_ukv).reshape(B, L, MLA_HEADS, MLA_NOPE + MLA_V)
    k_nope, v = kv[..., :MLA_NOPE], kv[..., MLA_NOPE:]
    k_nope = rmsnorm(k_nope, k_norm[:MLA_NOPE])
    k_pe = rmsnorm(k_pe, k_norm[MLA_NOPE:])[:, :, None, :]
    if rope is not None:
        k_pe = apply_axial_rope(k_pe, *rope)
    k = jnp.concatenate([k_nope, jnp.broadcast_to(k_pe, (B, L, MLA_HEADS, MLA_ROPE))], axis=-1)
    q = None
    if want_q:
        q = split_norm((rmsnorm(c_q, q_lora_norm) @ w_uq).reshape(B, L, MLA_HEADS, MLA_QK), q_norm)
        if rope is not None:
            q = jnp.concatenate([q[..., :MLA_NOPE], apply_axial_rope(q[..., MLA_NOPE:], *rope)], axis=-1)
    return q, k, v


def block_softmax_attention(q, k, v):
    B, L, H, Dh = q.shape
    nb = L // Q_BLOCK
    scale = Dh ** -0.5
    qb = jnp.moveaxis(q.reshape(B, nb, Q_BLOCK, H, Dh), 1, 0)
    def one(qblk):
        s = jnp.einsum('bqhd,bkhd->bhqk', qblk, k).astype(jnp.float32) * scale
        p = jax.nn.softmax(s, axis=-1).astype(v.dtype)
        return jnp.einsum('bhqk,bkhv->bqhv', p, v)
    o = lax.map(one, qb)
    return jnp.moveaxis(o, 0, 1).reshape(B, L, H, v.shape[-1])


def mla_mixer(h, hc, w_down, q_lora_norm, kv_lora_norm, w_uq, w_ukv, q_norm, k_norm, w_out, rope, need_ctx_out):
    B, L, _ = h.shape
    qc, kc, vc = mla_qkv(hc, w_down, q_lora_norm, kv_lora_norm, w_uq, w_ukv, q_norm, k_norm, None, need_ctx_out)
    q, k, v = mla_qkv(h, w_down, q_lora_norm, kv_lora_norm, w_uq, w_ukv, q_norm, k_norm, rope, True)
    o = block_softmax_attention(q, jnp.concatenate([kc, k], axis=1), jnp.concatenate([vc, v], axis=1))
    out = o.reshape(B, L, MLA_HEADS * MLA_V) @ w_out
    out_c = None
    if need_ctx_out:
        oc = block_softmax_attention(qc, kc, vc)
        out_c = oc.reshape(B, hc.shape[1], MLA_HEADS * MLA_V) @ w_out
    return out, out_c


def depthwise_conv3(u, w, b):
    up = jnp.pad(u, ((0, 0), (1, 1), (0, 0)))
    return up[:, :-2] * w[0] + up[:, 1:-1] * w[1] + up[:, 2:] * w[2] + b


def conv_ffn(h, w_up, conv_w, conv_b, w_down):
    u = depthwise_conv3(h @ w_up, conv_w, conv_b)
    a, g = jnp.split(u, 2, axis=-1)
    return (jax.nn.silu(g) * a) @ w_down


def setup_inputs(seed: int = 0) -> dict:
    key = jax.random.key(seed)
    ks = jax.random.split(key, 32)
    nrm = lambda k, shape, s: jax.random.normal(k, shape, jnp.float32) * s
    gain = lambda k, shape: 1.0 + 0.05 * jax.random.normal(k, shape, jnp.float32)
    return {
        'x': nrm(ks[0], (BATCH, SEQ, D_MODEL), 1.0),
        'c': nrm(ks[1], (BATCH, D_MODEL), 1.0),
        'ctx': nrm(ks[2], (BATCH, CTX_LEN, D_MODEL), 1.0),
        'c_ctx': nrm(ks[3], (D_MODEL,), 1.0),
        'w_ada': nrm(ks[4], (DEPTH, D_MODEL, 6 * D_MODEL), 0.5 * D_MODEL ** -0.5),
        'b_ada': nrm(ks[5], (DEPTH, 6 * D_MODEL), 0.02),
        'norm_mix': gain(ks[6], (DEPTH, D_MODEL)),
        'norm_ffn': gain(ks[7], (DEPTH, D_MODEL)),
        'gla_w_in': nrm(ks[8], (N_GLA_LAYERS, D_MODEL, GLA_IN), D_MODEL ** -0.5),
        'gla_gate_w1': nrm(ks[9], (N_GLA_LAYERS, 2, D_MODEL, GLA_GATE_RANK), D_MODEL ** -0.5),
        'gla_gate_w2': nrm(ks[10], (N_GLA_LAYERS, 2, GLA_GATE_RANK, GLA_HEADS * GLA_DK), GLA_GATE_RANK ** -0.5),
        'gla_gate_b': nrm(ks[11], (N_GLA_LAYERS, 2, GLA_HEADS * GLA_DK), 0.1),
        'gla_out_norm': gain(ks[12], (N_GLA_LAYERS, GLA_DV)),
        'gla_w_out': nrm(ks[13], (N_GLA_LAYERS, GLA_HEADS * GLA_DV, D_MODEL), (GLA_HEADS * GLA_DV) ** -0.5),
        'mla_w_down': nrm(ks[14], (N_MLA_LAYERS, D_MODEL, MLA_DOWN), D_MODEL ** -0.5),
        'mla_q_lora_norm': gain(ks[15], (N_MLA_LAYERS, MLA_Q_RANK)),
        'mla_kv_lora_norm': gain(ks[16], (N_MLA_LAYERS, MLA_KV_RANK)),
        'mla_w_uq': nrm(ks[17], (N_MLA_LAYERS, MLA_Q_RANK, MLA_HEADS * MLA_QK), MLA_Q_RANK ** -0.5),
        'mla_w_ukv': nrm(ks[18], (N_MLA_LAYERS, MLA_KV_RANK, MLA_HEADS * (MLA_NOPE + MLA_V)), MLA_KV_RANK ** -0.5),
        'mla_q_norm': gain(ks[19], (N_MLA_LAYERS, MLA_QK)),
        'mla_k_norm': gain(ks[20], (N_MLA_LAYERS, MLA_QK)),
        'mla_w_out': nrm(ks[21], (N_MLA_LAYERS, MLA_HEADS * MLA_V, D_MODEL), (MLA_HEADS * MLA_V) ** -0.5),
        'ffn_w_up': nrm(ks[22], (DEPTH, D_MODEL, 2 * D_FF), D_MODEL ** -0.5),
        'ffn_conv_w': nrm(ks[23], (DEPTH, CONV_W, 2 * D_FF), CONV_W ** -0.5),
        'ffn_conv_b': nrm(ks[24], (DEPTH, 2 * D_FF), 0.02),
        'ffn_w_down': nrm(ks[25], (DEPTH, D_FF, D_MODEL), D_FF ** -0.5),
    }


def reference(x, c, ctx, c_ctx, w_ada, b_ada, norm_mix, norm_ffn,
              gla_w_in, gla_gate_w1, gla_gate_w2, gla_gate_b, gla_out_norm, gla_w_out,
              mla_w_down, mla_q_lora_norm, mla_kv_lora_norm, mla_w_uq, mla_w_ukv,
              mla_q_norm, mla_k_norm, mla_w_out,
              ffn_w_up, ffn_conv_w, ffn_conv_b, ffn_w_down):
    rope = axial_rope_tables(x.shape[1])
    xc = ctx
    for i in range(DEPTH):
        last = i == DEPTH - 1
        j = i // N_MIXERS
        sh1, sc1, g1, sh2, sc2, g2 = ada_modulation(c, w_ada[i], b_ada[i])
        csh1, csc1, cg1, csh2, csc2, cg2 = ada_modulation(c_ctx, w_ada[i], b_ada[i])
        h = modulate(x, norm_mix[i], sh1, sc1)
        hc = modulate(xc, norm_mix[i], csh1, csc1)
        if i % N_MIXERS == 0:
            o, oc = gla_mixer(h, hc, gla_w_in[j], gla_gate_w1[j], gla_gate_w2[j], gla_gate_b[j],
                              gla_out_norm[j], gla_w_out[j], not last)
        else:
            o, oc = mla_mixer(h, hc, mla_w_down[j], mla_q_lora_norm[j], mla_kv_lora_norm[j],
                              mla_w_uq[j], mla_w_ukv[j], mla_q_norm[j], mla_k_norm[j], mla_w_out[j],
                              rope, not last)
        x = x + g1[:, None, :] * o
        x = x + g2[:, None, :] * conv_ffn(modulate(x, norm_ffn[i], sh2, sc2),
                                          ffn_w_up[i], ffn_conv_w[i], ffn_conv_b[i], ffn_w_down[i])
        if not last:
            xc = xc + cg1 * oc
            xc = xc + cg2 * conv_ffn(modulate(xc, norm_ffn[i], csh2, csc2),
                                     ffn_w_up[i], ffn_conv_w[i], ffn_conv_b[i], ffn_w_down[i])
    return x
```

```python
import numpy as np
from contextlib import ExitStack
import concourse.bass as bass
import concourse.mybir as mybir
from concourse.bass_utils import run_bass_kernel_spmd

F32 = mybir.dt.float32
BF16 = mybir.dt.bfloat16
AF = mybir.ActivationFunctionType
ALU = mybir.AluOpType
AX = mybir.AxisListType

D = 1024
L = 4096
LC = 256
NT = L + LC
DFF = 2816
NJ = 22
EPS = 1e-6
ENGS = ["pe", "act", "dve", "pool", "sp"]
BLK = {"pe": "tensor", "act": "scalar", "dve": "vector", "pool": "gpsimd", "sp": "sync"}
NDSEM = {"sp": 24, "pool": 10, "act": 6}


class Buf:
    __slots__ = ("w", "r")

    def __init__(self):
        self.w = None
        self.r = {}


class T:
    def __init__(self, h, nsub=0):
        self.h = h
        self.b = Buf()
        self.sub = [Buf() for _ in range(nsub)]

    def __getitem__(self, idx):
        return self.h[idx]


class Kern:
    def __init__(self, nc):
        self.nc = nc
        self.sem = {e: nc.alloc_semaphore(name="cs_" + e) for e in ENGS}
        self.cnt = {e: 0 for e in ENGS}
        self.dsem = {e: [nc.alloc_semaphore(name=f"ds_{e}{i}") for i in range(n)] for e, n in NDSEM.items()}
        self.dcnt = {e: [0] * n for e, n in NDSEM.items()}
        self.drr = {e: 0 for e in NDSEM}
        self.psum = [nc.alloc_psum_tensor(f"psb{i}", [128, 512], F32) for i in range(8)]
        self.nphase = 0


class Phase:
    def __init__(self, K, name):
        self.K = K
        self.nc = K.nc
        self.name = name
        self.es = ExitStack()
        self.prog = {e: [] for e in ENGS}
        self.known = {e: {} for e in ENGS}
        self.ps = [T(p) for p in K.psum]
        self.nsb = 0

    def sb(self, shape, dtype, nsub=0):
        self.nsb += 1
        h = self.es.enter_context(self.nc.sbuf_tensor(f"{self.name}{self.K.nphase}_{self.nsb}", list(shape), dtype))
        return T(h, nsub)

    def _bufs(self, lst):
        out = []
        for x in lst:
            if x is None:
                continue
            out.append(x.b if isinstance(x, T) else x)
        return out

    def _emit(self, e, fn, reads, writes, is_dma):
        K = self.K
        reads = self._bufs(reads)
        writes = self._bufs(writes)
        tag = "dma" if is_dma else e
        needs = {}

        def need(dep):
            sem, val, de = dep
            cur = needs.get(id(sem))
            if cur is None or cur[1] < val:
                needs[id(sem)] = (sem, val)

        for b in reads:
            if b.w is not None:
                if not (b.w[2] == "pe" and tag == "pe"):
                    need(b.w)
        for b in writes:
            if b.w is not None and (b.w[2] != tag or tag in ("dma", "pool", "act")):
                need(b.w)
            for r in b.r.values():
                if r[2] != tag or tag in ("dma", "pool", "act"):
                    need(r)
        if is_dma:
            k = K.drr[e]
            K.drr[e] = (k + 1) % len(K.dsem[e])
            if K.dcnt[e][k] > 0:
                need((K.dsem[e][k], K.dcnt[e][k], "dma"))
            K.dcnt[e][k] += 16
            sem, val, inc = K.dsem[e][k], K.dcnt[e][k], 16
        else:
            K.cnt[e] += 1
            sem, val, inc = K.sem[e], K.cnt[e], 1
        kn = self.known[e]
        for sid, (s, v) in needs.items():
            if kn.get(sid, 0) < v:
                kn[sid] = v
                self.prog[e].append((0, s, v))
        self.prog[e].append((1, fn, sem, inc))
        dep = (sem, val, tag)
        for b in writes:
            b.w = dep
            b.r = {}
        rk = (id(sem) if is_dma else tag)
        for b in reads:
            b.r[rk] = dep
        return dep

    def op(self, e, fn, reads=(), writes=()):
        return self._emit(e, fn, reads, writes, False)

    def dma(self, q, out, in_, reads=(), writes=(), **kw):
        return self._emit(q, lambda eng: eng.dma_start(out=out, in_=in_, **kw), reads, writes, True)

    def end(self):
        K = self.K
        for e in NDSEM:
            kn = self.known[e]
            for k, s in enumerate(K.dsem[e]):
                v = K.dcnt[e][k]
                if v > 0 and kn.get(id(s), 0) < v and self.prog[e]:
                    self.prog[e].append((0, s, v))
                    kn[id(s)] = v
        finals = {e: K.cnt[e] for e in ENGS}
        for e in ENGS:
            if not self.prog[e]:
                continue
            for e2 in ENGS:
                if e2 != e and self.prog[e2] and finals[e2] > 0 and self.known[e].get(id(K.sem[e2]), 0) < finals[e2]:
                    self.prog[e].append((0, K.sem[e2], finals[e2]))

        def run(eng, prog):
            for it in prog:
                if it[0] == 0:
                    eng.wait_ge(it[1], it[2])
                else:
                    ins = it[1](eng)
                    ins.then_inc(it[2], it[3])

        with self.nc.Block() as block:
            for e in ENGS:
                if self.prog[e]:
                    getattr(block, BLK[e])(lambda eng, p=self.prog[e]: run(eng, p))
        self.es.close()
        K.nphase += 1


def mm_group(ph, out_t, out_ap, pairs, reads, first=True, last=True):
    n = len(pairs)

    def fn(e):
        ins = None
        for i, (l, r) in enumerate(pairs):
            ins = e.matmul(out_ap, l, r, start=(first and i == 0), stop=(last and i == n - 1))
        return ins
    return ph.op("pe", fn, reads=reads, writes=[out_t])


def transposes(ph, out_t, items, ident, reads):
    def fn(e):
        ins = None
        for (o, i, n) in items:
            ins = e.transpose(o, i, ident[0:n, 0:n])
        return ins
    return ph.op("pe", fn, reads=reads, writes=[out_t])


def act(ph, out, in_, func, reads, writes, bias=None, scale=None, accum_out=None, eng="act"):
    kw = {}
    if bias is not None:
        kw["bias"] = bias
    if scale is not None:
        kw["scale"] = scale
    if accum_out is not None:
        kw["accum_out"] = accum_out
    return ph.op(eng, lambda e: e.activation(out=out, in_=in_, func=func, **kw), reads=reads, writes=writes)


def tt(ph, eng, out, in0, in1, op, reads, writes):
    return ph.op(eng, lambda e: e.tensor_tensor(out=out, in0=in0, in1=in1, op=op), reads=reads, writes=writes)


def ts(ph, eng, out, in0, s1, op0, reads, writes, s2=None, op1=None):
    if op1 is None and eng == "pool" and op0 == ALU.mult:
        s2, op1 = 0.0, ALU.add
    if op1 is None:
        return ph.op(eng, lambda e: e.tensor_scalar(out=out, in0=in0, scalar1=s1, scalar2=None, op0=op0),
                     reads=reads, writes=writes)
    return ph.op(eng, lambda e: e.tensor_scalar(out=out, in0=in0, scalar1=s1, scalar2=s2, op0=op0, op1=op1),
                 reads=reads, writes=writes)


def stt(ph, out, in0, scalar, in1, op0, op1, reads, writes):
    return ph.op("dve", lambda e: e.scalar_tensor_tensor(out=out, in0=in0, scalar=scalar, in1=in1, op0=op0, op1=op1),
                 reads=reads, writes=writes)


def cp(ph, eng, out, in_, reads, writes):
    if eng == "act":
        return ph.op("act", lambda e: e.activation(out=out, in_=in_, func=AF.Copy, scale=1.0), reads=reads, writes=writes)
    return ph.op(eng, lambda e: e.tensor_copy(out=out, in_=in_), reads=reads, writes=writes)


def memset(ph, eng, ap, val, writes):
    return ph.op(eng, lambda e: e.memset(ap, val), writes=writes)


def phase_norm(K, G, src, ntok, dst, dcol0, A, Bv):
    ph = Phase(K, "nm")
    xt = [ph.sb([128, 1024], F32) for _ in range(3)]
    sq = ph.sb([128, 1024], F32)
    ss = [ph.sb([128, 1], F32) for _ in range(3)]
    hsb = [ph.sb([128, 8, 512], BF16, nsub=8) for _ in range(2)]
    dstv = dst.rearrange("(k p) t -> p k t", p=128)
    ti = 0
    for gi, g0 in enumerate(range(0, ntok, 512)):
        H = hsb[gi % 2]
        gn = min(512, ntok - g0)
        for t0 in range(g0, g0 + gn, 128):
            X, S = xt[ti % 3], ss[ti % 3]
            n = min(128, ntok - t0)
            ph.dma("sp", X[0:n, :], src[t0:t0 + n, :], writes=[X])
            act(ph, sq[0:n, :], X[0:n, :], AF.Square, [X], [sq, S], accum_out=S[0:n, :])
            act(ph, S[0:n, :], S[0:n, :], AF.Sqrt, [S], [S], bias=G.eps[0:n, :], scale=1.0 / D)
            ph.op("dve", lambda e, S=S, n=n: e.reciprocal(out=S[0:n, :], in_=S[0:n, :]), reads=[S], writes=[S])
            act(ph, X[0:n, :], X[0:n, :], AF.Copy, [X, S], [X], scale=S[0:n, :])
            for half in range(2):
                P = ph.ps[(ti * 2 + half) % 8]
                transposes(ph, P, [(P[:, q * 128:q * 128 + n], X[0:n, (half * 4 + q) * 128:(half * 4 + q + 1) * 128], n)
                                   for q in range(4)], G.ident, [X])
                for q in range(4):
                    k = half * 4 + q
                    c0 = t0 - g0
                    ts(ph, "dve", H[:, k, c0:c0 + n], P[:, q * 128:q * 128 + n], A[:, k:k + 1], ALU.mult,
                       [P], [H.sub[k]], s2=Bv[:, k:k + 1], op1=ALU.add)
            ti += 1
        ph.dma("sp", dstv[:, :, dcol0 + g0:dcol0 + g0 + gn], H[:, :, 0:gn], reads=[H] + H.sub)
    ph.end()


class Glob:
    def __init__(self, K, ident_d, masks_d=None):
        nc = K.nc
        self.ident = nc.alloc_sbuf_tensor("g_ident", [128, 128], F32)
        self.identb = nc.alloc_sbuf_tensor("g_identb", [128, 128], BF16)
        self.eps = nc.alloc_sbuf_tensor("g_eps", [128, 1], F32)
        self.ones = nc.alloc_sbuf_tensor("g_ones", [128, 128], BF16)
        self.one1 = nc.alloc_sbuf_tensor("g_one1", [128, 1], F32)
        self.onesf = nc.alloc_sbuf_tensor("g_onesf", [128, 64], F32)
        self.mask_f = nc.alloc_sbuf_tensor("g_maskf", [64, 64], F32)
        self.mask_b = nc.alloc_sbuf_tensor("g_maskb", [64, 64], F32)
        ph = Phase(K, "g")
        I = T(self.ident)
        ph.dma("sp", self.ident[:], ident_d[:, :], writes=[I])
        cp(ph, "dve", self.identb[:], self.ident[:], [I], [])
        memset(ph, "dve", self.eps[:], EPS, [])
        memset(ph, "dve", self.ones[:], 1.0, [])
        memset(ph, "dve", self.one1[:], 1.0, [])
        memset(ph, "dve", self.onesf[:], 1.0, [])
        if masks_d is not None:
            ph.dma("sp", self.mask_f[:], masks_d[0], writes=[])
            ph.dma("sp", self.mask_b[:], masks_d[1], writes=[])
        ph.end()


def resid_evac(ph, pd, M, G_t, xr, tmp, xo, x_rows, width=1024):
    for half in range(2):
        tt(ph, "dve", tmp[0:M, half * 512:(half + 1) * 512], pd[half][0:M, :], G_t[0:M, half * 512:(half + 1) * 512],
           ALU.mult, [pd[half], G_t], [tmp])
    tt(ph, "pool", xo[0:M, :], tmp[0:M, :], xr[0:M, :], ALU.add, [tmp, xr], [xo])
    ph.dma("pool", x_rows, xo[0:M, :], reads=[xo])


def phase_ffn(K, G, wup_d, wdn_d, cw, cb, jobs):
    ph = Phase(K, "ff")
    wdn = ph.sb([128, NJ, 1024], BF16)
    for q in range(2):
        ph.dma("sp", wdn[:, q * 11:(q + 1) * 11, :], wdn_d[:, q * 11:(q + 1) * 11, :], writes=[wdn])
    G2 = ph.sb([128, 1024], F32)
    hblk = [ph.sb([128, 8, 512], BF16) for _ in range(2)]
    wup = [ph.sb([128, 8, 256], BF16) for _ in range(3)]
    actb = ph.sb([128, NJ, 512], BF16)
    ta = [ph.sb([128, 512], F32) for _ in range(2)]
    tg = [ph.sb([128, 512], F32) for _ in range(2)]
    sg = [ph.sb([128, 512], F32) for _ in range(2)]
    xr = [ph.sb([128, 1024], F32) for _ in range(2)]
    tmp = [ph.sb([128, 1024], F32) for _ in range(2)]
    xo = [ph.sb([128, 1024], F32) for _ in range(2)]
    cnt = 0
    bi = 0
    mi = 0
    for (hT, n_tok, xd, grow) in jobs:
        ph.dma("sp", G2[:], grow.partition_broadcast(128), writes=[G2])
        hv = hT.rearrange("(k p) t -> p k t", p=128)
        for s0 in range(0, n_tok, 510):
            n = min(510, n_tok - s0)
            W = n + 2
            H = hblk[bi % 2]
            bi += 1
            lo, hi = max(s0 - 1, 0), min(s0 + n + 1, n_tok)
            if s0 == 0:
                memset(ph, "pool", H[:, :, 0:1], 0.0, [H])
            if s0 + n == n_tok:
                memset(ph, "pool", H[:, :, W - 1:W], 0.0, [H])
            c0 = lo - (s0 - 1)
            ph.dma("sp", H[:, :, c0:c0 + hi - lo], hv[:, :, lo:hi], writes=[H])
            for j in range(NJ):
                Wj = wup[cnt % 3]
                pa, pg = ph.ps[(2 * cnt) % 6], ph.ps[(2 * cnt + 1) % 6]
                A_, G_, S_ = ta[cnt % 2], tg[cnt % 2], sg[cnt % 2]
                cnt += 1
                ph.dma("sp", Wj[:], wup_d[:, j], writes=[Wj])
                mm_group(ph, pa, pa[:, 0:W], [(Wj[:, k, 0:128], H[:, k, 0:W]) for k in range(8)], [Wj, H])
                mm_group(ph, pg, pg[:, 0:W], [(Wj[:, k, 128:256], H[:, k, 0:W]) for k in range(8)], [Wj, H])
                act(ph, A_[:, 0:n], pa[:, 1:W - 1], AF.Copy, [pa], [A_], scale=cw[:, 1, j, 0:1])
                act(ph, G_[:, 0:n], pg[:, 1:W - 1], AF.Copy, [pg], [G_], scale=cw[:, 1, j, 1:2])
                stt(ph, A_[:, 0:n], pa[:, 0:n], cw[:, 0, j, 0:1], A_[:, 0:n], ALU.mult, ALU.add, [pa, A_], [A_])
                stt(ph, G_[:, 0:n], pg[:, 0:n], cw[:, 0, j, 1:2], G_[:, 0:n], ALU.mult, ALU.add, [pg, G_], [G_])
                stt(ph, A_[:, 0:n], pa[:, 2:W], cw[:, 2, j, 0:1], A_[:, 0:n], ALU.mult, ALU.add, [pa, A_], [A_])
                stt(ph, G_[:, 0:n], pg[:, 2:W], cw[:, 2, j, 1:2], G_[:, 0:n], ALU.mult, ALU.add, [pg, G_], [G_])
                act(ph, S_[:, 0:n], G_[:, 0:n], AF.Silu, [G_], [S_], bias=cb[:, j, 1:2])
                stt(ph, actb[:, j, 0:n], A_[:, 0:n], cb[:, j, 0:1], S_[:, 0:n], ALU.add, ALU.mult, [A_, S_], [actb])
            for m0 in range(0, n, 128):
                M = min(128, n - m0)
                XR, TM, XO = xr[mi % 2], tmp[mi % 2], xo[mi % 2]
                mi += 1
                rows = xd[s0 + m0:s0 + m0 + M, :]
                ph.dma("sp", XR[0:M, :], rows, writes=[XR])
                pd = [ph.ps[6], ph.ps[7]]
                for half in range(2):
                    mm_group(ph, pd[half], pd[half][0:M, :],
                             [(actb[:, j, m0:m0 + M], wdn[:, j, half * 512:(half + 1) * 512]) for j in range(NJ)],
                             [actb, wdn])
                resid_evac(ph, pd, M, G2, XR, TM, XO, rows)
    ph.end()


def phase_outproj(K, G, w_d, jobs):
    ph = Phase(K, "op")
    w = ph.sb([128, 8, 1024], BF16)
    ph.dma("sp", w[:], w_d, writes=[w])
    G1 = ph.sb([128, 1024], F32)
    ob = [ph.sb([128, 8, 512], BF16) for _ in range(2)]
    xr = [ph.sb([128, 1024], F32) for _ in range(2)]
    tmp = [ph.sb([128, 1024], F32) for _ in range(2)]
    xo = [ph.sb([128, 1024], F32) for _ in range(2)]
    gi = 0
    mi = 0
    for (oT, n_tok, xd, grow) in jobs:
        ph.dma("sp", G1[:], grow.partition_broadcast(128), writes=[G1])
        ov = oT.rearrange("(k p) t -> p k t", p=128)
        for g0 in range(0, n_tok, 512):
            gn = min(512, n_tok - g0)
            O = ob[gi % 2]
            gi += 1
            ph.dma("sp", O[:, :, 0:gn], ov[:, :, g0:g0 + gn], writes=[O])
            for m0 in range(0, gn, 128):
                M = min(128, gn - m0)
                XR, TM, XO = xr[mi % 2], tmp[mi % 2], xo[mi % 2]
                pd = [ph.ps[(mi % 4) * 2], ph.ps[(mi % 4) * 2 + 1]]
                mi += 1
                rows = xd[g0 + m0:g0 + m0 + M, :]
                ph.dma("sp", XR[0:M, :], rows, writes=[XR])
                for half in range(2):
                    mm_group(ph, pd[half], pd[half][0:M, :],
                             [(O[:, k, m0:m0 + M], w[:, k, half * 512:(half + 1) * 512]) for k in range(8)], [O, w])
                resid_evac(ph, pd, M, G1, XR, TM, XO, rows)
    ph.end()


SCALE_MLA = 192.0 ** -0.5


def rstd_inplace(ph, G, S_t, ap, n_feat):
    act(ph, ap, ap, AF.Sqrt, [S_t], [S_t], bias=G.eps[0:ap.shape[0], :], scale=1.0 / n_feat)
    ph.op("dve", lambda e: e.reciprocal(out=ap, in_=ap), reads=[S_t], writes=[S_t])


def phase_mla_qkv(K, G, L_, hT, wdn_d, wuq_d, wukv_d, vec, rows_d, cos_d, sin_d, qnT, qrT, knT, kpT, v_d, want_ctx_q):
    ph = Phase(K, "mq")
    wdn = ph.sb([128, 8, 768], BF16)
    wuq = ph.sb([128, 3, 1536], BF16)
    wukv = ph.sb([128, 2, 2048], BF16)
    ph.dma("sp", wdn[:], wdn_d, writes=[wdn])
    ph.dma("sp", wuq[:], wuq_d, writes=[wuq])
    ph.dma("sp", wukv[:], wukv_d, writes=[wukv])
    grep_ = ph.sb([128, 128], F32)
    ph.dma("sp", grep_[:], rows_d.partition_broadcast(128), writes=[grep_])
    qln, kvln, qnn, knn = vec[:, 192:195], vec[:, 195:197], vec[:, 197:198], vec[:, 198:199]
    hb = [ph.sb([128, 8, 512], BF16) for _ in range(2)]
    st_2 = [ph.sb([128, 4], F32) for _ in range(2)]
    cqn_2 = [ph.sb([128, 384], F32) for _ in range(2)]
    ckvn_2 = [ph.sb([128, 256], F32) for _ in range(2)]
    kpe_2 = [ph.sb([128, 128], F32) for _ in range(2)]
    kpe2_2 = [ph.sb([128, 128], F32) for _ in range(2)]
    for _t in kpe_2 + kpe2_2:
        memset(ph, "dve", _t[:, 64:128], 0.0, [_t])
    cqT_2 = [ph.sb([128, 3, 128], BF16) for _ in range(2)]
    ckvT_2 = [ph.sb([128, 2, 128], BF16) for _ in range(2)]
    kpTs_2 = [ph.sb([64, 128], BF16) for _ in range(2)]
    cs_2 = [ph.sb([128, 2, 32], F32) for _ in range(2)]
    sq_2 = [ph.sb([128, 1536], F32) for _ in range(2)]
    st2_2 = [ph.sb([128, 16], F32) for _ in range(2)]
    qn_2 = [ph.sb([128, 1536], F32) for _ in range(2)]
    qr2_2 = [ph.sb([128, 512], F32) for _ in range(2)]
    r1_2 = [ph.sb([128, 256], F32) for _ in range(2)]
    r2_2 = [ph.sb([128, 256], F32) for _ in range(2)]
    qnTs_2 = [ph.sb([128, 8, 128], BF16) for _ in range(2)]
    qrTs_2 = [ph.sb([128, 4, 128], BF16) for _ in range(2)]
    st3_2 = [ph.sb([128, 8], F32) for _ in range(2)]
    kn_2 = [ph.sb([128, 1024], F32) for _ in range(2)]
    knTs_2 = [ph.sb([128, 8, 128], BF16) for _ in range(2)]
    vs_2 = [ph.sb([128, 1024], BF16) for _ in range(2)]
    qsc = ph.sb([128, 1], F32)
    ts(ph, "dve", qsc[:], qnn, SCALE_MLA, ALU.mult, [], [qsc])
    hv = hT.rearrange("(k p) t -> p k t", p=128)
    for gi, g0 in enumerate(range(0, NT, 512)):
        gn = min(512, NT - g0)
        H = hb[gi % 2]
        ph.dma("sp", H[:, :, 0:gn], hv[:, :, g0:g0 + gn], writes=[H])
        for m0 in range(0, gn, 128):
            t0 = g0 + m0
            par = (t0 // 128) % 2
            st = st_2[par]
            cqn = cqn_2[par]
            ckvn = ckvn_2[par]
            kpe = kpe_2[par]
            kpe2 = kpe2_2[par]
            cqT = cqT_2[par]
            ckvT = ckvT_2[par]
            kpTs = kpTs_2[par]
            cs = cs_2[par]
            sq = sq_2[par]
            st2 = st2_2[par]
            qn = qn_2[par]
            qr2 = qr2_2[par]
            r1 = r1_2[par]
            r2 = r2_2[par]
            qnTs = qnTs_2[par]
            qrTs = qrTs_2[par]
            st3 = st3_2[par]
            kn = kn_2[par]
            knTs = knTs_2[par]
            vs = vs_2[par]
            ps = [ph.ps[(i + 4 * par) % 8] for i in range(8)]
            latent = t0 >= LC
            need_q = latent or want_ctx_q
            mm_group(ph, ps[0], ps[0][:, :], [(H[:, k, m0:m0 + 128], wdn[:, k, 0:512]) for k in range(8)], [H, wdn])
            mm_group(ph, ps[1], ps[1][:, 0:256], [(H[:, k, m0:m0 + 128], wdn[:, k, 512:768]) for k in range(8)], [H, wdn])
            act(ph, sq[:, 0:384], ps[0][:, 0:384], AF.Square, [ps[0]], [sq, st], accum_out=st[:, 0:1])
            act(ph, sq[:, 384:448], ps[0][:, 384:448], AF.Square, [ps[0]], [sq, st], accum_out=st[:, 1:2])
            act(ph, sq[:, 512:768], ps[1][:, 0:256], AF.Square, [ps[1]], [sq, st], accum_out=st[:, 2:3])
            rstd_inplace(ph, G, st, st[:, 0:1], 384)
            rstd_inplace(ph, G, st, st[:, 1:2], 64)
            rstd_inplace(ph, G, st, st[:, 2:3], 256)

            ts(ph, "dve", cqn[:], ps[0][:, 0:384], st[:, 0:1], ALU.mult, [ps[0], st], [cqn])
            ts(ph, "dve", ckvn[:], ps[1][:, 0:256], st[:, 2:3], ALU.mult, [ps[1], st], [ckvn])
            stt(ph, kpe[:, 0:64], ps[0][:, 384:448], st[:, 1:2], grep_[:, 64:128], ALU.mult, ALU.mult, [ps[0], st, grep_], [kpe])

            if latent:
                ph.dma("sp", cs[:, 0, :], cos_d[t0 - LC:t0 - LC + 128, :], writes=[cs])
                ph.dma("sp", cs[:, 1, :], sin_d[t0 - LC:t0 - LC + 128, :], writes=[cs])
                cosv = cs[:, 0, :].rearrange("p (a f) -> p a f", a=2)
                sinv = cs[:, 1, :].rearrange("p (a f) -> p a f", a=2)
                kv5 = kpe[:, 0:64].rearrange("p (a h f) -> p a h f", a=2, h=2)
                ko5 = kpe2[:, 0:64].rearrange("p (a h f) -> p a h f", a=2, h=2)
                x1, x2 = kv5[:, :, 0, :], kv5[:, :, 1, :]
                a1 = r1[:, 0:32].rearrange("p (a f) -> p a f", a=2)
                a2 = r2[:, 0:32].rearrange("p (a f) -> p a f", a=2)
                tt(ph, "pool", a1, x1, cosv, ALU.mult, [kpe, cs], [r1])
                tt(ph, "pool", a2, x2, sinv, ALU.mult, [kpe, cs], [r2])
                tt(ph, "pool", ko5[:, :, 0, :], a1, a2, ALU.subtract, [r1, r2], [kpe2])
                tt(ph, "pool", a1, x1, sinv, ALU.mult, [kpe, cs], [r1])
                tt(ph, "pool", a2, x2, cosv, ALU.mult, [kpe, cs], [r2])
                tt(ph, "pool", ko5[:, :, 1, :], a1, a2, ALU.add, [r1, r2], [kpe2])
                kp_src = kpe2
            else:
                kp_src = kpe

            transposes(ph, ps[2], [(ps[2][:, kk * 128:(kk + 1) * 128], cqn[:, kk * 128:(kk + 1) * 128], 128) for kk in range(3)],
                       G.ident, [cqn])
            transposes(ph, ps[3], [(ps[3][:, kk * 128:(kk + 1) * 128], ckvn[:, kk * 128:(kk + 1) * 128], 128) for kk in range(2)]
                       + [(ps[3][:, 256:384], kp_src[:, :], 128)], G.ident, [ckvn, kp_src])

            for kk in range(3):
                ts(ph, "dve", cqT[:, kk, :], ps[2][:, kk * 128:(kk + 1) * 128], qln[:, kk:kk + 1], ALU.mult, [ps[2]], [cqT])
            for kk in range(2):
                ts(ph, "dve", ckvT[:, kk, :], ps[3][:, kk * 128:(kk + 1) * 128], kvln[:, kk:kk + 1], ALU.mult, [ps[3]], [ckvT])

            cp(ph, "dve", kpTs[:, :], ps[3][0:64, 256:384], [ps[3]], [kpTs])

            ph.dma("pool", kpT[:, t0:t0 + 128], kpTs[:, :], reads=[kpTs])

            if need_q:
                for nb in range(3):
                    mm_group(ph, ps[4 + nb], ps[4 + nb][:, :], [(cqT[:, kk, :], wuq[:, kk, nb * 512:(nb + 1) * 512]) for kk in range(3)],
                             [cqT, wuq])
                for nb in range(3):
                    act(ph, sq[:, nb * 512:(nb + 1) * 512], ps[4 + nb][:, :], AF.Square, [ps[4 + nb]], [sq])
                ph.op("dve", lambda e, st2=st2, sq=sq: e.tensor_reduce(out=st2[:, 0:8], in_=sq[:, 0:1024].rearrange("p (h d) -> p h d", h=8),
                                                       axis=AX.X, op=ALU.add), reads=[sq], writes=[st2])
                ph.op("dve", lambda e, st2=st2, sq=sq: e.tensor_reduce(out=st2[:, 8:16], in_=sq[:, 1024:1536].rearrange("p (h d) -> p h d", h=8),
                                                       axis=AX.X, op=ALU.add), reads=[sq], writes=[st2])
                rstd_inplace(ph, G, st2, st2[:, 0:8], 128)
                rstd_inplace(ph, G, st2, st2[:, 8:16], 64)
                for nb in range(2):
                    tt(ph, "dve", qn[:, nb * 512:(nb + 1) * 512].rearrange("p (h d) -> p h d", h=4),
                       ps[4 + nb][:, :].rearrange("p (h d) -> p h d", h=4),
                       st2[:, nb * 4:(nb + 1) * 4].unsqueeze(2).broadcast_to([128, 4, 128]), ALU.mult, [ps[4 + nb], st2], [qn])
                qrv = qn[:, 1024:1536].rearrange("p (h d) -> p h d", h=8)
                tt(ph, "dve", qrv, ps[6][:, :].rearrange("p (h d) -> p h d", h=8),
                   st2[:, 8:16].unsqueeze(2).broadcast_to([128, 8, 64]), ALU.mult, [ps[6], st2], [qn])
                tt(ph, "pool", qrv, qrv, grep_[:, 0:64].unsqueeze(1).broadcast_to([128, 8, 64]), ALU.mult, [qn, grep_], [qn])
                if latent:
                    q5 = qn[:, 1024:1536].rearrange("p (h a s f) -> p h a s f", h=8, a=2, s=2)
                    o5 = qr2[:, :].rearrange("p (h a s f) -> p h a s f", h=8, a=2, s=2)
                    cosb = cs[:, 0, :].rearrange("p (a f) -> p a f", a=2).unsqueeze(1).broadcast_to([128, 8, 2, 16])
                    sinb = cs[:, 1, :].rearrange("p (a f) -> p a f", a=2).unsqueeze(1).broadcast_to([128, 8, 2, 16])
                    x1, x2 = q5[:, :, :, 0, :], q5[:, :, :, 1, :]
                    a1 = r1[:, :].rearrange("p (h a f) -> p h a f", h=8, a=2)
                    a2 = r2[:, :].rearrange("p (h a f) -> p h a f", h=8, a=2)
                    tt(ph, "pool", a1, x1, cosb, ALU.mult, [qn, cs], [r1])
                    tt(ph, "pool", a2, x2, sinb, ALU.mult, [qn, cs], [r2])
                    tt(ph, "pool", o5[:, :, :, 0, :], a1, a2, ALU.subtract, [r1, r2], [qr2])
                    tt(ph, "pool", a1, x1, sinb, ALU.mult, [qn, cs], [r1])
                    tt(ph, "pool", a2, x2, cosb, ALU.mult, [qn, cs], [r2])
                    tt(ph, "pool", o5[:, :, :, 1, :], a1, a2, ALU.add, [r1, r2], [qr2])
                    qr_src, qr_t = qr2[:, :], qr2
                else:
                    qr_src, qr_t = qn[:, 1024:1536], qn
                for hb_ in range(2):
                    P = ps[hb_]
                    transposes(ph, P, [(P[:, q * 128:(q + 1) * 128], qn[:, (hb_ * 4 + q) * 128:(hb_ * 4 + q + 1) * 128], 128)
                                       for q in range(4)], G.ident, [qn])
                    ts(ph, "dve", qnTs[:, hb_ * 4:(hb_ + 1) * 4, :], P[:, :].rearrange("p (h t) -> p h t", h=4), qsc[:, 0:1],
                       ALU.mult, [P, qsc], [qnTs])
                P = ps[2]
                transposes(ph, P, [(P[:, q * 128:(q + 1) * 128], qr_src[:, q * 128:(q + 1) * 128], 128) for q in range(4)],
                           G.ident, [qr_t])
                ts(ph, "dve", qrTs[:, :, :], P[:, :].rearrange("p (g t) -> p g t", g=4), SCALE_MLA, ALU.mult, [P], [qrTs])
                ph.dma("pool", qnT[:, :, t0:t0 + 128].rearrange("h p t -> p h t"), qnTs[:, :, :], reads=[qnTs])
                ph.dma("pool", qrT.rearrange("h d t -> (h d) t").rearrange("(g q) t -> q g t", q=128)[:, :, t0:t0 + 128], qrTs[:, :, :], reads=[qrTs])

            kb = [ps[4], ps[5], ps[6], ps[7]]
            for nb in range(4):
                mm_group(ph, kb[nb], kb[nb][:, :], [(ckvT[:, kk, :], wukv[:, kk, nb * 512:(nb + 1) * 512]) for kk in range(2)],
                         [ckvT, wukv])
            for nb in range(2):
                act(ph, sq[:, nb * 512:(nb + 1) * 512], kb[nb][:, :], AF.Square, [kb[nb]], [sq])
            ph.op("dve", lambda e, st3=st3, sq=sq: e.tensor_reduce(out=st3[:, 0:8], in_=sq[:, 0:1024].rearrange("p (h d) -> p h d", h=8),
                                                   axis=AX.X, op=ALU.add), reads=[sq], writes=[st3])
            rstd_inplace(ph, G, st3, st3[:, 0:8], 128)
            for nb in range(2):
                tt(ph, "dve", kn[:, nb * 512:(nb + 1) * 512].rearrange("p (h d) -> p h d", h=4),
                   kb[nb][:, :].rearrange("p (h d) -> p h d", h=4),
                   st3[:, nb * 4:(nb + 1) * 4].unsqueeze(2).broadcast_to([128, 4, 128]), ALU.mult, [kb[nb], st3], [kn])
                cp(ph, "act", vs[:, nb * 512:(nb + 1) * 512], kb[2 + nb][:, :], [kb[2 + nb]], [vs])
            ph.dma("pool", v_d[t0:t0 + 128, :], vs[:, :], reads=[vs])
            for hb_ in range(2):
                P = ps[hb_]
                transposes(ph, P, [(P[:, q * 128:(q + 1) * 128], kn[:, (hb_ * 4 + q) * 128:(hb_ * 4 + q + 1) * 128], 128)
                                   for q in range(4)], G.ident, [kn])
                ts(ph, "dve", knTs[:, hb_ * 4:(hb_ + 1) * 4, :], P[:, :].rearrange("p (h t) -> p h t", h=4), knn[:, 0:1],
                   ALU.mult, [P], [knTs])
            ph.dma("pool", knT[:, :, t0:t0 + 128].rearrange("h p t -> p h t"), knTs[:, :, :], reads=[knTs])
    ph.end()


def phase_mla_attn(K, G, qnT, qrT, knT, kpT, v_d, oT, want_ctx):
    ph = Phase(K, "ma")
    kp = ph.sb([64, NT], BF16)
    ph.dma("sp", kp[:], kpT, writes=[kp])
    kn = [ph.sb([128, NT], BF16) for _ in range(2)]
    vh = [ph.sb([128, NT // 128, 128], BF16) for _ in range(2)]
    qn = [ph.sb([128, NT], BF16) for _ in range(2)]
    qr = [ph.sb([64, NT], BF16) for _ in range(2)]
    pT = [ph.sb([128, 512], BF16) for _ in range(4)]
    rec = [ph.sb([128, 512], F32) for _ in range(2)]
    ob = [ph.sb([128, 512], BF16) for _ in range(2)]
    ps = ph.ps
    it = 0
    qt = 0
    NCH = NT // 128
    for h in range(8):
        KN, VH, QN, QR = kn[h % 2], vh[h % 2], qn[h % 2], qr[h % 2]
        ph.dma("sp", KN[:], knT[h], writes=[KN])
        ph.dma("sp", VH[:], v_d[:, h * 128:(h + 1) * 128].rearrange("(c p) v -> p c v", p=128), writes=[VH])
        ph.dma("sp", QN[:], qnT[h], writes=[QN])
        ph.dma("sp", QR[:], qrT[h], writes=[QR])
        tiles = [(LC + i * 512, 512, NCH) for i in range(L // 512)]
        if want_ctx:
            tiles = [(0, LC, LC // 128)] + tiles
        for (q0, nq, nch) in tiles:
            O, Dn = ps[4 + qt % 2], ps[6 + qt % 2]

            def qk(c, slot):
                S = ps[slot % 4]
                mm_group(ph, S, S[:, 0:nq], [(KN[:, c * 128:(c + 1) * 128], QN[:, q0:q0 + nq]),
                                             (kp[:, c * 128:(c + 1) * 128], QR[:, q0:q0 + nq])], [KN, QN, kp, QR])
            qk(0, it)
            for c in range(nch):
                S = ps[it % 4]
                PT = pT[it % 4]
                if c + 1 < nch:
                    qk(c + 1, it + 1)
                it += 1
                act(ph, PT[:, 0:nq], S[:, 0:nq], AF.Exp, [S], [PT])
                mm_group(ph, O, O[:, 0:nq], [(VH[:, c, :], PT[:, 0:nq])], [VH, PT], first=(c == 0), last=(c == nch - 1))
                mm_group(ph, Dn, Dn[:, 0:nq], [(G.ones[:, :], PT[:, 0:nq])], [PT], first=(c == 0), last=(c == nch - 1))
            R, OB = rec[qt % 2], ob[qt % 2]
            qt += 1
            ph.op("dve", lambda e, R=R, Dn=Dn, nq=nq: e.reciprocal(out=R[:, 0:nq], in_=Dn[:, 0:nq]), reads=[Dn], writes=[R])
            tt(ph, "dve", OB[:, 0:nq], O[:, 0:nq], R[:, 0:nq], ALU.mult, [O, R], [OB])
            ph.dma("pool", oT[h * 128:(h + 1) * 128, q0:q0 + nq], OB[:, 0:nq], reads=[OB])
    ph.end()


def kmaj(w):
    Kd, N = w.shape
    return np.ascontiguousarray(w.reshape(Kd // 128, 128, N).transpose(1, 0, 2))


def fmaj(v):
    return np.ascontiguousarray(v.reshape(-1, 128).T)


def lay_mla(w_down, w_uq, w_ukv):
    wd = np.concatenate([w_down[:, 0:384], w_down[:, 640:704], np.zeros((1024, 64), w_down.dtype), w_down[:, 384:640]], axis=1)
    uq = w_uq.reshape(384, 8, 192)
    uq = np.concatenate([uq[:, :, 0:128].reshape(384, 1024), uq[:, :, 128:192].reshape(384, 512)], axis=1)
    ukv = w_ukv.reshape(256, 8, 256)
    ukv = np.concatenate([ukv[:, :, 0:128].reshape(256, 1024), ukv[:, :, 128:256].reshape(256, 1024)], axis=1)
    return kmaj(wd), kmaj(uq), kmaj(ukv)


def lay_ffn(w_up, w_down, conv_w, conv_b):
    wa = w_up[:, :DFF].reshape(8, 128, NJ, 128)
    wg = w_up[:, DFF:].reshape(8, 128, NJ, 128)
    wup_l = np.ascontiguousarray(np.concatenate([wa, wg], axis=3).transpose(1, 2, 0, 3))
    wdn_l = np.ascontiguousarray(w_down.reshape(NJ, 128, 1024).transpose(1, 0, 2))
    cw_l = np.stack([conv_w[:, :DFF].reshape(3, NJ, 128), conv_w[:, DFF:].reshape(3, NJ, 128)], axis=-1)
    cw_l = cw_l.transpose(2, 0, 1, 3).reshape(128, 3 * NJ * 2)
    cb_l = np.stack([conv_b[:DFF].reshape(NJ, 128), conv_b[DFF:].reshape(NJ, 128)], axis=-1).transpose(1, 0, 2).reshape(128, NJ * 2)
    return wup_l, wdn_l, np.ascontiguousarray(cw_l), np.ascontiguousarray(cb_l)


def rope_tables():
    t = np.arange(L)
    row = (t // 64).astype(np.float32)
    col = (t % 64).astype(np.float32)
    inv = (10000.0 ** (-np.arange(16, dtype=np.float32) / 16)).astype(np.float32)
    ang = np.stack([row[:, None] * inv, col[:, None] * inv], axis=1).astype(np.float32)
    return np.cos(ang).reshape(L, 32).astype(np.float32), np.sin(ang).reshape(L, 32).astype(np.float32)


NCH = None


def phase_gla_proj(K, G, hT, win_d, w1_d, w2_d, vec, qkT, gT, v_d, sr_d):
    ph = Phase(K, "gp")
    win = ph.sb([128, 8, 3072], BF16)
    for q in range(4):
        ph.dma("sp", win[:, q * 2:(q + 1) * 2, :], win_d[:, q * 2:(q + 1) * 2, :], writes=[win])
    w1 = ph.sb([128, 8, 32], BF16)
    ph.dma("sp", w1[:], w1_d, writes=[w1])
    w2f = ph.sb([16, 2, 512], F32)
    w2 = ph.sb([16, 2, 512], BF16)
    ph.dma("sp", w2f[:], w2_d, writes=[w2f])
    cp(ph, "dve", w2[:], w2f[:], [w2f], [w2])
    negb = ph.sb([128, 8], F32)
    ts(ph, "dve", negb[:], vec[:, 192:200], -1.0, ALU.mult, [], [negb])
    hb = [ph.sb([128, 8, 512], BF16) for _ in range(2)]
    stg = [ph.sb([128, 512], F32) for _ in range(3)]
    hw1 = [ph.sb([16, 512], BF16) for _ in range(2)]
    e1 = [ph.sb([128, 512], F32) for _ in range(2)]
    vs = [ph.sb([128, 1024], BF16) for _ in range(2)]
    srs = [ph.sb([128, 1024], F32) for _ in range(2)]
    hv = hT.rearrange("(k p) t -> p k t", p=128)
    ps = ph.ps
    pi = 0
    si = 0
    for gi, g0 in enumerate(range(0, NT, 512)):
        gn = min(512, NT - g0)
        H = hb[gi % 2]
        ph.dma("sp", H[:, :, 0:gn], hv[:, :, g0:g0 + gn], writes=[H])
        for fc in range(8):
            P = ps[pi % 8]
            pi += 1
            ST = stg[si % 3]
            si += 1
            mm_group(ph, P, P[:, 0:gn], [(win[:, k, fc * 128:(fc + 1) * 128], H[:, k, 0:gn]) for k in range(8)], [win, H])
            if fc < 4:
                ts(ph, "dve", ST[:, 0:gn], P[:, 0:gn], 128.0 ** -0.5, ALU.mult, [P], [ST])
            else:
                cp(ph, "act", ST[:, 0:gn], P[:, 0:gn], [P], [ST])
            ph.dma("pool", qkT[fc * 128:(fc + 1) * 128, g0:g0 + gn], ST[:, 0:gn], reads=[ST])
        for d in range(2):
            P = ps[pi % 8]
            pi += 1
            HW = hw1[d]
            mm_group(ph, P, P[0:16, 0:gn], [(w1[:, k, d * 16:(d + 1) * 16], H[:, k, 0:gn]) for k in range(8)], [w1, H])
            cp(ph, "dve", HW[:, 0:gn], P[0:16, 0:gn], [P], [HW])
            for h in range(4):
                P2 = ps[pi % 8]
                pi += 1
                ST = stg[si % 3]
                si += 1
                E = e1[h % 2]
                mm_group(ph, P2, P2[:, 0:gn], [(w2[:, d, h * 128:(h + 1) * 128], HW[:, 0:gn])], [w2, HW])
                act(ph, E[:, 0:gn], P2[:, 0:gn], AF.Exp, [P2, negb], [E], bias=negb[:, d * 4 + h:d * 4 + h + 1], scale=-1.0)
                act(ph, E[:, 0:gn], E[:, 0:gn], AF.Ln, [E], [E], bias=G.one1[:, :], scale=1.0)
                ts(ph, "dve", ST[:, 0:gn], E[:, 0:gn], -1.0 / 16.0, ALU.mult, [E], [ST])
                ph.dma("pool", gT[d, h * 128:(h + 1) * 128, g0:g0 + gn], ST[:, 0:gn], reads=[ST])
        for m0 in range(0, gn, 128):
            t0 = g0 + m0
            VS, SR = vs[(t0 // 128) % 2], srs[(t0 // 128) % 2]
            for nb in range(4):
                P = ps[pi % 8]
                pi += 1
                mm_group(ph, P, P[:, :], [(H[:, k, m0:m0 + 128], win[:, k, 1024 + nb * 512:1024 + (nb + 1) * 512]) for k in range(8)],
                         [H, win])
                if nb < 2:
                    cp(ph, "dve", VS[:, nb * 512:(nb + 1) * 512], P[:, :], [P], [VS])
                else:
                    act(ph, SR[:, (nb - 2) * 512:(nb - 1) * 512], P[:, :], AF.Silu, [P], [SR])
            ph.dma("pool", v_d[t0:t0 + 128, :], VS[:, :], reads=[VS])
            ph.dma("pool", sr_d[t0:t0 + 128, :], SR[:, :], reads=[SR])
    ph.end()


def phase_gla_scan(K, G, qkT, gT, v_d, o_d):
    ph = Phase(K, "gs")
    nch = NT // 64
    ncc = LC // 64
    q = ph.sb([128, NT], F32)
    k = ph.sb([128, NT], F32)
    g = ph.sb([128, NT], F32)
    Pc = ph.sb([128, NT], F32)
    E = ph.sb([128, NT], F32)
    qd = ph.sb([128, NT], BF16)
    ki = ph.sb([128, NT], BF16)
    kitok = ph.sb([64, nch, 128], BF16)
    vh = ph.sb([64, nch, 256], BF16)
    dec = ph.sb([128, nch], F32)
    S = ph.sb([128, 256], F32)
    Sd = ph.sb([128, 256], F32)
    Sb = ph.sb([128, 256], BF16)
    sc = [ph.sb([64, 64], BF16) for _ in range(2)]
    osb = [ph.sb([64, 256], F32) for _ in range(3)]
    ps = ph.ps
    it = 0
    odb = [Buf() for _ in range(nch)]
    for h in range(4):
        ph.dma("sp", q[:], qkT[h * 128:(h + 1) * 128, :], writes=[q])
        ph.dma("sp", k[:], qkT[512 + h * 128:512 + (h + 1) * 128, :], writes=[k])
        ph.dma("sp", vh[:], v_d[:, h * 256:(h + 1) * 256].rearrange("(c p) v -> p c v", p=64), writes=[vh])
        for d in range(2):
            ph.dma("sp", g[:], gT[d, h * 128:(h + 1) * 128, :], writes=[g])
            for c in range(nch):
                ph.op("dve", lambda e, c=c: e.tensor_tensor_scan(out=Pc[:, c * 64:(c + 1) * 64], data0=G.onesf[:, 0:64],
                                                                 data1=g[:, c * 64:(c + 1) * 64], initial=0.0,
                                                                 op0=ALU.mult, op1=ALU.add), reads=[g], writes=[Pc])
            P3 = Pc[:].rearrange("p (c t) -> p c t", t=64)
            tot = P3[:, :, 63:64]
            act(ph, dec[:].unsqueeze(2), tot, AF.Exp, [Pc], [dec])
            if d == 0:
                bq, bq_t = Pc, Pc
            else:
                g3 = g[:].rearrange("p (c t) -> p c t", t=64)
                tt(ph, "pool", g[:], g[:], Pc[:], ALU.subtract, [g, Pc], [g])
                tt(ph, "pool", g3, g3, tot.broadcast_to([128, nch, 64]), ALU.add, [g, Pc], [g])
                bq, bq_t = g, g
            act(ph, E[:], bq[:], AF.Exp, [bq_t], [E])
            tt(ph, "dve", qd[:], q[:], E[:], ALU.mult, [q, E], [qd])
            act(ph, E[:], bq[:], AF.Exp, [bq_t], [E], scale=-1.0)
            tt(ph, "dve", ki[:], k[:], E[:], ALU.mult, [k, E], [ki])
            for c0 in range(0, nch, 8):
                nb = min(8, nch - c0)
                P = ps[4 + (c0 // 8) % 2]
                Pb = P[:, :].bitcast(BF16)
                transposes(ph, P, [(Pb[0:64, j * 128:(j + 1) * 128], ki[:, (c0 + j) * 64:(c0 + j + 1) * 64], 128) for j in range(nb)],
                           G.identb, [ki])
                cp(ph, "dve", kitok[:, c0:c0 + nb, :], Pb[0:64, 0:nb * 128].rearrange("p (j x) -> p j x", j=nb), [P], [kitok])
            memset(ph, "dve", S[:], 0.0, [S])
            memset(ph, "dve", Sb[:], 0.0, [Sb])
            order = list(range(nch)) if d == 0 else (list(range(ncc - 1, -1, -1)) + list(range(nch - 1, ncc - 1, -1)))
            mask = G.mask_f if d == 0 else G.mask_b
            prev_c = None
            for c in order:
                cs_ = slice(c * 64, (c + 1) * 64)
                A, B, C = ps[it % 2], ps[2 + it % 2], ps[6 + it % 2]
                SC, OS = sc[it % 2], osb[it % 3]
                it += 1
                mm_group(ph, A, A[0:64, 0:64], [(ki[:, cs_], qd[:, cs_])], [ki, qd])
                tt(ph, "dve", SC[:, :], A[0:64, 0:64], mask[0:64, 0:64], ALU.mult, [A], [SC])
                mm_group(ph, B, B[0:64, 0:256], [(SC[:, :], vh[:, c, :]), (qd[:, cs_], Sb[:, :])], [SC, vh, qd, Sb])
                mm_group(ph, C, C[:, 0:256], [(kitok[:, c, :], vh[:, c, :])], [kitok, vh])
                cprev = c if prev_c is None else prev_c
                stt(ph, S[:], S[:], dec[:, cprev:cprev + 1], C[:, 0:256], ALU.mult, ALU.add, [S, dec, C], [S])
                act(ph, Sb[:], S[:], AF.Copy, [S, dec], [Sb], scale=dec[:, c:c + 1])
                prev_c = c
                cp(ph, "dve", OS[:, :], B[0:64, 0:256], [B], [OS])
                dst = o_d[c * 64:(c + 1) * 64, h * 256:(h + 1) * 256]
                if d == 0:
                    ph.dma("pool", dst, OS[:, :], reads=[OS], writes=[odb[c]])
                else:
                    ph.dma("pool", dst, OS[:, :], reads=[OS], writes=[odb[c]], accum_op=ALU.add)
    ph.end()


def phase_gla_out(K, G, o_d, sr_d, vec, ogT):
    ph = Phase(K, "go")
    ob = [ph.sb([128, 1024], F32) for _ in range(2)]
    sb_ = [ph.sb([128, 1024], F32) for _ in range(2)]
    sq = ph.sb([128, 1024], F32)
    st = [ph.sb([128, 4], F32) for _ in range(2)]
    on = [ph.sb([128, 1024], F32) for _ in range(2)]
    og = [ph.sb([128, 8, 512], BF16) for _ in range(2)]
    ogv = ogT.rearrange("(k p) t -> p k t", p=128)
    onorm = vec[:, 200:202]
    ps = ph.ps
    ti = 0
    for gi, g0 in enumerate(range(0, NT, 512)):
        gn = min(512, NT - g0)
        OG = og[gi % 2]
        for m0 in range(0, gn, 128):
            t0 = g0 + m0
            O, SR, ST, ON = ob[ti % 2], sb_[ti % 2], st[ti % 2], on[ti % 2]
            ph.dma("sp", O[:], o_d[t0:t0 + 128, :], writes=[O])
            ph.dma("sp", SR[:], sr_d[t0:t0 + 128, :], writes=[SR])
            act(ph, sq[:], O[:], AF.Square, [O], [sq])
            ph.op("dve", lambda e, ST=ST: e.tensor_reduce(out=ST[:, 0:4], in_=sq[:, :].rearrange("p (h d) -> p h d", h=4),
                                                          axis=AX.X, op=ALU.add), reads=[sq], writes=[ST])
            rstd_inplace(ph, G, ST, ST[:, 0:4], 256)
            tt(ph, "dve", ON[:].rearrange("p (h d) -> p h d", h=4), O[:].rearrange("p (h d) -> p h d", h=4),
               ST[:, 0:4].unsqueeze(2).broadcast_to([128, 4, 256]), ALU.mult, [O, ST], [ON])
            tt(ph, "pool", ON[:], ON[:], SR[:], ALU.mult, [ON, SR], [ON])
            for half in range(2):
                P = ps[(ti * 2 + half) % 8]
                transposes(ph, P, [(P[:, q * 128:(q + 1) * 128], ON[:, (half * 4 + q) * 128:(half * 4 + q + 1) * 128], 128)
                                   for q in range(4)], G.ident, [ON])
                Pv = P[:, :].rearrange("p (h s t) -> p h s t", h=2, s=2)
                for s_ in range(2):
                    ts(ph, "dve", OG[:, half * 4:(half + 1) * 4, m0:m0 + 128].rearrange("p (h s) t -> p h s t", s=2)[:, :, s_, :],
                       Pv[:, :, s_, :], onorm[:, s_:s_ + 1], ALU.mult, [P], [OG])
            ti += 1
        ph.dma("pool", ogv[:, :, g0:g0 + gn], OG[:, :, 0:gn], reads=[OG])
    ph.end()


def lay_gla(w_in, w1, w2):
    w1c = np.concatenate([w1[0], w1[1]], axis=1)
    w2l = np.ascontiguousarray(w2.transpose(1, 0, 2))
    return kmaj(w_in), kmaj(w1c), w2l


def gla_masks():
    s_ = np.arange(64)[:, None]
    t_ = np.arange(64)[None, :]
    return np.stack([(t_ >= s_), (t_ <= s_)]).astype(np.float32)


def phase_ada(K, G, condT, wada_d, bada_d, m_dram, mT, modv, vec):
    ph = Phase(K, "ad")
    msb = ph.sb([4, 6144], F32)
    bt = ph.sb([4, 6144], F32)
    ph.dma("sp", bt[:], bada_d.partition_broadcast(4), writes=[bt])
    wt = [ph.sb([128, 8, 512], F32) for _ in range(2)]
    ps = ph.ps
    for n in range(12):
        W = wt[n % 2]
        ph.dma("sp", W[:], wada_d[:, :, n * 512:(n + 1) * 512], writes=[W])
        P = ps[n % 2]
        mm_group(ph, P, P[0:4, :], [(condT[:, k, :], W[:, k, :]) for k in range(8)], [W])
        tt(ph, "dve", msb[:, n * 512:(n + 1) * 512], P[0:4, :], bt[:, n * 512:(n + 1) * 512], ALU.add, [P, bt], [msb])
    ph.dma("sp", m_dram[:, :], msb[:], reads=[msb])
    PT = ps[2]
    transposes(ph, PT, [(PT[:, c * 4:(c + 1) * 4], msb[0:4, c * 128:(c + 1) * 128], 4) for c in range(48)], G.ident, [msb])
    MT = T(mT)
    cp(ph, "dve", mT[:].rearrange("p c j -> p (c j)"), PT[:, 0:192], [PT], [MT])
    MV = T(modv)
    for c in range(3):
        for (slot, jsc, jsh, goff) in ((0, 1, 0, 0), (2, 4, 3, 8)):
            ts(ph, "dve", modv[:, c, slot, :], mT[:, jsc * 8:(jsc + 1) * 8, c], 1.0, ALU.add, [MT], [MV])
            tt(ph, "dve", modv[:, c, slot, :], modv[:, c, slot, :], vec[:, goff:goff + 8], ALU.mult, [MV], [MV])
            cp(ph, "dve", modv[:, c, slot + 1, :], mT[:, jsh * 8:(jsh + 1) * 8, c], [MT], [MV])
    ph.end()


def cast_dram(ph, dst2, src2):
    F = src2.shape[1]
    for a in range(0, F, 8192):
        b = min(F, a + 8192)
        ph.dma("pool", dst2[:, a:b], src2[:, a:b], max_dma_last_dim=8192)


def build_program():
    nc = bass.Bass("TRN2", target_bir_lowering=False)

    def din(n, s, t=F32):
        return nc.dram_tensor(n, list(s), t, kind="ExternalInput").ap()

    def dsc(n, s, t):
        return nc.dram_tensor(n, list(s), t, kind="Internal").ap()

    x = din("x", [2, L, D])
    ctx = din("ctx", [2, LC, D])
    condT_d = din("condT", [128, 8, 4])
    wada = din("wada", [4, 128, 8, 6144])
    bada = din("bada", [4, 6144])
    vecs = din("vecs", [4, 128, 224])
    mrows = din("mrows", [2, 128])
    cos_d = din("cos", [L, 32])
    sin_d = din("sin", [L, 32])
    ident_d = din("ident", [128, 128])
    masks_d = din("masks", [2, 64, 64])
    gw2 = din("gw2", [2, 16, 2, 512])
    wspec = {"gwin": [2, 128, 8 * 3072], "gw1": [2, 128, 8 * 32], "gwout": [2, 128, 8 * 1024],
             "mwdn": [2, 128, 8 * 768], "mwuq": [2, 128, 3 * 1536], "mwukv": [2, 128, 2 * 2048], "mwout": [2, 128, 8 * 1024],
             "fwup": [4, 128, NJ * 8 * 256], "fwdn": [4, 128, NJ * 1024]}
    wf = {n: din(n, s) for n, s in wspec.items()}
    wb = {n: dsc(n + "_b", s, BF16) for n, s in wspec.items()}
    y = nc.dram_tensor("y", [2, L, D], F32, kind="ExternalOutput").ap()
    xc = dsc("xc", [2, LC, D], F32)
    m_dram = dsc("m_dram", [4, 6144], F32)
    hT = dsc("hT", [D, NT], BF16)
    oT = dsc("oT", [D, NT], BF16)
    qnT = dsc("qnT", [8, 128, NT], BF16)
    qrT = dsc("qrT", [8, 64, NT], BF16)
    knT = dsc("knT", [8, 128, NT], BF16)
    kpT = dsc("kpT", [64, NT], BF16)
    v_d = dsc("v_d", [NT, D], BF16)
    qkT = dsc("qkT", [D, NT], F32)
    gT = dsc("gT", [2, 512, NT], F32)
    sr_d = dsc("sr_d", [NT, D], F32)
    o_d = dsc("o_d", [NT, D], F32)

    K = Kern(nc)
    G = Glob(K, ident_d, masks_d)
    condT = nc.alloc_sbuf_tensor("condT_sb", [128, 8, 4], F32)
    vec = nc.alloc_sbuf_tensor("vec_sb", [128, 224], F32)
    mT = nc.alloc_sbuf_tensor("mT_sb", [128, 48, 4], F32)
    modv = nc.alloc_sbuf_tensor("modv_sb", [128, 3, 4, 8], F32)

    ph = Phase(K, "pro")
    CT = T(condT)
    ph.dma("sp", condT[:], condT_d, writes=[CT])
    act(ph, condT[:], condT[:], AF.Silu, [CT], [CT])
    for b in range(2):
        for r0 in range(0, L, 1024):
            ph.dma("sp", y[b, r0:r0 + 1024, :], x[b, r0:r0 + 1024, :])
        ph.dma("sp", xc[b], ctx[b])
    for n in wspec:
        for i in range(wspec[n][0]):
            cast_dram(ph, wb[n][i], wf[n][i])
    ph.end()

    for l in range(4):
        j = l // 2
        last = l == 3
        ph = Phase(K, "lv")
        ph.dma("sp", vec[:], vecs[l])
        ph.end()
        phase_ada(K, G, condT, wada[l], bada[l:l + 1, :], m_dram, mT, modv, vec)
        for b in range(2):
            phase_norm(K, G, xc[b], LC, hT, 0, modv[:, 2, 0, :], modv[:, 2, 1, :])
            phase_norm(K, G, y[b], L, hT, LC, modv[:, b, 0, :], modv[:, b, 1, :])
            if l % 2 == 0:
                phase_gla_proj(K, G, hT, wb["gwin"][j].rearrange("p (k n) -> p k n", k=8),
                               wb["gw1"][j].rearrange("p (k n) -> p k n", k=8), gw2[j], vec, qkT, gT, v_d, sr_d)
                phase_gla_scan(K, G, qkT, gT, v_d, o_d)
                phase_gla_out(K, G, o_d, sr_d, vec, oT)
                wo = wb["gwout"][j]
            else:
                phase_mla_qkv(K, G, l, hT, wb["mwdn"][j].rearrange("p (k n) -> p k n", k=8),
                              wb["mwuq"][j].rearrange("p (k n) -> p k n", k=3),
                              wb["mwukv"][j].rearrange("p (k n) -> p k n", k=2), vec, mrows[j:j + 1, :], cos_d, sin_d,
                              qnT, qrT, knT, kpT, v_d, not last)
                phase_mla_attn(K, G, qnT, qrT, knT, kpT, v_d, oT, not last)
                wo = wb["mwout"][j]
            jobs = [(oT[:, LC:NT], L, y[b], m_dram[b:b + 1, 2048:3072])]
            if not last:
                jobs.append((oT[:, 0:LC], LC, xc[b], m_dram[2:3, 2048:3072]))
            phase_outproj(K, G, wo.rearrange("p (k n) -> p k n", k=8), jobs)
            phase_norm(K, G, y[b], L, hT, LC, modv[:, b, 2, :], modv[:, b, 3, :])
            jobs = [(hT[:, LC:NT], L, y[b], m_dram[b:b + 1, 5120:6144])]
            if not last:
                phase_norm(K, G, xc[b], LC, hT, 0, modv[:, 2, 2, :], modv[:, 2, 3, :])
                jobs.append((hT[:, 0:LC], LC, xc[b], m_dram[2:3, 5120:6144]))
            cw = vec[:, 16:148].rearrange("p (a j c) -> p a j c", a=3, j=NJ)
            cb = vec[:, 148:192].rearrange("p (j c) -> p j c", j=NJ)
            phase_ffn(K, G, wb["fwup"][l].rearrange("p (j k c) -> p j k c", j=NJ, k=8),
                      wb["fwdn"][l].rearrange("p (j n) -> p j n", j=NJ), cw, cb, jobs)
    return nc


def host_inputs(inp):
    f = lambda a: np.ascontiguousarray(np.asarray(a, dtype=np.float32))
    g = {k: f(v) for k, v in inp.items()}
    shared = {}
    shared["wada"] = np.ascontiguousarray(g["w_ada"].reshape(4, 8, 128, 6144).transpose(0, 2, 1, 3))
    shared["bada"] = g["b_ada"]
    vecs = np.zeros((4, 128, 224), np.float32)
    fwup, fwdn = [], []
    for l in range(4):
        wup_l, wdn_l, cw_l, cb_l = lay_ffn(g["ffn_w_up"][l], g["ffn_w_down"][l], g["ffn_conv_w"][l], g["ffn_conv_b"][l])
        fwup.append(wup_l.reshape(128, -1))
        fwdn.append(wdn_l.reshape(128, -1))
        vecs[l, :, 0:8] = fmaj(g["norm_mix"][l])
        vecs[l, :, 8:16] = fmaj(g["norm_ffn"][l])
        vecs[l, :, 16:148] = cw_l
        vecs[l, :, 148:192] = cb_l
        j = l // 2
        if l % 2 == 0:
            vecs[l, :, 192:200] = g["gla_gate_b"][j].reshape(2, 4, 128).transpose(2, 0, 1).reshape(128, 8)
            vecs[l, :, 200:202] = fmaj(g["gla_out_norm"][j])
        else:
            vecs[l, :, 192:195] = fmaj(g["mla_q_lora_norm"][j])
            vecs[l, :, 195:197] = fmaj(g["mla_kv_lora_norm"][j])
            vecs[l, :, 197] = g["mla_q_norm"][j][:128]
            vecs[l, :, 198] = g["mla_k_norm"][j][:128]
    shared["vecs"] = vecs
    shared["fwup"] = np.stack(fwup)
    shared["fwdn"] = np.stack(fwdn)
    shared["mrows"] = np.ascontiguousarray(np.concatenate([g["mla_q_norm"][:, 128:], g["mla_k_norm"][:, 128:]], axis=1))
    gl = [lay_gla(g["gla_w_in"][j], g["gla_gate_w1"][j], g["gla_gate_w2"][j]) for j in range(2)]
    shared["gwin"] = np.stack([a[0].reshape(128, -1) for a in gl])
    shared["gw1"] = np.stack([a[1].reshape(128, -1) for a in gl])
    shared["gw2"] = np.stack([a[2] for a in gl])
    shared["gwout"] = np.stack([kmaj(g["gla_w_out"][j]).reshape(128, -1) for j in range(2)])
    ml = [lay_mla(g["mla_w_down"][j], g["mla_w_uq"][j], g["mla_w_ukv"][j]) for j in range(2)]
    shared["mwdn"] = np.stack([a[0].reshape(128, -1) for a in ml])
    shared["mwuq"] = np.stack([a[1].reshape(128, -1) for a in ml])
    shared["mwukv"] = np.stack([a[2].reshape(128, -1) for a in ml])
    shared["mwout"] = np.stack([kmaj(g["mla_w_out"][j]).reshape(128, -1) for j in range(2)])
    cos, sin = rope_tables()
    shared["cos"], shared["sin"] = cos, sin
    shared["ident"] = np.eye(128, dtype=np.float32)
    shared["masks"] = gla_masks()
    maps = []
    for c in range(8):
        m = dict(shared)
        m["x"] = g["x"][2 * c:2 * c + 2]
        m["ctx"] = g["ctx"][2 * c:2 * c + 2]
        cond = np.zeros((4, D), np.float32)
        cond[0:2] = g["c"][2 * c:2 * c + 2]
        cond[2] = g["c_ctx"]
        m["condT"] = np.ascontiguousarray(cond.reshape(4, 8, 128).transpose(2, 1, 0))
        maps.append(m)
    return maps


def kernel(**inputs):
    maps = host_inputs(inputs)
    nc = build_program()
    res = run_bass_kernel_spmd(nc, maps, core_ids=list(range(8)))
    return np.concatenate([np.asarray(r["y"], dtype=np.float32) for r in res.results], axis=0)
```

```python
import numpy as np
from contextlib import ExitStack
import concourse.bass as bass
import concourse.mybir as mybir
from concourse.bass_utils import run_bass_kernel_spmd

F32 = mybir.dt.float32
BF16 = mybir.dt.bfloat16
AF = mybir.ActivationFunctionType
ALU = mybir.AluOpType
AX = mybir.AxisListType

D = 1024
L = 4096
LC = 256
NT = L + LC
DFF = 2816
NJ = 22
EPS = 1e-6
ENGS = ["pe", "act", "dve", "pool", "sp"]
BLK = {"pe": "tensor", "act": "scalar", "dve": "vector", "pool": "gpsimd", "sp": "sync"}
NDSEM = {"sp": 24, "pool": 10, "act": 6}


class Buf:
    __slots__ = ("w", "r")

    def __init__(self):
        self.w = None
        self.r = {}


class T:
    def __init__(self, h, nsub=0):
        self.h = h
        self.b = Buf()
        self.sub = [Buf() for _ in range(nsub)]

    def __getitem__(self, idx):
        return self.h[idx]


class Kern:
    def __init__(self, nc):
        self.nc = nc
        self.sem = {e: nc.alloc_semaphore(name="cs_" + e) for e in ENGS}
        self.cnt = {e: 0 for e in ENGS}
        self.dsem = {e: [nc.alloc_semaphore(name=f"ds_{e}{i}") for i in range(n)] for e, n in NDSEM.items()}
        self.dcnt = {e: [0] * n for e, n in NDSEM.items()}
        self.drr = {e: 0 for e in NDSEM}
        self.psum = [nc.alloc_psum_tensor(f"psb{i}", [128, 512], F32) for i in range(8)]
        self.nphase = 0


class Phase:
    def __init__(self, K, name):
        self.K = K
        self.nc = K.nc
        self.name = name
        self.es = ExitStack()
        self.prog = {e: [] for e in ENGS}
        self.known = {e: {} for e in ENGS}
        self.ps = [T(p) for p in K.psum]
        self.nsb = 0

    def sb(self, shape, dtype, nsub=0):
        self.nsb += 1
        h = self.es.enter_context(self.nc.sbuf_tensor(f"{self.name}{self.K.nphase}_{self.nsb}", list(shape), dtype))
        return T(h, nsub)

    def _bufs(self, lst):
        out = []
        for x in lst:
            if x is None:
                continue
            out.append(x.b if isinstance(x, T) else x)
        return out

    def _emit(self, e, fn, reads, writes, is_dma):
        K = self.K
        reads = self._bufs(reads)
        writes = self._bufs(writes)
        tag = "dma" if is_dma else e
        needs = {}

        def need(dep):
            sem, val, de = dep
            cur = needs.get(id(sem))
            if cur is None or cur[1] < val:
                needs[id(sem)] = (sem, val)

        for b in reads:
            if b.w is not None:
                if not (b.w[2] == "pe" and tag == "pe"):
                    need(b.w)
        for b in writes:
            if b.w is not None and (b.w[2] != tag or tag in ("dma", "pool", "act")):
                need(b.w)
            for r in b.r.values():
                if r[2] != tag or tag in ("dma", "pool", "act"):
                    need(r)
        if is_dma:
            k = K.drr[e]
            K.drr[e] = (k + 1) % len(K.dsem[e])
            if K.dcnt[e][k] > 0:
                need((K.dsem[e][k], K.dcnt[e][k], "dma"))
            K.dcnt[e][k] += 16
            sem, val, inc = K.dsem[e][k], K.dcnt[e][k], 16
        else:
            K.cnt[e] += 1
            sem, val, inc = K.sem[e], K.cnt[e], 1
        kn = self.known[e]
        for sid, (s, v) in needs.items():
            if kn.get(sid, 0) < v:
                kn[sid] = v
                self.prog[e].append((0, s, v))
        self.prog[e].append((1, fn, sem, inc))
        dep = (sem, val, tag)
        for b in writes:
            b.w = dep
            b.r = {}
        rk = (id(sem) if is_dma else tag)
        for b in reads:
            b.r[rk] = dep
        return dep

    def op(self, e, fn, reads=(), writes=()):
        return self._emit(e, fn, reads, writes, False)

    def dma(self, q, out, in_, reads=(), writes=(), **kw):
        return self._emit(q, lambda eng: eng.dma_start(out=out, in_=in_, **kw), reads, writes, True)

    def end(self):
        K = self.K
        for e in NDSEM:
            kn = self.known[e]
            for k, s in enumerate(K.dsem[e]):
                v = K.dcnt[e][k]
                if v > 0 and kn.get(id(s), 0) < v and self.prog[e]:
                    self.prog[e].append((0, s, v))
                    kn[id(s)] = v
        finals = {e: K.cnt[e] for e in ENGS}
        for e in ENGS:
            if not self.prog[e]:
                continue
            for e2 in ENGS:
                if e2 != e and self.prog[e2] and finals[e2] > 0 and self.known[e].get(id(K.sem[e2]), 0) < finals[e2]:
                    self.prog[e].append((0, K.sem[e2], finals[e2]))

        def run(eng, prog):
            for it in prog:
                if it[0] == 0:
                    eng.wait_ge(it[1], it[2])
                else:
                    ins = it[1](eng)
                    ins.then_inc(it[2], it[3])

        with self.nc.Block() as block:
            for e in ENGS:
                if self.prog[e]:
                    getattr(block, BLK[e])(lambda eng, p=self.prog[e]: run(eng, p))
        self.es.close()
        K.nphase += 1


def mm_group(ph, out_t, out_ap, pairs, reads, first=True, last=True):
    n = len(pairs)

    def fn(e):
        ins = None
        for i, (l, r) in enumerate(pairs):
            ins = e.matmul(out_ap, l, r, start=(first and i == 0), stop=(last and i == n - 1))
        return ins
    return ph.op("pe", fn, reads=reads, writes=[out_t])


def transposes(ph, out_t, items, ident, reads):
    def fn(e):
        ins = None
        for (o, i, n) in items:
            ins = e.transpose(o, i, ident[0:n, 0:n])
        return ins
    return ph.op("pe", fn, reads=reads, writes=[out_t])


def act(ph, out, in_, func, reads, writes, bias=None, scale=None, accum_out=None, eng="act"):
    kw = {}
    if bias is not None:
        kw["bias"] = bias
    if scale is not None:
        kw["scale"] = scale
    if accum_out is not None:
        kw["accum_out"] = accum_out
    return ph.op(eng, lambda e: e.activation(out=out, in_=in_, func=func, **kw), reads=reads, writes=writes)


def tt(ph, eng, out, in0, in1, op, reads, writes):
    return ph.op(eng, lambda e: e.tensor_tensor(out=out, in0=in0, in1=in1, op=op), reads=reads, writes=writes)


def ts(ph, eng, out, in0, s1, op0, reads, writes, s2=None, op1=None):
    if op1 is None and eng == "pool" and op0 == ALU.mult:
        s2, op1 = 0.0, ALU.add
    if op1 is None:
        return ph.op(eng, lambda e: e.tensor_scalar(out=out, in0=in0, scalar1=s1, scalar2=None, op0=op0),
                     reads=reads, writes=writes)
    return ph.op(eng, lambda e: e.tensor_scalar(out=out, in0=in0, scalar1=s1, scalar2=s2, op0=op0, op1=op1),
                 reads=reads, writes=writes)


def stt(ph, out, in0, scalar, in1, op0, op1, reads, writes):
    return ph.op("dve", lambda e: e.scalar_tensor_tensor(out=out, in0=in0, scalar=scalar, in1=in1, op0=op0, op1=op1),
                 reads=reads, writes=writes)


def cp(ph, eng, out, in_, reads, writes):
    if eng == "act":
        return ph.op("act", lambda e: e.activation(out=out, in_=in_, func=AF.Copy, scale=1.0), reads=reads, writes=writes)
    return ph.op(eng, lambda e: e.tensor_copy(out=out, in_=in_), reads=reads, writes=writes)


def memset(ph, eng, ap, val, writes):
    return ph.op(eng, lambda e: e.memset(ap, val), writes=writes)


def phase_norm(K, G, src, ntok, dst, dcol0, A, Bv):
    ph = Phase(K, "nm")
    xt = [ph.sb([128, 1024], F32) for _ in range(3)]
    sq = ph.sb([128, 1024], F32)
    ss = [ph.sb([128, 1], F32) for _ in range(3)]
    hsb = [ph.sb([128, 8, 512], BF16, nsub=8) for _ in range(2)]
    dstv = dst.rearrange("(k p) t -> p k t", p=128)
    ti = 0
    for gi, g0 in enumerate(range(0, ntok, 512)):
        H = hsb[gi % 2]
        gn = min(512, ntok - g0)
        for t0 in range(g0, g0 + gn, 128):
            X, S = xt[ti % 3], ss[ti % 3]
            n = min(128, ntok - t0)
            ph.dma("sp", X[0:n, :], src[t0:t0 + n, :], writes=[X])
            act(ph, sq[0:n, :], X[0:n, :], AF.Square, [X], [sq, S], accum_out=S[0:n, :])
            act(ph, S[0:n, :], S[0:n, :], AF.Sqrt, [S], [S], bias=G.eps[0:n, :], scale=1.0 / D)
            ph.op("dve", lambda e, S=S, n=n: e.reciprocal(out=S[0:n, :], in_=S[0:n, :]), reads=[S], writes=[S])
            act(ph, X[0:n, :], X[0:n, :], AF.Copy, [X, S], [X], scale=S[0:n, :])
            for half in range(2):
                P = ph.ps[(ti * 2 + half) % 8]
                transposes(ph, P, [(P[:, q * 128:q * 128 + n], X[0:n, (half * 4 + q) * 128:(half * 4 + q + 1) * 128], n)
                                   for q in range(4)], G.ident, [X])
                for q in range(4):
                    k = half * 4 + q
                    c0 = t0 - g0
                    ts(ph, "dve", H[:, k, c0:c0 + n], P[:, q * 128:q * 128 + n], A[:, k:k + 1], ALU.mult,
                       [P], [H.sub[k]], s2=Bv[:, k:k + 1], op1=ALU.add)
            ti += 1
        ph.dma("sp", dstv[:, :, dcol0 + g0:dcol0 + g0 + gn], H[:, :, 0:gn], reads=[H] + H.sub)
    ph.end()


class Glob:
    def __init__(self, K, ident_d, masks_d=None):
        nc = K.nc
        self.ident = nc.alloc_sbuf_tensor("g_ident", [128, 128], F32)
        self.identb = nc.alloc_sbuf_tensor("g_identb", [128, 128], BF16)
        self.eps = nc.alloc_sbuf_tensor("g_eps", [128, 1], F32)
        self.ones = nc.alloc_sbuf_tensor("g_ones", [128, 128], BF16)
        self.one1 = nc.alloc_sbuf_tensor("g_one1", [128, 1], F32)
        self.ones32 = nc.alloc_sbuf_tensor("g_ones32", [128, 128], F32)
        self.onesf = nc.alloc_sbuf_tensor("g_onesf", [128, 64], F32)
        self.mask_f = nc.alloc_sbuf_tensor("g_maskf", [64, 64], F32)
        self.mask_b = nc.alloc_sbuf_tensor("g_maskb", [64, 64], F32)
        ph = Phase(K, "g")
        I = T(self.ident)
        ph.dma("sp", self.ident[:], ident_d[:, :], writes=[I])
        cp(ph, "dve", self.identb[:], self.ident[:], [I], [])
        memset(ph, "dve", self.eps[:], EPS, [])
        memset(ph, "dve", self.ones[:], 1.0, [])
        memset(ph, "dve", self.one1[:], 1.0, [])
        memset(ph, "dve", self.ones32[:], 1.0, [])
        memset(ph, "dve", self.onesf[:], 1.0, [])
        if masks_d is not None:
            ph.dma("sp", self.mask_f[:], masks_d[0], writes=[])
            ph.dma("sp", self.mask_b[:], masks_d[1], writes=[])
        ph.end()


def resid_evac(ph, pd, M, G_t, xr, tmp, xo, x_rows, width=1024):
    for half in range(2):
        tt(ph, "dve", tmp[0:M, half * 512:(half + 1) * 512], pd[half][0:M, :], G_t[0:M, half * 512:(half + 1) * 512],
           ALU.mult, [pd[half], G_t], [tmp])
    tt(ph, "pool", xo[0:M, :], tmp[0:M, :], xr[0:M, :], ALU.add, [tmp, xr], [xo])
    ph.dma("pool", x_rows, xo[0:M, :], reads=[xo])


def phase_ffn(K, G, wup_d, wdn_d, cw, cb, jobs):
    ph = Phase(K, "ff")
    wdn = ph.sb([128, NJ, 1024], BF16)
    for q in range(2):
        ph.dma("sp", wdn[:, q * 11:(q + 1) * 11, :], wdn_d[:, q * 11:(q + 1) * 11, :], writes=[wdn])
    G2 = ph.sb([128, 1024], F32)
    hblk = [ph.sb([128, 8, 512], BF16) for _ in range(2)]
    wup = [ph.sb([128, 8, 256], BF16) for _ in range(3)]
    actb = ph.sb([128, NJ, 512], BF16)
    ta = [ph.sb([128, 512], F32) for _ in range(2)]
    tg = [ph.sb([128, 512], F32) for _ in range(2)]
    sg = [ph.sb([128, 512], F32) for _ in range(2)]
    xr = [ph.sb([128, 1024], F32) for _ in range(2)]
    tmp = [ph.sb([128, 1024], F32) for _ in range(2)]
    xo = [ph.sb([128, 1024], F32) for _ in range(2)]
    cnt = 0
    bi = 0
    mi = 0
    for (hT, n_tok, xd, grow) in jobs:
        ph.dma("sp", G2[:], grow.partition_broadcast(128), writes=[G2])
        hv = hT.rearrange("(k p) t -> p k t", p=128)
        for s0 in range(0, n_tok, 510):
            n = min(510, n_tok - s0)
            W = n + 2
            H = hblk[bi % 2]
            bi += 1
            lo, hi = max(s0 - 1, 0), min(s0 + n + 1, n_tok)
            if s0 == 0:
                memset(ph, "pool", H[:, :, 0:1], 0.0, [H])
            if s0 + n == n_tok:
                memset(ph, "pool", H[:, :, W - 1:W], 0.0, [H])
            c0 = lo - (s0 - 1)
            ph.dma("sp", H[:, :, c0:c0 + hi - lo], hv[:, :, lo:hi], writes=[H])
            for j in range(NJ):
                Wj = wup[cnt % 3]
                pa, pg = ph.ps[(2 * cnt) % 6], ph.ps[(2 * cnt + 1) % 6]
                A_, G_, S_ = ta[cnt % 2], tg[cnt % 2], sg[cnt % 2]
                cnt += 1
                ph.dma("sp", Wj[:], wup_d[:, j], writes=[Wj])
                mm_group(ph, pa, pa[:, 0:W], [(Wj[:, k, 0:128], H[:, k, 0:W]) for k in range(8)], [Wj, H])
                mm_group(ph, pg, pg[:, 0:W], [(Wj[:, k, 128:256], H[:, k, 0:W]) for k in range(8)], [Wj, H])
                act(ph, A_[:, 0:n], pa[:, 1:W - 1], AF.Copy, [pa], [A_], scale=cw[:, 1, j, 0:1])
                act(ph, G_[:, 0:n], pg[:, 1:W - 1], AF.Copy, [pg], [G_], scale=cw[:, 1, j, 1:2])
                stt(ph, A_[:, 0:n], pa[:, 0:n], cw[:, 0, j, 0:1], A_[:, 0:n], ALU.mult, ALU.add, [pa, A_], [A_])
                stt(ph, G_[:, 0:n], pg[:, 0:n], cw[:, 0, j, 1:2], G_[:, 0:n], ALU.mult, ALU.add, [pg, G_], [G_])
                stt(ph, A_[:, 0:n], pa[:, 2:W], cw[:, 2, j, 0:1], A_[:, 0:n], ALU.mult, ALU.add, [pa, A_], [A_])
                stt(ph, G_[:, 0:n], pg[:, 2:W], cw[:, 2, j, 1:2], G_[:, 0:n], ALU.mult, ALU.add, [pg, G_], [G_])
                act(ph, S_[:, 0:n], G_[:, 0:n], AF.Silu, [G_], [S_], bias=cb[:, j, 1:2])
                stt(ph, actb[:, j, 0:n], A_[:, 0:n], cb[:, j, 0:1], S_[:, 0:n], ALU.add, ALU.mult, [A_, S_], [actb])
            for m0 in range(0, n, 128):
                M = min(128, n - m0)
                XR, TM, XO = xr[mi % 2], tmp[mi % 2], xo[mi % 2]
                mi += 1
                rows = xd[s0 + m0:s0 + m0 + M, :]
                ph.dma("sp", XR[0:M, :], rows, writes=[XR])
                pd = [ph.ps[6], ph.ps[7]]
                for half in range(2):
                    mm_group(ph, pd[half], pd[half][0:M, :],
                             [(actb[:, j, m0:m0 + M], wdn[:, j, half * 512:(half + 1) * 512]) for j in range(NJ)],
                             [actb, wdn])
                resid_evac(ph, pd, M, G2, XR, TM, XO, rows)
    ph.end()


def phase_outproj(K, G, w_d, jobs):
    ph = Phase(K, "op")
    w = ph.sb([128, 8, 1024], BF16)
    ph.dma("sp", w[:], w_d, writes=[w])
    G1 = ph.sb([128, 1024], F32)
    ob = [ph.sb([128, 8, 512], BF16) for _ in range(2)]
    xr = [ph.sb([128, 1024], F32) for _ in range(2)]
    tmp = [ph.sb([128, 1024], F32) for _ in range(2)]
    xo = [ph.sb([128, 1024], F32) for _ in range(2)]
    gi = 0
    mi = 0
    for (oT, n_tok, xd, grow) in jobs:
        ph.dma("sp", G1[:], grow.partition_broadcast(128), writes=[G1])
        ov = oT.rearrange("(k p) t -> p k t", p=128)
        for g0 in range(0, n_tok, 512):
            gn = min(512, n_tok - g0)
            O = ob[gi % 2]
            gi += 1
            ph.dma("sp", O[:, :, 0:gn], ov[:, :, g0:g0 + gn], writes=[O])
            for m0 in range(0, gn, 128):
                M = min(128, gn - m0)
                XR, TM, XO = xr[mi % 2], tmp[mi % 2], xo[mi % 2]
                pd = [ph.ps[(mi % 4) * 2], ph.ps[(mi % 4) * 2 + 1]]
                mi += 1
                rows = xd[g0 + m0:g0 + m0 + M, :]
                ph.dma("sp", XR[0:M, :], rows, writes=[XR])
                for half in range(2):
                    mm_group(ph, pd[half], pd[half][0:M, :],
                             [(O[:, k, m0:m0 + M], w[:, k, half * 512:(half + 1) * 512]) for k in range(8)], [O, w])
                resid_evac(ph, pd, M, G1, XR, TM, XO, rows)
    ph.end()


SCALE_MLA = 192.0 ** -0.5


def rstd_inplace(ph, G, S_t, ap, n_feat):
    act(ph, ap, ap, AF.Sqrt, [S_t], [S_t], bias=G.eps[0:ap.shape[0], :], scale=1.0 / n_feat)
    ph.op("dve", lambda e: e.reciprocal(out=ap, in_=ap), reads=[S_t], writes=[S_t])


def phase_mla_qkv(K, G, L_, hT, wdn_d, wuq_d, wukv_d, vec, rows_d, cos_d, sin_d, qnT, qrT, knT, kpT, v_d, want_ctx_q):
    ph = Phase(K, "mq")
    wdn = ph.sb([128, 8, 768], BF16)
    wuq = ph.sb([128, 3, 1536], BF16)
    wukv = ph.sb([128, 2, 2048], BF16)
    ph.dma("sp", wdn[:], wdn_d, writes=[wdn])
    ph.dma("sp", wuq[:], wuq_d, writes=[wuq])
    ph.dma("sp", wukv[:], wukv_d, writes=[wukv])
    grep_ = ph.sb([128, 128], F32)
    ph.dma("sp", grep_[:], rows_d.partition_broadcast(128), writes=[grep_])
    qln, kvln, qnn, knn = vec[:, 192:195], vec[:, 195:197], vec[:, 197:198], vec[:, 198:199]
    hb = [ph.sb([128, 8, 512], BF16) for _ in range(2)]
    st_2 = [ph.sb([128, 4], F32) for _ in range(2)]
    cqn_2 = [ph.sb([128, 384], F32) for _ in range(2)]
    ckvn_2 = [ph.sb([128, 256], F32) for _ in range(2)]
    kpe_2 = [ph.sb([128, 128], F32) for _ in range(2)]
    kpe2_2 = [ph.sb([128, 128], F32) for _ in range(2)]
    for _t in kpe_2 + kpe2_2:
        memset(ph, "dve", _t[:, 64:128], 0.0, [_t])
    cqT_2 = [ph.sb([128, 3, 128], BF16) for _ in range(2)]
    ckvT_2 = [ph.sb([128, 2, 128], BF16) for _ in range(2)]
    kpTs_2 = [ph.sb([64, 128], BF16) for _ in range(2)]
    cs_2 = [ph.sb([128, 2, 32], F32) for _ in range(2)]
    sq_2 = [ph.sb([128, 1536], F32) for _ in range(2)]
    st2_2 = [ph.sb([128, 16], F32) for _ in range(2)]
    qn_2 = [ph.sb([128, 1536], F32) for _ in range(2)]
    qr2_2 = [ph.sb([128, 512], F32) for _ in range(2)]
    r1_2 = [ph.sb([128, 256], F32) for _ in range(2)]
    r2_2 = [ph.sb([128, 256], F32) for _ in range(2)]
    qnTs_2 = [ph.sb([128, 8, 128], BF16) for _ in range(2)]
    qrTs_2 = [ph.sb([128, 4, 128], BF16) for _ in range(2)]
    st3_2 = [ph.sb([128, 8], F32) for _ in range(2)]
    kn_2 = [ph.sb([128, 1024], F32) for _ in range(2)]
    knTs_2 = [ph.sb([128, 8, 128], BF16) for _ in range(2)]
    vs_2 = [ph.sb([128, 1024], BF16) for _ in range(2)]
    qsc = ph.sb([128, 1], F32)
    ts(ph, "dve", qsc[:], qnn, SCALE_MLA, ALU.mult, [], [qsc])
    hv = hT.rearrange("(k p) t -> p k t", p=128)
    for gi, g0 in enumerate(range(0, NT, 512)):
        gn = min(512, NT - g0)
        H = hb[gi % 2]
        ph.dma("sp", H[:, :, 0:gn], hv[:, :, g0:g0 + gn], writes=[H])
        for m0 in range(0, gn, 128):
            t0 = g0 + m0
            par = (t0 // 128) % 2
            st = st_2[par]
            cqn = cqn_2[par]
            ckvn = ckvn_2[par]
            kpe = kpe_2[par]
            kpe2 = kpe2_2[par]
            cqT = cqT_2[par]
            ckvT = ckvT_2[par]
            kpTs = kpTs_2[par]
            cs = cs_2[par]
            sq = sq_2[par]
            st2 = st2_2[par]
            qn = qn_2[par]
            qr2 = qr2_2[par]
            r1 = r1_2[par]
            r2 = r2_2[par]
            qnTs = qnTs_2[par]
            qrTs = qrTs_2[par]
            st3 = st3_2[par]
            kn = kn_2[par]
            knTs = knTs_2[par]
            vs = vs_2[par]
            ps = [ph.ps[(i + 4 * par) % 8] for i in range(8)]
            latent = t0 >= LC
            need_q = latent or want_ctx_q
            mm_group(ph, ps[0], ps[0][:, :], [(H[:, k, m0:m0 + 128], wdn[:, k, 0:512]) for k in range(8)], [H, wdn])
            mm_group(ph, ps[1], ps[1][:, 0:256], [(H[:, k, m0:m0 + 128], wdn[:, k, 512:768]) for k in range(8)], [H, wdn])
            act(ph, sq[:, 0:384], ps[0][:, 0:384], AF.Square, [ps[0]], [sq, st], accum_out=st[:, 0:1])
            act(ph, sq[:, 384:448], ps[0][:, 384:448], AF.Square, [ps[0]], [sq, st], accum_out=st[:, 1:2])
            act(ph, sq[:, 512:768], ps[1][:, 0:256], AF.Square, [ps[1]], [sq, st], accum_out=st[:, 2:3])
            rstd_inplace(ph, G, st, st[:, 0:1], 384)
            rstd_inplace(ph, G, st, st[:, 1:2], 64)
            rstd_inplace(ph, G, st, st[:, 2:3], 256)

            ts(ph, "dve", cqn[:], ps[0][:, 0:384], st[:, 0:1], ALU.mult, [ps[0], st], [cqn])
            ts(ph, "dve", ckvn[:], ps[1][:, 0:256], st[:, 2:3], ALU.mult, [ps[1], st], [ckvn])
            stt(ph, kpe[:, 0:64], ps[0][:, 384:448], st[:, 1:2], grep_[:, 64:128], ALU.mult, ALU.mult, [ps[0], st, grep_], [kpe])

            if latent:
                ph.dma("sp", cs[:, 0, :], cos_d[t0 - LC:t0 - LC + 128, :], writes=[cs])
                ph.dma("sp", cs[:, 1, :], sin_d[t0 - LC:t0 - LC + 128, :], writes=[cs])
                cosv = cs[:, 0, :].rearrange("p (a f) -> p a f", a=2)
                sinv = cs[:, 1, :].rearrange("p (a f) -> p a f", a=2)
                kv5 = kpe[:, 0:64].rearrange("p (a h f) -> p a h f", a=2, h=2)
                ko5 = kpe2[:, 0:64].rearrange("p (a h f) -> p a h f", a=2, h=2)
                x1, x2 = kv5[:, :, 0, :], kv5[:, :, 1, :]
                a1 = r1[:, 0:32].rearrange("p (a f) -> p a f", a=2)
                a2 = r2[:, 0:32].rearrange("p (a f) -> p a f", a=2)
                tt(ph, "pool", a1, x1, cosv, ALU.mult, [kpe, cs], [r1])
                tt(ph, "pool", a2, x2, sinv, ALU.mult, [kpe, cs], [r2])
                tt(ph, "pool", ko5[:, :, 0, :], a1, a2, ALU.subtract, [r1, r2], [kpe2])
                tt(ph, "pool", a1, x1, sinv, ALU.mult, [kpe, cs], [r1])
                tt(ph, "pool", a2, x2, cosv, ALU.mult, [kpe, cs], [r2])
                tt(ph, "pool", ko5[:, :, 1, :], a1, a2, ALU.add, [r1, r2], [kpe2])
                kp_src = kpe2
            else:
                kp_src = kpe

            transposes(ph, ps[2], [(ps[2][:, kk * 128:(kk + 1) * 128], cqn[:, kk * 128:(kk + 1) * 128], 128) for kk in range(3)],
                       G.ident, [cqn])
            transposes(ph, ps[3], [(ps[3][:, kk * 128:(kk + 1) * 128], ckvn[:, kk * 128:(kk + 1) * 128], 128) for kk in range(2)]
                       + [(ps[3][:, 256:384], kp_src[:, :], 128)], G.ident, [ckvn, kp_src])

            for kk in range(3):
                ts(ph, "dve", cqT[:, kk, :], ps[2][:, kk * 128:(kk + 1) * 128], qln[:, kk:kk + 1], ALU.mult, [ps[2]], [cqT])
            for kk in range(2):
                ts(ph, "dve", ckvT[:, kk, :], ps[3][:, kk * 128:(kk + 1) * 128], kvln[:, kk:kk + 1], ALU.mult, [ps[3]], [ckvT])

            cp(ph, "dve", kpTs[:, :], ps[3][0:64, 256:384], [ps[3]], [kpTs])

            ph.dma("pool", kpT[:, t0:t0 + 128], kpTs[:, :], reads=[kpTs])

            if need_q:
                for nb in range(3):
                    mm_group(ph, ps[4 + nb], ps[4 + nb][:, :], [(cqT[:, kk, :], wuq[:, kk, nb * 512:(nb + 1) * 512]) for kk in range(3)],
                             [cqT, wuq])
                for nb in range(3):
                    act(ph, sq[:, nb * 512:(nb + 1) * 512], ps[4 + nb][:, :], AF.Square, [ps[4 + nb]], [sq])
                ph.op("dve", lambda e, st2=st2, sq=sq: e.tensor_reduce(out=st2[:, 0:8], in_=sq[:, 0:1024].rearrange("p (h d) -> p h d", h=8),
                                                       axis=AX.X, op=ALU.add), reads=[sq], writes=[st2])
                ph.op("dve", lambda e, st2=st2, sq=sq: e.tensor_reduce(out=st2[:, 8:16], in_=sq[:, 1024:1536].rearrange("p (h d) -> p h d", h=8),
                                                       axis=AX.X, op=ALU.add), reads=[sq], writes=[st2])
                rstd_inplace(ph, G, st2, st2[:, 0:8], 128)
                rstd_inplace(ph, G, st2, st2[:, 8:16], 64)
                for nb in range(2):
                    tt(ph, "dve", qn[:, nb * 512:(nb + 1) * 512].rearrange("p (h d) -> p h d", h=4),
                       ps[4 + nb][:, :].rearrange("p (h d) -> p h d", h=4),
                       st2[:, nb * 4:(nb + 1) * 4].unsqueeze(2).broadcast_to([128, 4, 128]), ALU.mult, [ps[4 + nb], st2], [qn])
                qrv = qn[:, 1024:1536].rearrange("p (h d) -> p h d", h=8)
                tt(ph, "dve", qrv, ps[6][:, :].rearrange("p (h d) -> p h d", h=8),
                   st2[:, 8:16].unsqueeze(2).broadcast_to([128, 8, 64]), ALU.mult, [ps[6], st2], [qn])
                tt(ph, "pool", qrv, qrv, grep_[:, 0:64].unsqueeze(1).broadcast_to([128, 8, 64]), ALU.mult, [qn, grep_], [qn])
                if latent:
                    q5 = qn[:, 1024:1536].rearrange("p (h a s f) -> p h a s f", h=8, a=2, s=2)
                    o5 = qr2[:, :].rearrange("p (h a s f) -> p h a s f", h=8, a=2, s=2)
                    cosb = cs[:, 0, :].rearrange("p (a f) -> p a f", a=2).unsqueeze(1).broadcast_to([128, 8, 2, 16])
                    sinb = cs[:, 1, :].rearrange("p (a f) -> p a f", a=2).unsqueeze(1).broadcast_to([128, 8, 2, 16])
                    x1, x2 = q5[:, :, :, 0, :], q5[:, :, :, 1, :]
                    a1 = r1[:, :].rearrange("p (h a f) -> p h a f", h=8, a=2)
                    a2 = r2[:, :].rearrange("p (h a f) -> p h a f", h=8, a=2)
                    tt(ph, "pool", a1, x1, cosb, ALU.mult, [qn, cs], [r1])
                    tt(ph, "pool", a2, x2, sinb, ALU.mult, [qn, cs], [r2])
                    tt(ph, "pool", o5[:, :, :, 0, :], a1, a2, ALU.subtract, [r1, r2], [qr2])
                    tt(ph, "pool", a1, x1, sinb, ALU.mult, [qn, cs], [r1])
                    tt(ph, "pool", a2, x2, cosb, ALU.mult, [qn, cs], [r2])
                    tt(ph, "pool", o5[:, :, :, 1, :], a1, a2, ALU.add, [r1, r2], [qr2])
                    qr_src, qr_t = qr2[:, :], qr2
                else:
                    qr_src, qr_t = qn[:, 1024:1536], qn
                for hb_ in range(2):
                    P = ps[hb_]
                    transposes(ph, P, [(P[:, q * 128:(q + 1) * 128], qn[:, (hb_ * 4 + q) * 128:(hb_ * 4 + q + 1) * 128], 128)
                                       for q in range(4)], G.ident, [qn])
                    ts(ph, "dve", qnTs[:, hb_ * 4:(hb_ + 1) * 4, :], P[:, :].rearrange("p (h t) -> p h t", h=4), qsc[:, 0:1],
                       ALU.mult, [P, qsc], [qnTs])
                P = ps[2]
                transposes(ph, P, [(P[:, q * 128:(q + 1) * 128], qr_src[:, q * 128:(q + 1) * 128], 128) for q in range(4)],
                           G.ident, [qr_t])
                ts(ph, "dve", qrTs[:, :, :], P[:, :].rearrange("p (g t) -> p g t", g=4), SCALE_MLA, ALU.mult, [P], [qrTs])
                ph.dma("pool", qnT[:, :, t0:t0 + 128].rearrange("h p t -> p h t"), qnTs[:, :, :], reads=[qnTs])
                ph.dma("pool", qrT.rearrange("h d t -> (h d) t").rearrange("(g q) t -> q g t", q=128)[:, :, t0:t0 + 128], qrTs[:, :, :], reads=[qrTs])

            kb = [ps[4], ps[5], ps[6], ps[7]]
            for nb in range(4):
                mm_group(ph, kb[nb], kb[nb][:, :], [(ckvT[:, kk, :], wukv[:, kk, nb * 512:(nb + 1) * 512]) for kk in range(2)],
                         [ckvT, wukv])
            for nb in range(2):
                act(ph, sq[:, nb * 512:(nb + 1) * 512], kb[nb][:, :], AF.Square, [kb[nb]], [sq])
            ph.op("dve", lambda e, st3=st3, sq=sq: e.tensor_reduce(out=st3[:, 0:8], in_=sq[:, 0:1024].rearrange("p (h d) -> p h d", h=8),
                                                   axis=AX.X, op=ALU.add), reads=[sq], writes=[st3])
            rstd_inplace(ph, G, st3, st3[:, 0:8], 128)
            for nb in range(2):
                tt(ph, "dve", kn[:, nb * 512:(nb + 1) * 512].rearrange("p (h d) -> p h d", h=4),
                   kb[nb][:, :].rearrange("p (h d) -> p h d", h=4),
                   st3[:, nb * 4:(nb + 1) * 4].unsqueeze(2).broadcast_to([128, 4, 128]), ALU.mult, [kb[nb], st3], [kn])
                cp(ph, "act", vs[:, nb * 512:(nb + 1) * 512], kb[2 + nb][:, :], [kb[2 + nb]], [vs])
            ph.dma("pool", v_d[t0:t0 + 128, :], vs[:, :], reads=[vs])
            for hb_ in range(2):
                P = ps[hb_]
                transposes(ph, P, [(P[:, q * 128:(q + 1) * 128], kn[:, (hb_ * 4 + q) * 128:(hb_ * 4 + q + 1) * 128], 128)
                                   for q in range(4)], G.ident, [kn])
                ts(ph, "dve", knTs[:, hb_ * 4:(hb_ + 1) * 4, :], P[:, :].rearrange("p (h t) -> p h t", h=4), knn[:, 0:1],
                   ALU.mult, [P], [knTs])
            ph.dma("pool", knT[:, :, t0:t0 + 128].rearrange("h p t -> p h t"), knTs[:, :, :], reads=[knTs])
    ph.end()


def phase_mla_attn(K, G, qnT, qrT, knT, kpT, v_d, oT, want_ctx):
    ph = Phase(K, "ma")
    kp = ph.sb([64, NT], BF16)
    ph.dma("sp", kp[:], kpT, writes=[kp])
    kn = [ph.sb([128, NT], BF16) for _ in range(2)]
    vh = [ph.sb([128, NT // 128, 128], BF16) for _ in range(2)]
    qn = [ph.sb([128, NT], BF16) for _ in range(2)]
    qr = [ph.sb([64, NT], BF16) for _ in range(2)]
    pT = [ph.sb([128, 512], BF16) for _ in range(4)]
    rec = [ph.sb([128, 512], F32) for _ in range(2)]
    ob = [ph.sb([128, 512], BF16) for _ in range(2)]
    acc = [ph.sb([128, 512], F32) for _ in range(2)]
    ps = ph.ps
    it = 0
    qt = 0
    NCH = NT // 128
    for h in range(8):
        KN, VH, QN, QR = kn[h % 2], vh[h % 2], qn[h % 2], qr[h % 2]
        ph.dma("sp", KN[:], knT[h], writes=[KN])
        ph.dma("sp", VH[:], v_d[:, h * 128:(h + 1) * 128].rearrange("(c p) v -> p c v", p=128), writes=[VH])
        ph.dma("sp", QN[:], qnT[h], writes=[QN])
        ph.dma("sp", QR[:], qrT[h], writes=[QR])
        tiles = [(LC + i * 512, 512, NCH) for i in range(L // 512)]
        if want_ctx:
            tiles = [(0, LC, LC // 128)] + tiles
        for (q0, nq, nch) in tiles:
            O, Dn = ps[4 + qt % 2], ps[6 + qt % 2]
            ACC = acc[qt % 2]

            def qk(c, slot):
                S = ps[slot % 4]
                mm_group(ph, S, S[:, 0:nq], [(KN[:, c * 128:(c + 1) * 128], QN[:, q0:q0 + nq]),
                                             (kp[:, c * 128:(c + 1) * 128], QR[:, q0:q0 + nq])], [KN, QN, kp, QR])
            qk(0, it)
            for c in range(nch):
                S = ps[it % 4]
                PT = pT[it % 4]
                if c + 1 < nch:
                    qk(c + 1, it + 1)
                it += 1
                act(ph, PT[:, 0:nq], S[:, 0:nq], AF.Exp, [S], [PT])
                mm_group(ph, O, O[:, 0:nq], [(VH[:, c, :], PT[:, 0:nq])], [VH, PT], first=(c == 0), last=(c == nch - 1))
                if c == 0:
                    cp(ph, "dve", ACC[:, 0:nq], PT[:, 0:nq], [PT], [ACC])
                else:
                    tt(ph, "dve", ACC[:, 0:nq], ACC[:, 0:nq], PT[:, 0:nq], ALU.add, [ACC, PT], [ACC])
            mm_group(ph, Dn, Dn[:, 0:nq], [(G.ones32[:, :], ACC[:, 0:nq])], [ACC])
            R, OB = rec[qt % 2], ob[qt % 2]
            qt += 1
            ph.op("dve", lambda e, R=R, Dn=Dn, nq=nq: e.reciprocal(out=R[:, 0:nq], in_=Dn[:, 0:nq]), reads=[Dn], writes=[R])
            tt(ph, "dve", OB[:, 0:nq], O[:, 0:nq], R[:, 0:nq], ALU.mult, [O, R], [OB])
            ph.dma("pool", oT[h * 128:(h + 1) * 128, q0:q0 + nq], OB[:, 0:nq], reads=[OB])
    ph.end()


def kmaj(w):
    Kd, N = w.shape
    return np.ascontiguousarray(w.reshape(Kd // 128, 128, N).transpose(1, 0, 2))


def fmaj(v):
    return np.ascontiguousarray(v.reshape(-1, 128).T)


def lay_mla(w_down, w_uq, w_ukv):
    wd = np.concatenate([w_down[:, 0:384], w_down[:, 640:704], np.zeros((1024, 64), w_down.dtype), w_down[:, 384:640]], axis=1)
    uq = w_uq.reshape(384, 8, 192)
    uq = np.concatenate([uq[:, :, 0:128].reshape(384, 1024), uq[:, :, 128:192].reshape(384, 512)], axis=1)
    ukv = w_ukv.reshape(256, 8, 256)
    ukv = np.concatenate([ukv[:, :, 0:128].reshape(256, 1024), ukv[:, :, 128:256].reshape(256, 1024)], axis=1)
    return kmaj(wd), kmaj(uq), kmaj(ukv)


def lay_ffn(w_up, w_down, conv_w, conv_b):
    wa = w_up[:, :DFF].reshape(8, 128, NJ, 128)
    wg = w_up[:, DFF:].reshape(8, 128, NJ, 128)
    wup_l = np.ascontiguousarray(np.concatenate([wa, wg], axis=3).transpose(1, 2, 0, 3))
    wdn_l = np.ascontiguousarray(w_down.reshape(NJ, 128, 1024).transpose(1, 0, 2))
    cw_l = np.stack([conv_w[:, :DFF].reshape(3, NJ, 128), conv_w[:, DFF:].reshape(3, NJ, 128)], axis=-1)
    cw_l = cw_l.transpose(2, 0, 1, 3).reshape(128, 3 * NJ * 2)
    cb_l = np.stack([conv_b[:DFF].reshape(NJ, 128), conv_b[DFF:].reshape(NJ, 128)], axis=-1).transpose(1, 0, 2).reshape(128, NJ * 2)
    return wup_l, wdn_l, np.ascontiguousarray(cw_l), np.ascontiguousarray(cb_l)


def rope_tables():
    t = np.arange(L)
    row = (t // 64).astype(np.float32)
    col = (t % 64).astype(np.float32)
    inv = (10000.0 ** (-np.arange(16, dtype=np.float32) / 16)).astype(np.float32)
    ang = np.stack([row[:, None] * inv, col[:, None] * inv], axis=1).astype(np.float32)
    return np.cos(ang).reshape(L, 32).astype(np.float32), np.sin(ang).reshape(L, 32).astype(np.float32)


NCH = None


def phase_gla_proj(K, G, hT, win_d, w1_d, w2_d, vec, qkT, gT, v_d, sr_d):
    ph = Phase(K, "gp")
    win = ph.sb([128, 8, 3072], BF16)
    for q in range(4):
        ph.dma("sp", win[:, q * 2:(q + 1) * 2, :], win_d[:, q * 2:(q + 1) * 2, :], writes=[win])
    w1 = ph.sb([128, 8, 32], BF16)
    ph.dma("sp", w1[:], w1_d, writes=[w1])
    w2f = ph.sb([16, 2, 512], F32)
    w2 = ph.sb([16, 2, 512], BF16)
    ph.dma("sp", w2f[:], w2_d, writes=[w2f])
    cp(ph, "dve", w2[:], w2f[:], [w2f], [w2])
    negb = ph.sb([128, 8], F32)
    ts(ph, "dve", negb[:], vec[:, 192:200], -1.0, ALU.mult, [], [negb])
    hb = [ph.sb([128, 8, 512], BF16) for _ in range(2)]
    stg = [ph.sb([128, 512], F32) for _ in range(3)]
    hw1 = [ph.sb([16, 512], BF16) for _ in range(2)]
    e1 = [ph.sb([128, 512], F32) for _ in range(2)]
    vs = [ph.sb([128, 1024], BF16) for _ in range(2)]
    srs = [ph.sb([128, 1024], F32) for _ in range(2)]
    hv = hT.rearrange("(k p) t -> p k t", p=128)
    ps = ph.ps
    pi = 0
    si = 0
    for gi, g0 in enumerate(range(0, NT, 512)):
        gn = min(512, NT - g0)
        H = hb[gi % 2]
        ph.dma("sp", H[:, :, 0:gn], hv[:, :, g0:g0 + gn], writes=[H])
        for fc in range(8):
            P = ps[pi % 8]
            pi += 1
            ST = stg[si % 3]
            si += 1
            mm_group(ph, P, P[:, 0:gn], [(win[:, k, fc * 128:(fc + 1) * 128], H[:, k, 0:gn]) for k in range(8)], [win, H])
            if fc < 4:
                ts(ph, "dve", ST[:, 0:gn], P[:, 0:gn], 128.0 ** -0.5, ALU.mult, [P], [ST])
            else:
                cp(ph, "act", ST[:, 0:gn], P[:, 0:gn], [P], [ST])
            ph.dma("pool", qkT[fc * 128:(fc + 1) * 128, g0:g0 + gn], ST[:, 0:gn], reads=[ST])
        for d in range(2):
            P = ps[pi % 8]
            pi += 1
            HW = hw1[d]
            mm_group(ph, P, P[0:16, 0:gn], [(w1[:, k, d * 16:(d + 1) * 16], H[:, k, 0:gn]) for k in range(8)], [w1, H])
            cp(ph, "dve", HW[:, 0:gn], P[0:16, 0:gn], [P], [HW])
            for h in range(4):
                P2 = ps[pi % 8]
                pi += 1
                ST = stg[si % 3]
                si += 1
                E = e1[h % 2]
                mm_group(ph, P2, P2[:, 0:gn], [(w2[:, d, h * 128:(h + 1) * 128], HW[:, 0:gn])], [w2, HW])
                act(ph, E[:, 0:gn], P2[:, 0:gn], AF.Exp, [P2, negb], [E], bias=negb[:, d * 4 + h:d * 4 + h + 1], scale=-1.0)
                act(ph, E[:, 0:gn], E[:, 0:gn], AF.Ln, [E], [E], bias=G.one1[:, :], scale=1.0)
                ts(ph, "dve", ST[:, 0:gn], E[:, 0:gn], -1.0 / 16.0, ALU.mult, [E], [ST])
                ph.dma("pool", gT[d, h * 128:(h + 1) * 128, g0:g0 + gn], ST[:, 0:gn], reads=[ST])
        for m0 in range(0, gn, 128):
            t0 = g0 + m0
            VS, SR = vs[(t0 // 128) % 2], srs[(t0 // 128) % 2]
            for nb in range(4):
                P = ps[pi % 8]
                pi += 1
                mm_group(ph, P, P[:, :], [(H[:, k, m0:m0 + 128], win[:, k, 1024 + nb * 512:1024 + (nb + 1) * 512]) for k in range(8)],
                         [H, win])
                if nb < 2:
                    cp(ph, "dve", VS[:, nb * 512:(nb + 1) * 512], P[:, :], [P], [VS])
                else:
                    act(ph, SR[:, (nb - 2) * 512:(nb - 1) * 512], P[:, :], AF.Silu, [P], [SR])
            ph.dma("pool", v_d[t0:t0 + 128, :], VS[:, :], reads=[VS])
            ph.dma("pool", sr_d[t0:t0 + 128, :], SR[:, :], reads=[SR])
    ph.end()


def phase_gla_scan(K, G, qkT, gT, v_d, o_d):
    ph = Phase(K, "gs")
    nch = NT // 64
    ncc = LC // 64
    q = ph.sb([128, NT], F32)
    k = ph.sb([128, NT], F32)
    g = ph.sb([128, NT], F32)
    Pc = ph.sb([128, NT], F32)
    E = ph.sb([128, NT], F32, nsub=2)
    Eall = [E] + E.sub
    qd = ph.sb([128, NT], BF16)
    ki = ph.sb([128, NT], BF16)
    kitok = ph.sb([64, nch, 128], BF16)
    vh = ph.sb([64, nch, 256], BF16)
    dec = ph.sb([128, nch], F32)
    S = ph.sb([128, 256], F32)
    Sb = ph.sb([128, 256], BF16)
    SCall = ph.sb([64, nch, 64], BF16)
    ps = ph.ps
    it = 0
    odb = [Buf() for _ in range(nch)]
    for h in range(4):
        ph.dma("sp", q[:], qkT[h * 128:(h + 1) * 128, :], writes=[q])
        ph.dma("sp", k[:], qkT[512 + h * 128:512 + (h + 1) * 128, :], writes=[k])
        ph.dma("sp", vh[:], v_d[:, h * 256:(h + 1) * 256].rearrange("(c p) v -> p c v", p=64), writes=[vh])
        for d in range(2):
            ph.dma("sp", g[:], gT[d, h * 128:(h + 1) * 128, :], writes=[g])
            for c in range(nch):
                ph.op("dve", lambda e, c=c: e.tensor_tensor_scan(out=Pc[:, c * 64:(c + 1) * 64], data0=G.onesf[:, 0:64],
                                                                 data1=g[:, c * 64:(c + 1) * 64], initial=0.0,
                                                                 op0=ALU.mult, op1=ALU.add), reads=[g], writes=[Pc])
            P3 = Pc[:].rearrange("p (c t) -> p c t", t=64)
            tot = P3[:, :, 63:64]
            act(ph, dec[:].unsqueeze(2), tot, AF.Exp, [Pc], [dec])
            if d == 0:
                bq, bq_t = Pc, Pc
            else:
                g3 = g[:].rearrange("p (c t) -> p c t", t=64)
                tt(ph, "pool", g[:], g[:], Pc[:], ALU.subtract, [g, Pc], [g])
                tt(ph, "pool", g3, g3, tot.broadcast_to([128, nch, 64]), ALU.add, [g, Pc], [g])
                bq, bq_t = g, g
            act(ph, E[:], bq[:], AF.Exp, [bq_t], Eall)
            tt(ph, "dve", qd[:], q[:], E[:], ALU.mult, [q] + Eall, [qd])
            act(ph, E[:], bq[:], AF.Exp, [bq_t], Eall, scale=-1.0)
            tt(ph, "dve", ki[:], k[:], E[:], ALU.mult, [k] + Eall, [ki])
            for c0 in range(0, nch, 8):
                nb = min(8, nch - c0)
                P = ps[(c0 // 8) % 2]
                Pb = P[:, :].bitcast(BF16)
                transposes(ph, P, [(Pb[0:64, j * 128:(j + 1) * 128], ki[:, (c0 + j) * 64:(c0 + j + 1) * 64], 128) for j in range(nb)],
                           G.identb, [ki])
                cp(ph, "dve", kitok[:, c0:c0 + nb, :], Pb[0:64, 0:nb * 128].rearrange("p (j x) -> p j x", j=nb), [P], [kitok])
            memset(ph, "dve", S[:], 0.0, [S])
            memset(ph, "dve", Sb[:], 0.0, [Sb])
            order = list(range(nch)) if d == 0 else (list(range(ncc - 1, -1, -1)) + list(range(nch - 1, ncc - 1, -1)))
            mask = G.mask_f if d == 0 else G.mask_b
            for c0 in range(0, nch, 8):
                nb = min(8, nch - c0)
                P = ps[(c0 // 8) % 2]

                def fn(e, P=P, c0=c0, nb=nb):
                    ins = None
                    for j in range(nb):
                        cs_ = slice((c0 + j) * 64, (c0 + j + 1) * 64)
                        ins = e.matmul(P[0:64, j * 64:(j + 1) * 64], ki[:, cs_], qd[:, cs_], start=True, stop=True)
                    return ins
                ph.op("pe", fn, reads=[ki, qd], writes=[P])
                tt(ph, "dve", SCall[:, c0:c0 + nb, :], P[0:64, 0:nb * 64].rearrange("p (j t) -> p j t", j=nb),
                   mask[0:64, 0:64].unsqueeze(1).broadcast_to([64, nb, 64]), ALU.mult, [P], [SCall])
            c_first = order[0]
            Cf = ps[6 + it % 2]
            mm_group(ph, Cf, Cf[:, 0:256], [(kitok[:, c_first, :], vh[:, c_first, :])], [kitok, vh])
            prev_c = None
            pend = None
            if d == 0:
                groups = [list(range(0, ncc))] + [list(range(a, a + 8)) for a in range(ncc, nch, 8)]
            else:
                groups = [list(range(ncc - 1, -1, -1))] + [list(range(a + 7, a - 1, -1)) for a in range(nch - 8, ncc - 1, -8)]
            assert [c for g_ in groups for c in g_] == order
            gidx = {}
            for gi_, g_ in enumerate(groups):
                for c in g_:
                    gidx[c] = (gi_, c - min(g_), min(g_), len(g_))

            def flush(B, c, h=h, d=d):
                gi_, slot, cmin, glen = gidx[c]
                stage = E[0:64, (gi_ % 2) * 2048:(gi_ % 2) * 2048 + 2048].rearrange("p (j v) -> p j v", j=8)
                eb = E.sub[gi_ % 2]
                cp(ph, "dve", stage[:, slot, :], B[0:64, 0:256], [B], [eb])
                if c == groups[gi_][-1]:
                    dst = o_d[cmin * 64:(cmin + glen) * 64, h * 256:(h + 1) * 256].rearrange("(j p) v -> p j v", p=64)
                    dbufs = [odb[x] for x in groups[gi_]]
                    if d == 0:
                        ph.dma("sp", dst, stage[:, 0:glen, :], reads=[eb], writes=dbufs)
                    else:
                        ph.dma("pool", dst, stage[:, 0:glen, :], reads=[eb], writes=dbufs, accum_op=ALU.add)
            for i_, c in enumerate(order):
                cs_ = slice(c * 64, (c + 1) * 64)
                B, C = ps[2 + it % 3], ps[6 + it % 2]
                it += 1
                mm_group(ph, B, B[0:64, 0:256], [(SCall[:, c, :], vh[:, c, :]), (qd[:, cs_], Sb[:, :])], [SCall, vh, qd, Sb])
                if i_ + 1 < len(order):
                    cn = order[i_ + 1]
                    Cn = ps[6 + it % 2]
                    mm_group(ph, Cn, Cn[:, 0:256], [(kitok[:, cn, :], vh[:, cn, :])], [kitok, vh])
                cprev = c if prev_c is None else prev_c
                stt(ph, S[:], S[:], dec[:, cprev:cprev + 1], C[:, 0:256], ALU.mult, ALU.add, [S, dec, C], [S])
                act(ph, Sb[:], S[:], AF.Copy, [S, dec], [Sb], scale=dec[:, c:c + 1])
                prev_c = c
                if pend is not None:
                    flush(*pend)
                pend = (B, c)
            flush(*pend)
    ph.end()


def phase_gla_out(K, G, o_d, sr_d, vec, ogT):
    ph = Phase(K, "go")
    ob = [ph.sb([128, 1024], F32) for _ in range(2)]
    sb_ = [ph.sb([128, 1024], F32) for _ in range(2)]
    sq = ph.sb([128, 1024], F32)
    st = [ph.sb([128, 4], F32) for _ in range(2)]
    on = [ph.sb([128, 1024], F32) for _ in range(2)]
    og = [ph.sb([128, 8, 512], BF16) for _ in range(2)]
    ogv = ogT.rearrange("(k p) t -> p k t", p=128)
    onorm = vec[:, 200:202]
    ps = ph.ps
    ti = 0
    for gi, g0 in enumerate(range(0, NT, 512)):
        gn = min(512, NT - g0)
        OG = og[gi % 2]
        for m0 in range(0, gn, 128):
            t0 = g0 + m0
            O, SR, ST, ON = ob[ti % 2], sb_[ti % 2], st[ti % 2], on[ti % 2]
            ph.dma("sp", O[:], o_d[t0:t0 + 128, :], writes=[O])
            ph.dma("sp", SR[:], sr_d[t0:t0 + 128, :], writes=[SR])
            act(ph, sq[:], O[:], AF.Square, [O], [sq])
            ph.op("dve", lambda e, ST=ST: e.tensor_reduce(out=ST[:, 0:4], in_=sq[:, :].rearrange("p (h d) -> p h d", h=4),
                                                          axis=AX.X, op=ALU.add), reads=[sq], writes=[ST])
            rstd_inplace(ph, G, ST, ST[:, 0:4], 256)
            tt(ph, "dve", ON[:].rearrange("p (h d) -> p h d", h=4), O[:].rearrange("p (h d) -> p h d", h=4),
               ST[:, 0:4].unsqueeze(2).broadcast_to([128, 4, 256]), ALU.mult, [O, ST], [ON])
            tt(ph, "pool", ON[:], ON[:], SR[:], ALU.mult, [ON, SR], [ON])
            for half in range(2):
                P = ps[(ti * 2 + half) % 8]
                transposes(ph, P, [(P[:, q * 128:(q + 1) * 128], ON[:, (half * 4 + q) * 128:(half * 4 + q + 1) * 128], 128)
                                   for q in range(4)], G.ident, [ON])
                Pv = P[:, :].rearrange("p (h s t) -> p h s t", h=2, s=2)
                for s_ in range(2):
                    ts(ph, "dve", OG[:, half * 4:(half + 1) * 4, m0:m0 + 128].rearrange("p (h s) t -> p h s t", s=2)[:, :, s_, :],
                       Pv[:, :, s_, :], onorm[:, s_:s_ + 1], ALU.mult, [P], [OG])
            ti += 1
        ph.dma("pool", ogv[:, :, g0:g0 + gn], OG[:, :, 0:gn], reads=[OG])
    ph.end()


def lay_gla(w_in, w1, w2):
    w1c = np.concatenate([w1[0], w1[1]], axis=1)
    w2l = np.ascontiguousarray(w2.transpose(1, 0, 2))
    return kmaj(w_in), kmaj(w1c), w2l


def gla_masks():
    s_ = np.arange(64)[:, None]
    t_ = np.arange(64)[None, :]
    return np.stack([(t_ >= s_), (t_ <= s_)]).astype(np.float32)


def phase_ada(K, G, condT, wada_d, bada_d, m_dram, mT, modv, vec):
    ph = Phase(K, "ad")
    msb = ph.sb([4, 6144], F32)
    bt = ph.sb([4, 6144], F32)
    ph.dma("sp", bt[:], bada_d.partition_broadcast(4), writes=[bt])
    wt = [ph.sb([128, 8, 512], F32) for _ in range(2)]
    ps = ph.ps
    for n in range(12):
        W = wt[n % 2]
        ph.dma("sp", W[:], wada_d[:, :, n * 512:(n + 1) * 512], writes=[W])
        P = ps[n % 2]
        mm_group(ph, P, P[0:4, :], [(condT[:, k, :], W[:, k, :]) for k in range(8)], [W])
        tt(ph, "dve", msb[:, n * 512:(n + 1) * 512], P[0:4, :], bt[:, n * 512:(n + 1) * 512], ALU.add, [P, bt], [msb])
    ph.dma("sp", m_dram[:, :], msb[:], reads=[msb])
    PT = ps[2]
    transposes(ph, PT, [(PT[:, c * 4:(c + 1) * 4], msb[0:4, c * 128:(c + 1) * 128], 4) for c in range(48)], G.ident, [msb])
    MT = T(mT)
    cp(ph, "dve", mT[:].rearrange("p c j -> p (c j)"), PT[:, 0:192], [PT], [MT])
    MV = T(modv)
    for c in range(3):
        for (slot, jsc, jsh, goff) in ((0, 1, 0, 0), (2, 4, 3, 8)):
            ts(ph, "dve", modv[:, c, slot, :], mT[:, jsc * 8:(jsc + 1) * 8, c], 1.0, ALU.add, [MT], [MV])
            tt(ph, "dve", modv[:, c, slot, :], modv[:, c, slot, :], vec[:, goff:goff + 8], ALU.mult, [MV], [MV])
            cp(ph, "dve", modv[:, c, slot + 1, :], mT[:, jsh * 8:(jsh + 1) * 8, c], [MT], [MV])
    ph.end()


def cast_dram(ph, dst2, src2):
    F = src2.shape[1]
    for a in range(0, F, 8192):
        b = min(F, a + 8192)
        ph.dma("pool", dst2[:, a:b], src2[:, a:b], max_dma_last_dim=8192)


def build_program():
    nc = bass.Bass("TRN2", target_bir_lowering=False)

    def din(n, s, t=F32):
        return nc.dram_tensor(n, list(s), t, kind="ExternalInput").ap()

    def dsc(n, s, t):
        return nc.dram_tensor(n, list(s), t, kind="Internal").ap()

    x = din("x", [2, L, D])
    ctx = din("ctx", [2, LC, D])
    condT_d = din("condT", [128, 8, 4])
    wada = din("wada", [4, 128, 8, 6144])
    bada = din("bada", [4, 6144])
    vecs = din("vecs", [4, 128, 224])
    mrows = din("mrows", [2, 128])
    cos_d = din("cos", [L, 32])
    sin_d = din("sin", [L, 32])
    ident_d = din("ident", [128, 128])
    masks_d = din("masks", [2, 64, 64])
    gw2 = din("gw2", [2, 16, 2, 512])
    wspec = {"gwin": [2, 128, 8 * 3072], "gw1": [2, 128, 8 * 32], "gwout": [2, 128, 8 * 1024],
             "mwdn": [2, 128, 8 * 768], "mwuq": [2, 128, 3 * 1536], "mwukv": [2, 128, 2 * 2048], "mwout": [2, 128, 8 * 1024],
             "fwup": [4, 128, NJ * 8 * 256], "fwdn": [4, 128, NJ * 1024]}
    wf = {n: din(n, s) for n, s in wspec.items()}
    wb = {n: dsc(n + "_b", s, BF16) for n, s in wspec.items()}
    y = nc.dram_tensor("y", [2, L, D], F32, kind="ExternalOutput").ap()
    xc = dsc("xc", [2, LC, D], F32)
    m_dram = dsc("m_dram", [4, 6144], F32)
    hT = dsc("hT", [D, NT], BF16)
    oT = dsc("oT", [D, NT], BF16)
    qnT = dsc("qnT", [8, 128, NT], BF16)
    qrT = dsc("qrT", [8, 64, NT], BF16)
    knT = dsc("knT", [8, 128, NT], BF16)
    kpT = dsc("kpT", [64, NT], BF16)
    v_d = dsc("v_d", [NT, D], BF16)
    qkT = dsc("qkT", [D, NT], F32)
    gT = dsc("gT", [2, 512, NT], F32)
    sr_d = dsc("sr_d", [NT, D], F32)
    o_d = dsc("o_d", [NT, D], F32)

    K = Kern(nc)
    G = Glob(K, ident_d, masks_d)
    condT = nc.alloc_sbuf_tensor("condT_sb", [128, 8, 4], F32)
    vec = nc.alloc_sbuf_tensor("vec_sb", [128, 224], F32)
    mT = nc.alloc_sbuf_tensor("mT_sb", [128, 48, 4], F32)
    modv = nc.alloc_sbuf_tensor("modv_sb", [128, 3, 4, 8], F32)

    ph = Phase(K, "pro")
    CT = T(condT)
    ph.dma("sp", condT[:], condT_d, writes=[CT])
    act(ph, condT[:], condT[:], AF.Silu, [CT], [CT])
    for b in range(2):
        for r0 in range(0, L, 1024):
            ph.dma("sp", y[b, r0:r0 + 1024, :], x[b, r0:r0 + 1024, :])
        ph.dma("sp", xc[b], ctx[b])
    for n in wspec:
        for i in range(wspec[n][0]):
            cast_dram(ph, wb[n][i], wf[n][i])
    ph.end()

    for l in range(4):
        j = l // 2
        last = l == 3
        ph = Phase(K, "lv")
        ph.dma("sp", vec[:], vecs[l])
        ph.end()
        phase_ada(K, G, condT, wada[l], bada[l:l + 1, :], m_dram, mT, modv, vec)
        for b in range(2):
            phase_norm(K, G, xc[b], LC, hT, 0, modv[:, 2, 0, :], modv[:, 2, 1, :])
            phase_norm(K, G, y[b], L, hT, LC, modv[:, b, 0, :], modv[:, b, 1, :])
            if l % 2 == 0:
                phase_gla_proj(K, G, hT, wb["gwin"][j].rearrange("p (k n) -> p k n", k=8),
                               wb["gw1"][j].rearrange("p (k n) -> p k n", k=8), gw2[j], vec, qkT, gT, v_d, sr_d)
                phase_gla_scan(K, G, qkT, gT, v_d, o_d)
                phase_gla_out(K, G, o_d, sr_d, vec, oT)
                wo = wb["gwout"][j]
            else:
                phase_mla_qkv(K, G, l, hT, wb["mwdn"][j].rearrange("p (k n) -> p k n", k=8),
                              wb["mwuq"][j].rearrange("p (k n) -> p k n", k=3),
                              wb["mwukv"][j].rearrange("p (k n) -> p k n", k=2), vec, mrows[j:j + 1, :], cos_d, sin_d,
                              qnT, qrT, knT, kpT, v_d, not last)
                phase_mla_attn(K, G, qnT, qrT, knT, kpT, v_d, oT, not last)
                wo = wb["mwout"][j]
            jobs = [(oT[:, LC:NT], L, y[b], m_dram[b:b + 1, 2048:3072])]
            if not last:
                jobs.append((oT[:, 0:LC], LC, xc[b], m_dram[2:3, 2048:3072]))
            phase_outproj(K, G, wo.rearrange("p (k n) -> p k n", k=8), jobs)
            phase_norm(K, G, y[b], L, hT, LC, modv[:, b, 2, :], modv[:, b, 3, :])
            jobs = [(hT[:, LC:NT], L, y[b], m_dram[b:b + 1, 5120:6144])]
            if not last:
                phase_norm(K, G, xc[b], LC, hT, 0, modv[:, 2, 2, :], modv[:, 2, 3, :])
                jobs.append((hT[:, 0:LC], LC, xc[b], m_dram[2:3, 5120:6144]))
            cw = vec[:, 16:148].rearrange("p (a j c) -> p a j c", a=3, j=NJ)
            cb = vec[:, 148:192].rearrange("p (j c) -> p j c", j=NJ)
            phase_ffn(K, G, wb["fwup"][l].rearrange("p (j k c) -> p j k c", j=NJ, k=8),
                      wb["fwdn"][l].rearrange("p (j n) -> p j n", j=NJ), cw, cb, jobs)
    return nc


def host_inputs(inp):
    f = lambda a: np.ascontiguousarray(np.asarray(a, dtype=np.float32))
    g = {k: f(v) for k, v in inp.items()}
    shared = {}
    shared["wada"] = np.ascontiguousarray(g["w_ada"].reshape(4, 8, 128, 6144).transpose(0, 2, 1, 3))
    shared["bada"] = g["b_ada"]
    vecs = np.zeros((4, 128, 224), np.float32)
    fwup, fwdn = [], []
    for l in range(4):
        wup_l, wdn_l, cw_l, cb_l = lay_ffn(g["ffn_w_up"][l], g["ffn_w_down"][l], g["ffn_conv_w"][l], g["ffn_conv_b"][l])
        fwup.append(wup_l.reshape(128, -1))
        fwdn.append(wdn_l.reshape(128, -1))
        vecs[l, :, 0:8] = fmaj(g["norm_mix"][l])
        vecs[l, :, 8:16] = fmaj(g["norm_ffn"][l])
        vecs[l, :, 16:148] = cw_l
        vecs[l, :, 148:192] = cb_l
        j = l // 2
        if l % 2 == 0:
            vecs[l, :, 192:200] = g["gla_gate_b"][j].reshape(2, 4, 128).transpose(2, 0, 1).reshape(128, 8)
            vecs[l, :, 200:202] = fmaj(g["gla_out_norm"][j])
        else:
            vecs[l, :, 192:195] = fmaj(g["mla_q_lora_norm"][j])
            vecs[l, :, 195:197] = fmaj(g["mla_kv_lora_norm"][j])
            vecs[l, :, 197] = g["mla_q_norm"][j][:128]
            vecs[l, :, 198] = g["mla_k_norm"][j][:128]
    shared["vecs"] = vecs
    shared["fwup"] = np.stack(fwup)
    shared["fwdn"] = np.stack(fwdn)
    shared["mrows"] = np.ascontiguousarray(np.concatenate([g["mla_q_norm"][:, 128:], g["mla_k_norm"][:, 128:]], axis=1))
    gl = [lay_gla(g["gla_w_in"][j], g["gla_gate_w1"][j], g["gla_gate_w2"][j]) for j in range(2)]
    shared["gwin"] = np.stack([a[0].reshape(128, -1) for a in gl])
    shared["gw1"] = np.stack([a[1].reshape(128, -1) for a in gl])
    shared["gw2"] = np.stack([a[2] for a in gl])
    shared["gwout"] = np.stack([kmaj(g["gla_w_out"][j]).reshape(128, -1) for j in range(2)])
    ml = [lay_mla(g["mla_w_down"][j], g["mla_w_uq"][j], g["mla_w_ukv"][j]) for j in range(2)]
    shared["mwdn"] = np.stack([a[0].reshape(128, -1) for a in ml])
    shared["mwuq"] = np.stack([a[1].reshape(128, -1) for a in ml])
    shared["mwukv"] = np.stack([a[2].reshape(128, -1) for a in ml])
    shared["mwout"] = np.stack([kmaj(g["mla_w_out"][j]).reshape(128, -1) for j in range(2)])
    cos, sin = rope_tables()
    shared["cos"], shared["sin"] = cos, sin
    shared["ident"] = np.eye(128, dtype=np.float32)
    shared["masks"] = gla_masks()
    maps = []
    for c in range(8):
        m = dict(shared)
        m["x"] = g["x"][2 * c:2 * c + 2]
        m["ctx"] = g["ctx"][2 * c:2 * c + 2]
        cond = np.zeros((4, D), np.float32)
        cond[0:2] = g["c"][2 * c:2 * c + 2]
        cond[2] = g["c_ctx"]
        m["condT"] = np.ascontiguousarray(cond.reshape(4, 8, 128).transpose(2, 1, 0))
        maps.append(m)
    return maps


def kernel(**inputs):
    maps = host_inputs(inputs)
    nc = build_program()
    res = run_bass_kernel_spmd(nc, maps, core_ids=list(range(8)))
    return np.concatenate([np.asarray(r["y"], dtype=np.float32) for r in res.results], axis=0)
```

```python
import numpy as np
from contextlib import ExitStack
import concourse.bass as bass
import concourse.mybir as mybir
from concourse.bass_utils import run_bass_kernel_spmd

F32 = mybir.dt.float32
BF16 = mybir.dt.bfloat16
AF = mybir.ActivationFunctionType
ALU = mybir.AluOpType
AX = mybir.AxisListType

D = 1024
L = 4096
LC = 256
NT = L + LC
DFF = 2816
NJ = 22
EPS = 1e-6
ENGS = ["pe", "act", "dve", "pool", "sp"]
BLK = {"pe": "tensor", "act": "scalar", "dve": "vector", "pool": "gpsimd", "sp": "sync"}
NDSEM = {"sp": 24, "pool": 10, "act": 6}


class Buf:
    __slots__ = ("w", "r")

    def __init__(self):
        self.w = None
        self.r = {}


class T:
    def __init__(self, h, nsub=0):
        self.h = h
        self.b = Buf()
        self.sub = [Buf() for _ in range(nsub)]

    def __getitem__(self, idx):
        return self.h[idx]


class Kern:
    def __init__(self, nc):
        self.nc = nc
        self.sem = {e: nc.alloc_semaphore(name="cs_" + e) for e in ENGS}
        self.cnt = {e: 0 for e in ENGS}
        self.dsem = {e: [nc.alloc_semaphore(name=f"ds_{e}{i}") for i in range(n)] for e, n in NDSEM.items()}
        self.dcnt = {e: [0] * n for e, n in NDSEM.items()}
        self.drr = {e: 0 for e in NDSEM}
        self.psum = [nc.alloc_psum_tensor(f"psb{i}", [128, 512], F32) for i in range(8)]
        self.nphase = 0


class Phase:
    def __init__(self, K, name):
        self.K = K
        self.nc = K.nc
        self.name = name
        self.es = ExitStack()
        self.prog = {e: [] for e in ENGS}
        self.known = {e: {} for e in ENGS}
        self.ps = [T(p) for p in K.psum]
        self.nsb = 0

    def sb(self, shape, dtype, nsub=0):
        self.nsb += 1
        h = self.es.enter_context(self.nc.sbuf_tensor(f"{self.name}{self.K.nphase}_{self.nsb}", list(shape), dtype))
        return T(h, nsub)

    def _bufs(self, lst):
        out = []
        for x in lst:
            if x is None:
                continue
            out.append(x.b if isinstance(x, T) else x)
        return out

    def _emit(self, e, fn, reads, writes, is_dma):
        K = self.K
        reads = self._bufs(reads)
        writes = self._bufs(writes)
        tag = "dma" if is_dma else e
        needs = {}

        def need(dep):
            sem, val, de = dep
            cur = needs.get(id(sem))
            if cur is None or cur[1] < val:
                needs[id(sem)] = (sem, val)

        for b in reads:
            if b.w is not None:
                if not (b.w[2] == "pe" and tag == "pe"):
                    need(b.w)
        for b in writes:
            if b.w is not None and (b.w[2] != tag or tag in ("dma", "pool", "act")):
                need(b.w)
            for r in b.r.values():
                if r[2] != tag or tag in ("dma", "pool", "act"):
                    need(r)
        if is_dma:
            k = K.drr[e]
            K.drr[e] = (k + 1) % len(K.dsem[e])
            if K.dcnt[e][k] > 0:
                need((K.dsem[e][k], K.dcnt[e][k], "dma"))
            K.dcnt[e][k] += 16
            sem, val, inc = K.dsem[e][k], K.dcnt[e][k], 16
        else:
            K.cnt[e] += 1
            sem, val, inc = K.sem[e], K.cnt[e], 1
        kn = self.known[e]
        for sid, (s, v) in needs.items():
            if kn.get(sid, 0) < v:
                kn[sid] = v
                self.prog[e].append((0, s, v))
        self.prog[e].append((1, fn, sem, inc))
        dep = (sem, val, tag)
        for b in writes:
            b.w = dep
            b.r = {}
        rk = (id(sem) if is_dma else tag)
        for b in reads:
            b.r[rk] = dep
        return dep

    def op(self, e, fn, reads=(), writes=()):
        return self._emit(e, fn, reads, writes, False)

    def dma(self, q, out, in_, reads=(), writes=(), **kw):
        return self._emit(q, lambda eng: eng.dma_start(out=out, in_=in_, **kw), reads, writes, True)

    def end(self):
        K = self.K
        for e in NDSEM:
            kn = self.known[e]
            for k, s in enumerate(K.dsem[e]):
                v = K.dcnt[e][k]
                if v > 0 and kn.get(id(s), 0) < v and self.prog[e]:
                    self.prog[e].append((0, s, v))
                    kn[id(s)] = v
        finals = {e: K.cnt[e] for e in ENGS}
        for e in ENGS:
            if not self.prog[e]:
                continue
            for e2 in ENGS:
                if e2 != e and self.prog[e2] and finals[e2] > 0 and self.known[e].get(id(K.sem[e2]), 0) < finals[e2]:
                    self.prog[e].append((0, K.sem[e2], finals[e2]))

        def run(eng, prog):
            for it in prog:
                if it[0] == 0:
                    eng.wait_ge(it[1], it[2])
                else:
                    ins = it[1](eng)
                    ins.then_inc(it[2], it[3])

        with self.nc.Block() as block:
            for e in ENGS:
                if self.prog[e]:
                    getattr(block, BLK[e])(lambda eng, p=self.prog[e]: run(eng, p))
        self.es.close()
        K.nphase += 1


def mm_group(ph, out_t, out_ap, pairs, reads, first=True, last=True):
    n = len(pairs)

    def fn(e):
        ins = None
        for i, (l, r) in enumerate(pairs):
            ins = e.matmul(out_ap, l, r, start=(first and i == 0), stop=(last and i == n - 1))
        return ins
    return ph.op("pe", fn, reads=reads, writes=[out_t])


def transposes(ph, out_t, items, ident, reads):
    def fn(e):
        ins = None
        for (o, i, n) in items:
            ins = e.transpose(o, i, ident[0:n, 0:n])
        return ins
    return ph.op("pe", fn, reads=reads, writes=[out_t])


def act(ph, out, in_, func, reads, writes, bias=None, scale=None, accum_out=None, eng="act"):
    kw = {}
    if bias is not None:
        kw["bias"] = bias
    if scale is not None:
        kw["scale"] = scale
    if accum_out is not None:
        kw["accum_out"] = accum_out
    return ph.op(eng, lambda e: e.activation(out=out, in_=in_, func=func, **kw), reads=reads, writes=writes)


def tt(ph, eng, out, in0, in1, op, reads, writes):
    return ph.op(eng, lambda e: e.tensor_tensor(out=out, in0=in0, in1=in1, op=op), reads=reads, writes=writes)


def ts(ph, eng, out, in0, s1, op0, reads, writes, s2=None, op1=None):
    if op1 is None and eng == "pool" and op0 == ALU.mult:
        s2, op1 = 0.0, ALU.add
    if op1 is None:
        return ph.op(eng, lambda e: e.tensor_scalar(out=out, in0=in0, scalar1=s1, scalar2=None, op0=op0),
                     reads=reads, writes=writes)
    return ph.op(eng, lambda e: e.tensor_scalar(out=out, in0=in0, scalar1=s1, scalar2=s2, op0=op0, op1=op1),
                 reads=reads, writes=writes)


def stt(ph, out, in0, scalar, in1, op0, op1, reads, writes):
    return ph.op("dve", lambda e: e.scalar_tensor_tensor(out=out, in0=in0, scalar=scalar, in1=in1, op0=op0, op1=op1),
                 reads=reads, writes=writes)


def cp(ph, eng, out, in_, reads, writes):
    if eng == "act":
        return ph.op("act", lambda e: e.activation(out=out, in_=in_, func=AF.Copy, scale=1.0), reads=reads, writes=writes)
    return ph.op(eng, lambda e: e.tensor_copy(out=out, in_=in_), reads=reads, writes=writes)


def memset(ph, eng, ap, val, writes):
    return ph.op(eng, lambda e: e.memset(ap, val), writes=writes)


def phase_norm(K, G, src, ntok, dst, dcol0, A, Bv):
    ph = Phase(K, "nm")
    xt = [ph.sb([128, 1024], F32) for _ in range(3)]
    sq = ph.sb([128, 1024], F32)
    ss = [ph.sb([128, 1], F32) for _ in range(3)]
    hsb = [ph.sb([128, 8, 512], BF16, nsub=8) for _ in range(2)]
    dstv = dst.rearrange("(k p) t -> p k t", p=128)
    ti = 0
    for gi, g0 in enumerate(range(0, ntok, 512)):
        H = hsb[gi % 2]
        gn = min(512, ntok - g0)
        for t0 in range(g0, g0 + gn, 128):
            X, S = xt[ti % 3], ss[ti % 3]
            n = min(128, ntok - t0)
            ph.dma("sp", X[0:n, :], src[t0:t0 + n, :], writes=[X])
            act(ph, sq[0:n, :], X[0:n, :], AF.Square, [X], [sq, S], accum_out=S[0:n, :])
            act(ph, S[0:n, :], S[0:n, :], AF.Sqrt, [S], [S], bias=G.eps[0:n, :], scale=1.0 / D)
            ph.op("dve", lambda e, S=S, n=n: e.reciprocal(out=S[0:n, :], in_=S[0:n, :]), reads=[S], writes=[S])
            act(ph, X[0:n, :], X[0:n, :], AF.Copy, [X, S], [X], scale=S[0:n, :])
            for half in range(2):
                P = ph.ps[(ti * 2 + half) % 8]
                transposes(ph, P, [(P[:, q * 128:q * 128 + n], X[0:n, (half * 4 + q) * 128:(half * 4 + q + 1) * 128], n)
                                   for q in range(4)], G.ident, [X])
                for q in range(4):
                    k = half * 4 + q
                    c0 = t0 - g0
                    ts(ph, "dve", H[:, k, c0:c0 + n], P[:, q * 128:q * 128 + n], A[:, k:k + 1], ALU.mult,
                       [P], [H.sub[k]], s2=Bv[:, k:k + 1], op1=ALU.add)
            ti += 1
        ph.dma("sp", dstv[:, :, dcol0 + g0:dcol0 + g0 + gn], H[:, :, 0:gn], reads=[H] + H.sub)
    ph.end()


class Glob:
    def __init__(self, K, ident_d, masks_d=None):
        nc = K.nc
        self.ident = nc.alloc_sbuf_tensor("g_ident", [128, 128], F32)
        self.identb = nc.alloc_sbuf_tensor("g_identb", [128, 128], BF16)
        self.eps = nc.alloc_sbuf_tensor("g_eps", [128, 1], F32)
        self.ones = nc.alloc_sbuf_tensor("g_ones", [128, 128], BF16)
        self.one1 = nc.alloc_sbuf_tensor("g_one1", [128, 1], F32)
        self.ones32 = nc.alloc_sbuf_tensor("g_ones32", [128, 128], F32)
        self.onesf = nc.alloc_sbuf_tensor("g_onesf", [128, 64], F32)
        self.mask_f = nc.alloc_sbuf_tensor("g_maskf", [64, 64], F32)
        self.mask_b = nc.alloc_sbuf_tensor("g_maskb", [64, 64], F32)
        ph = Phase(K, "g")
        I = T(self.ident)
        ph.dma("sp", self.ident[:], ident_d[:, :], writes=[I])
        cp(ph, "dve", self.identb[:], self.ident[:], [I], [])
        memset(ph, "dve", self.eps[:], EPS, [])
        memset(ph, "dve", self.ones[:], 1.0, [])
        memset(ph, "dve", self.one1[:], 1.0, [])
        memset(ph, "dve", self.ones32[:], 1.0, [])
        memset(ph, "dve", self.onesf[:], 1.0, [])
        if masks_d is not None:
            ph.dma("sp", self.mask_f[:], masks_d[0], writes=[])
            ph.dma("sp", self.mask_b[:], masks_d[1], writes=[])
        ph.end()


def resid_evac(ph, pd, M, G_t, xr, tmp, xo, x_rows, width=1024):
    for half in range(2):
        tt(ph, "dve", tmp[0:M, half * 512:(half + 1) * 512], pd[half][0:M, :], G_t[0:M, half * 512:(half + 1) * 512],
           ALU.mult, [pd[half], G_t], [tmp])
    tt(ph, "pool", xo[0:M, :], tmp[0:M, :], xr[0:M, :], ALU.add, [tmp, xr], [xo])
    ph.dma("pool", x_rows, xo[0:M, :], reads=[xo])


def phase_ffn(K, G, wup_d, wdn_d, cw, cb, jobs):
    ph = Phase(K, "ff")
    wdn = ph.sb([128, NJ, 1024], BF16)
    for q in range(2):
        ph.dma("sp", wdn[:, q * 11:(q + 1) * 11, :], wdn_d[:, q * 11:(q + 1) * 11, :], writes=[wdn])
    G2 = ph.sb([128, 1024], F32)
    hblk = [ph.sb([128, 8, 512], BF16) for _ in range(2)]
    wup = [ph.sb([128, 8, 256], BF16) for _ in range(3)]
    actb = ph.sb([128, NJ, 512], BF16)
    ta = [ph.sb([128, 512], F32) for _ in range(2)]
    tg = [ph.sb([128, 512], F32) for _ in range(2)]
    sg = [ph.sb([128, 512], F32) for _ in range(2)]
    xr = [ph.sb([128, 1024], F32) for _ in range(2)]
    tmp = [ph.sb([128, 1024], F32) for _ in range(2)]
    xo = [ph.sb([128, 1024], F32) for _ in range(2)]
    cnt = 0
    bi = 0
    mi = 0
    for (hT, n_tok, xd, grow) in jobs:
        ph.dma("sp", G2[:], grow.partition_broadcast(128), writes=[G2])
        hv = hT.rearrange("(k p) t -> p k t", p=128)
        for s0 in range(0, n_tok, 510):
            n = min(510, n_tok - s0)
            W = n + 2
            H = hblk[bi % 2]
            bi += 1
            lo, hi = max(s0 - 1, 0), min(s0 + n + 1, n_tok)
            if s0 == 0:
                memset(ph, "pool", H[:, :, 0:1], 0.0, [H])
            if s0 + n == n_tok:
                memset(ph, "pool", H[:, :, W - 1:W], 0.0, [H])
            c0 = lo - (s0 - 1)
            ph.dma("sp", H[:, :, c0:c0 + hi - lo], hv[:, :, lo:hi], writes=[H])
            for j in range(NJ):
                Wj = wup[cnt % 3]
                pa, pg = ph.ps[(2 * cnt) % 6], ph.ps[(2 * cnt + 1) % 6]
                A_, G_, S_ = ta[cnt % 2], tg[cnt % 2], sg[cnt % 2]
                cnt += 1
                ph.dma("sp", Wj[:], wup_d[:, j], writes=[Wj])
                mm_group(ph, pa, pa[:, 0:W], [(Wj[:, k, 0:128], H[:, k, 0:W]) for k in range(8)], [Wj, H])
                mm_group(ph, pg, pg[:, 0:W], [(Wj[:, k, 128:256], H[:, k, 0:W]) for k in range(8)], [Wj, H])
                act(ph, A_[:, 0:n], pa[:, 1:W - 1], AF.Copy, [pa], [A_], scale=cw[:, 1, j, 0:1])
                act(ph, G_[:, 0:n], pg[:, 1:W - 1], AF.Copy, [pg], [G_], scale=cw[:, 1, j, 1:2])
                stt(ph, A_[:, 0:n], pa[:, 0:n], cw[:, 0, j, 0:1], A_[:, 0:n], ALU.mult, ALU.add, [pa, A_], [A_])
                stt(ph, G_[:, 0:n], pg[:, 0:n], cw[:, 0, j, 1:2], G_[:, 0:n], ALU.mult, ALU.add, [pg, G_], [G_])
                stt(ph, A_[:, 0:n], pa[:, 2:W], cw[:, 2, j, 0:1], A_[:, 0:n], ALU.mult, ALU.add, [pa, A_], [A_])
                stt(ph, G_[:, 0:n], pg[:, 2:W], cw[:, 2, j, 1:2], G_[:, 0:n], ALU.mult, ALU.add, [pg, G_], [G_])
                act(ph, S_[:, 0:n], G_[:, 0:n], AF.Silu, [G_], [S_], bias=cb[:, j, 1:2])
                stt(ph, actb[:, j, 0:n], A_[:, 0:n], cb[:, j, 0:1], S_[:, 0:n], ALU.add, ALU.mult, [A_, S_], [actb])
            for m0 in range(0, n, 128):
                M = min(128, n - m0)
                XR, TM, XO = xr[mi % 2], tmp[mi % 2], xo[mi % 2]
                mi += 1
                rows = xd[s0 + m0:s0 + m0 + M, :]
                ph.dma("sp", XR[0:M, :], rows, writes=[XR])
                pd = [ph.ps[6], ph.ps[7]]
                for half in range(2):
                    mm_group(ph, pd[half], pd[half][0:M, :],
                             [(actb[:, j, m0:m0 + M], wdn[:, j, half * 512:(half + 1) * 512]) for j in range(NJ)],
                             [actb, wdn])
                resid_evac(ph, pd, M, G2, XR, TM, XO, rows)
    ph.end()


def phase_outproj(K, G, w_d, jobs):
    ph = Phase(K, "op")
    w = ph.sb([128, 8, 1024], BF16)
    ph.dma("sp", w[:], w_d, writes=[w])
    G1 = ph.sb([128, 1024], F32)
    ob = [ph.sb([128, 8, 512], BF16) for _ in range(2)]
    xr = [ph.sb([128, 1024], F32) for _ in range(2)]
    tmp = [ph.sb([128, 1024], F32) for _ in range(2)]
    xo = [ph.sb([128, 1024], F32) for _ in range(2)]
    gi = 0
    mi = 0
    for (oT, n_tok, xd, grow) in jobs:
        ph.dma("sp", G1[:], grow.partition_broadcast(128), writes=[G1])
        ov = oT.rearrange("(k p) t -> p k t", p=128)
        for g0 in range(0, n_tok, 512):
            gn = min(512, n_tok - g0)
            O = ob[gi % 2]
            gi += 1
            ph.dma("sp", O[:, :, 0:gn], ov[:, :, g0:g0 + gn], writes=[O])
            for m0 in range(0, gn, 128):
                M = min(128, gn - m0)
                XR, TM, XO = xr[mi % 2], tmp[mi % 2], xo[mi % 2]
                pd = [ph.ps[(mi % 4) * 2], ph.ps[(mi % 4) * 2 + 1]]
                mi += 1
                rows = xd[g0 + m0:g0 + m0 + M, :]
                ph.dma("sp", XR[0:M, :], rows, writes=[XR])
                for half in range(2):
                    mm_group(ph, pd[half], pd[half][0:M, :],
                             [(O[:, k, m0:m0 + M], w[:, k, half * 512:(half + 1) * 512]) for k in range(8)], [O, w])
                resid_evac(ph, pd, M, G1, XR, TM, XO, rows)
    ph.end()


SCALE_MLA = 192.0 ** -0.5


def rstd_inplace(ph, G, S_t, ap, n_feat):
    act(ph, ap, ap, AF.Sqrt, [S_t], [S_t], bias=G.eps[0:ap.shape[0], :], scale=1.0 / n_feat)
    ph.op("dve", lambda e: e.reciprocal(out=ap, in_=ap), reads=[S_t], writes=[S_t])


def phase_mla_qkv(K, G, L_, hT, wdn_d, wuq_d, wukv_d, vec, rows_d, cos_d, sin_d, qnT, qrT, knT, kpT, v_d, want_ctx_q):
    ph = Phase(K, "mq")
    wdn = ph.sb([128, 8, 768], BF16)
    wuq = ph.sb([128, 3, 1536], BF16)
    wukv = ph.sb([128, 2, 2048], BF16)
    ph.dma("sp", wdn[:], wdn_d, writes=[wdn])
    ph.dma("sp", wuq[:], wuq_d, writes=[wuq])
    ph.dma("sp", wukv[:], wukv_d, writes=[wukv])
    grep_ = ph.sb([128, 128], F32)
    ph.dma("sp", grep_[:], rows_d.partition_broadcast(128), writes=[grep_])
    qln, kvln, qnn, knn = vec[:, 192:195], vec[:, 195:197], vec[:, 197:198], vec[:, 198:199]
    hb = [ph.sb([128, 8, 512], BF16) for _ in range(2)]
    st_2 = [ph.sb([128, 4], F32) for _ in range(2)]
    cqn_2 = [ph.sb([128, 384], F32) for _ in range(2)]
    ckvn_2 = [ph.sb([128, 256], F32) for _ in range(2)]
    kpe_2 = [ph.sb([128, 128], F32) for _ in range(2)]
    kpe2_2 = [ph.sb([128, 128], F32) for _ in range(2)]
    for _t in kpe_2 + kpe2_2:
        memset(ph, "dve", _t[:, 64:128], 0.0, [_t])
    cqT_2 = [ph.sb([128, 3, 128], BF16) for _ in range(2)]
    ckvT_2 = [ph.sb([128, 2, 128], BF16) for _ in range(2)]
    kpTs_2 = [ph.sb([64, 128], BF16) for _ in range(2)]
    cs_2 = [ph.sb([128, 2, 32], F32) for _ in range(2)]
    sq_2 = [ph.sb([128, 1536], F32) for _ in range(2)]
    st2_2 = [ph.sb([128, 16], F32) for _ in range(2)]
    qn_2 = [ph.sb([128, 1536], F32) for _ in range(2)]
    qr2_2 = [ph.sb([128, 512], F32) for _ in range(2)]
    r1_2 = [ph.sb([128, 256], F32) for _ in range(2)]
    r2_2 = [ph.sb([128, 256], F32) for _ in range(2)]
    qnTs_2 = [ph.sb([128, 8, 128], BF16) for _ in range(2)]
    qrTs_2 = [ph.sb([128, 4, 128], BF16) for _ in range(2)]
    st3_2 = [ph.sb([128, 8], F32) for _ in range(2)]
    kn_2 = [ph.sb([128, 1024], F32) for _ in range(2)]
    knTs_2 = [ph.sb([128, 8, 128], BF16) for _ in range(2)]
    vs_2 = [ph.sb([128, 1024], BF16) for _ in range(2)]
    qsc = ph.sb([128, 1], F32)
    ts(ph, "dve", qsc[:], qnn, SCALE_MLA, ALU.mult, [], [qsc])
    hv = hT.rearrange("(k p) t -> p k t", p=128)
    for gi, g0 in enumerate(range(0, NT, 512)):
        gn = min(512, NT - g0)
        H = hb[gi % 2]
        ph.dma("sp", H[:, :, 0:gn], hv[:, :, g0:g0 + gn], writes=[H])
        for m0 in range(0, gn, 128):
            t0 = g0 + m0
            par = (t0 // 128) % 2
            st = st_2[par]
            cqn = cqn_2[par]
            ckvn = ckvn_2[par]
            kpe = kpe_2[par]
            kpe2 = kpe2_2[par]
            cqT = cqT_2[par]
            ckvT = ckvT_2[par]
            kpTs = kpTs_2[par]
            cs = cs_2[par]
            sq = sq_2[par]
            st2 = st2_2[par]
            qn = qn_2[par]
            qr2 = qr2_2[par]
            r1 = r1_2[par]
            r2 = r2_2[par]
            qnTs = qnTs_2[par]
            qrTs = qrTs_2[par]
            st3 = st3_2[par]
            kn = kn_2[par]
            knTs = knTs_2[par]
            vs = vs_2[par]
            ps = [ph.ps[(i + 4 * par) % 8] for i in range(8)]
            latent = t0 >= LC
            need_q = latent or want_ctx_q
            mm_group(ph, ps[0], ps[0][:, :], [(H[:, k, m0:m0 + 128], wdn[:, k, 0:512]) for k in range(8)], [H, wdn])
            mm_group(ph, ps[1], ps[1][:, 0:256], [(H[:, k, m0:m0 + 128], wdn[:, k, 512:768]) for k in range(8)], [H, wdn])
            act(ph, sq[:, 0:384], ps[0][:, 0:384], AF.Square, [ps[0]], [sq, st], accum_out=st[:, 0:1])
            act(ph, sq[:, 384:448], ps[0][:, 384:448], AF.Square, [ps[0]], [sq, st], accum_out=st[:, 1:2])
            act(ph, sq[:, 512:768], ps[1][:, 0:256], AF.Square, [ps[1]], [sq, st], accum_out=st[:, 2:3])
            rstd_inplace(ph, G, st, st[:, 0:1], 384)
            rstd_inplace(ph, G, st, st[:, 1:2], 64)
            rstd_inplace(ph, G, st, st[:, 2:3], 256)

            ts(ph, "dve", cqn[:], ps[0][:, 0:384], st[:, 0:1], ALU.mult, [ps[0], st], [cqn])
            ts(ph, "dve", ckvn[:], ps[1][:, 0:256], st[:, 2:3], ALU.mult, [ps[1], st], [ckvn])
            stt(ph, kpe[:, 0:64], ps[0][:, 384:448], st[:, 1:2], grep_[:, 64:128], ALU.mult, ALU.mult, [ps[0], st, grep_], [kpe])

            if latent:
                ph.dma("sp", cs[:, 0, :], cos_d[t0 - LC:t0 - LC + 128, :], writes=[cs])
                ph.dma("sp", cs[:, 1, :], sin_d[t0 - LC:t0 - LC + 128, :], writes=[cs])
                cosv = cs[:, 0, :].rearrange("p (a f) -> p a f", a=2)
                sinv = cs[:, 1, :].rearrange("p (a f) -> p a f", a=2)
                kv5 = kpe[:, 0:64].rearrange("p (a h f) -> p a h f", a=2, h=2)
                ko5 = kpe2[:, 0:64].rearrange("p (a h f) -> p a h f", a=2, h=2)
                x1, x2 = kv5[:, :, 0, :], kv5[:, :, 1, :]
                a1 = r1[:, 0:32].rearrange("p (a f) -> p a f", a=2)
                a2 = r2[:, 0:32].rearrange("p (a f) -> p a f", a=2)
                tt(ph, "pool", a1, x1, cosv, ALU.mult, [kpe, cs], [r1])
                tt(ph, "pool", a2, x2, sinv, ALU.mult, [kpe, cs], [r2])
                tt(ph, "pool", ko5[:, :, 0, :], a1, a2, ALU.subtract, [r1, r2], [kpe2])
                tt(ph, "pool", a1, x1, sinv, ALU.mult, [kpe, cs], [r1])
                tt(ph, "pool", a2, x2, cosv, ALU.mult, [kpe, cs], [r2])
                tt(ph, "pool", ko5[:, :, 1, :], a1, a2, ALU.add, [r1, r2], [kpe2])
                kp_src = kpe2
            else:
                kp_src = kpe

            transposes(ph, ps[2], [(ps[2][:, kk * 128:(kk + 1) * 128], cqn[:, kk * 128:(kk + 1) * 128], 128) for kk in range(3)],
                       G.ident, [cqn])
            transposes(ph, ps[3], [(ps[3][:, kk * 128:(kk + 1) * 128], ckvn[:, kk * 128:(kk + 1) * 128], 128) for kk in range(2)]
                       + [(ps[3][:, 256:384], kp_src[:, :], 128)], G.ident, [ckvn, kp_src])

            for kk in range(3):
                ts(ph, "dve", cqT[:, kk, :], ps[2][:, kk * 128:(kk + 1) * 128], qln[:, kk:kk + 1], ALU.mult, [ps[2]], [cqT])
            for kk in range(2):
                ts(ph, "dve", ckvT[:, kk, :], ps[3][:, kk * 128:(kk + 1) * 128], kvln[:, kk:kk + 1], ALU.mult, [ps[3]], [ckvT])

            cp(ph, "dve", kpTs[:, :], ps[3][0:64, 256:384], [ps[3]], [kpTs])

            ph.dma("pool", kpT[:, t0:t0 + 128], kpTs[:, :], reads=[kpTs])

            if need_q:
                for nb in range(3):
                    mm_group(ph, ps[4 + nb], ps[4 + nb][:, :], [(cqT[:, kk, :], wuq[:, kk, nb * 512:(nb + 1) * 512]) for kk in range(3)],
                             [cqT, wuq])
                for nb in range(3):
                    act(ph, sq[:, nb * 512:(nb + 1) * 512], ps[4 + nb][:, :], AF.Square, [ps[4 + nb]], [sq])
                ph.op("dve", lambda e, st2=st2, sq=sq: e.tensor_reduce(out=st2[:, 0:8], in_=sq[:, 0:1024].rearrange("p (h d) -> p h d", h=8),
                                                       axis=AX.X, op=ALU.add), reads=[sq], writes=[st2])
                ph.op("dve", lambda e, st2=st2, sq=sq: e.tensor_reduce(out=st2[:, 8:16], in_=sq[:, 1024:1536].rearrange("p (h d) -> p h d", h=8),
                                                       axis=AX.X, op=ALU.add), reads=[sq], writes=[st2])
                rstd_inplace(ph, G, st2, st2[:, 0:8], 128)
                rstd_inplace(ph, G, st2, st2[:, 8:16], 64)
                for nb in range(2):
                    tt(ph, "dve", qn[:, nb * 512:(nb + 1) * 512].rearrange("p (h d) -> p h d", h=4),
                       ps[4 + nb][:, :].rearrange("p (h d) -> p h d", h=4),
                       st2[:, nb * 4:(nb + 1) * 4].unsqueeze(2).broadcast_to([128, 4, 128]), ALU.mult, [ps[4 + nb], st2], [qn])
                qrv = qn[:, 1024:1536].rearrange("p (h d) -> p h d", h=8)
                tt(ph, "dve", qrv, ps[6][:, :].rearrange("p (h d) -> p h d", h=8),
                   st2[:, 8:16].unsqueeze(2).broadcast_to([128, 8, 64]), ALU.mult, [ps[6], st2], [qn])
                tt(ph, "pool", qrv, qrv, grep_[:, 0:64].unsqueeze(1).broadcast_to([128, 8, 64]), ALU.mult, [qn, grep_], [qn])
                if latent:
                    q5 = qn[:, 1024:1536].rearrange("p (h a s f) -> p h a s f", h=8, a=2, s=2)
                    o5 = qr2[:, :].rearrange("p (h a s f) -> p h a s f", h=8, a=2, s=2)
                    cosb = cs[:, 0, :].rearrange("p (a f) -> p a f", a=2).unsqueeze(1).broadcast_to([128, 8, 2, 16])
                    sinb = cs[:, 1, :].rearrange("p (a f) -> p a f", a=2).unsqueeze(1).broadcast_to([128, 8, 2, 16])
                    x1, x2 = q5[:, :, :, 0, :], q5[:, :, :, 1, :]
                    a1 = r1[:, :].rearrange("p (h a f) -> p h a f", h=8, a=2)
                    a2 = r2[:, :].rearrange("p (h a f) -> p h a f", h=8, a=2)
                    tt(ph, "pool", a1, x1, cosb, ALU.mult, [qn, cs], [r1])
                    tt(ph, "pool", a2, x2, sinb, ALU.mult, [qn, cs], [r2])
                    tt(ph, "pool", o5[:, :, :, 0, :], a1, a2, ALU.subtract, [r1, r2], [qr2])
                    tt(ph, "pool", a1, x1, sinb, ALU.mult, [qn, cs], [r1])
                    tt(ph, "pool", a2, x2, cosb, ALU.mult, [qn, cs], [r2])
                    tt(ph, "pool", o5[:, :, :, 1, :], a1, a2, ALU.add, [r1, r2], [qr2])
                    qr_src, qr_t = qr2[:, :], qr2
                else:
                    qr_src, qr_t = qn[:, 1024:1536], qn
                for hb_ in range(2):
                    P = ps[hb_]
                    transposes(ph, P, [(P[:, q * 128:(q + 1) * 128], qn[:, (hb_ * 4 + q) * 128:(hb_ * 4 + q + 1) * 128], 128)
                                       for q in range(4)], G.ident, [qn])
                    ts(ph, "dve", qnTs[:, hb_ * 4:(hb_ + 1) * 4, :], P[:, :].rearrange("p (h t) -> p h t", h=4), qsc[:, 0:1],
                       ALU.mult, [P, qsc], [qnTs])
                P = ps[2]
                transposes(ph, P, [(P[:, q * 128:(q + 1) * 128], qr_src[:, q * 128:(q + 1) * 128], 128) for q in range(4)],
                           G.ident, [qr_t])
                ts(ph, "dve", qrTs[:, :, :], P[:, :].rearrange("p (g t) -> p g t", g=4), SCALE_MLA, ALU.mult, [P], [qrTs])
                ph.dma("pool", qnT[:, :, t0:t0 + 128].rearrange("h p t -> p h t"), qnTs[:, :, :], reads=[qnTs])
                ph.dma("pool", qrT.rearrange("h d t -> (h d) t").rearrange("(g q) t -> q g t", q=128)[:, :, t0:t0 + 128], qrTs[:, :, :], reads=[qrTs])

            kb = [ps[4], ps[5], ps[6], ps[7]]
            for nb in range(4):
                mm_group(ph, kb[nb], kb[nb][:, :], [(ckvT[:, kk, :], wukv[:, kk, nb * 512:(nb + 1) * 512]) for kk in range(2)],
                         [ckvT, wukv])
            for nb in range(2):
                act(ph, sq[:, nb * 512:(nb + 1) * 512], kb[nb][:, :], AF.Square, [kb[nb]], [sq])
            ph.op("dve", lambda e, st3=st3, sq=sq: e.tensor_reduce(out=st3[:, 0:8], in_=sq[:, 0:1024].rearrange("p (h d) -> p h d", h=8),
                                                   axis=AX.X, op=ALU.add), reads=[sq], writes=[st3])
            rstd_inplace(ph, G, st3, st3[:, 0:8], 128)
            for nb in range(2):
                tt(ph, "dve", kn[:, nb * 512:(nb + 1) * 512].rearrange("p (h d) -> p h d", h=4),
                   kb[nb][:, :].rearrange("p (h d) -> p h d", h=4),
                   st3[:, nb * 4:(nb + 1) * 4].unsqueeze(2).broadcast_to([128, 4, 128]), ALU.mult, [kb[nb], st3], [kn])
                cp(ph, "act", vs[:, nb * 512:(nb + 1) * 512], kb[2 + nb][:, :], [kb[2 + nb]], [vs])
            ph.dma("pool", v_d[t0:t0 + 128, :], vs[:, :], reads=[vs])
            for hb_ in range(2):
                P = ps[hb_]
                transposes(ph, P, [(P[:, q * 128:(q + 1) * 128], kn[:, (hb_ * 4 + q) * 128:(hb_ * 4 + q + 1) * 128], 128)
                                   for q in range(4)], G.ident, [kn])
                ts(ph, "dve", knTs[:, hb_ * 4:(hb_ + 1) * 4, :], P[:, :].rearrange("p (h t) -> p h t", h=4), knn[:, 0:1],
                   ALU.mult, [P], [knTs])
            ph.dma("pool", knT[:, :, t0:t0 + 128].rearrange("h p t -> p h t"), knTs[:, :, :], reads=[knTs])
    ph.end()


def phase_mla_attn(K, G, qnT, qrT, knT, kpT, v_d, oT, want_ctx):
    ph = Phase(K, "ma")
    kp = ph.sb([64, NT], BF16)
    ph.dma("sp", kp[:], kpT, writes=[kp])
    kn = [ph.sb([128, NT], BF16) for _ in range(2)]
    vh = [ph.sb([128, NT // 128, 128], BF16) for _ in range(2)]
    qn = [ph.sb([128, NT], BF16) for _ in range(2)]
    qr = [ph.sb([64, NT], BF16) for _ in range(2)]
    pT = [ph.sb([128, 512], BF16) for _ in range(4)]
    rec = [ph.sb([128, 512], F32) for _ in range(2)]
    ob = [ph.sb([128, 512], BF16) for _ in range(2)]
    acc = [ph.sb([128, 512], F32) for _ in range(2)]
    ps = ph.ps
    it = 0
    qt = 0
    NCH = NT // 128
    for h in range(8):
        KN, VH, QN, QR = kn[h % 2], vh[h % 2], qn[h % 2], qr[h % 2]
        ph.dma("sp", KN[:], knT[h], writes=[KN])
        ph.dma("sp", VH[:], v_d[:, h * 128:(h + 1) * 128].rearrange("(c p) v -> p c v", p=128), writes=[VH])
        ph.dma("sp", QN[:], qnT[h], writes=[QN])
        ph.dma("sp", QR[:], qrT[h], writes=[QR])
        tiles = [(LC + i * 512, 512, NCH) for i in range(L // 512)]
        if want_ctx:
            tiles = [(0, LC, LC // 128)] + tiles
        for (q0, nq, nch) in tiles:
            O, Dn, Dn2 = ps[4 + qt % 2], ps[6], ps[7]
            ACC = acc[qt % 2]

            def qk(c, slot):
                S = ps[slot % 4]
                mm_group(ph, S, S[:, 0:nq], [(KN[:, c * 128:(c + 1) * 128], QN[:, q0:q0 + nq]),
                                             (kp[:, c * 128:(c + 1) * 128], QR[:, q0:q0 + nq])], [KN, QN, kp, QR])
            qk(0, it)
            for c in range(nch):
                S = ps[it % 4]
                PT = pT[it % 4]
                if c + 1 < nch:
                    qk(c + 1, it + 1)
                it += 1
                act(ph, PT[:, 0:nq], S[:, 0:nq], AF.Exp, [S], [PT])
                mm_group(ph, O, O[:, 0:nq], [(VH[:, c, :], PT[:, 0:nq])], [VH, PT], first=(c == 0), last=(c == nch - 1))
                if c % 2 == 1:
                    mm_group(ph, Dn, Dn[:, 0:nq], [(G.ones[:, :], PT[:, 0:nq])], [PT], first=(c == 1), last=(c >= nch - 2))
                elif c == 0:
                    cp(ph, "dve", ACC[:, 0:nq], PT[:, 0:nq], [PT], [ACC])
                else:
                    tt(ph, "dve", ACC[:, 0:nq], ACC[:, 0:nq], PT[:, 0:nq], ALU.add, [ACC, PT], [ACC])
            mm_group(ph, Dn2, Dn2[:, 0:nq], [(G.ones32[:, :], ACC[:, 0:nq])], [ACC])
            R, OB = rec[qt % 2], ob[qt % 2]
            qt += 1
            cp(ph, "dve", R[:, 0:nq], Dn2[:, 0:nq], [Dn2], [R])
            tt(ph, "dve", R[:, 0:nq], Dn[:, 0:nq], R[:, 0:nq], ALU.add, [Dn, R], [R])
            ph.op("dve", lambda e, R=R, nq=nq: e.reciprocal(out=R[:, 0:nq], in_=R[:, 0:nq]), reads=[R], writes=[R])
            tt(ph, "dve", OB[:, 0:nq], O[:, 0:nq], R[:, 0:nq], ALU.mult, [O, R], [OB])
            ph.dma("pool", oT[h * 128:(h + 1) * 128, q0:q0 + nq], OB[:, 0:nq], reads=[OB])
    ph.end()


def kmaj(w):
    Kd, N = w.shape
    return np.ascontiguousarray(w.reshape(Kd // 128, 128, N).transpose(1, 0, 2))


def fmaj(v):
    return np.ascontiguousarray(v.reshape(-1, 128).T)


def lay_mla(w_down, w_uq, w_ukv):
    wd = np.concatenate([w_down[:, 0:384], w_down[:, 640:704], np.zeros((1024, 64), w_down.dtype), w_down[:, 384:640]], axis=1)
    uq = w_uq.reshape(384, 8, 192)
    uq = np.concatenate([uq[:, :, 0:128].reshape(384, 1024), uq[:, :, 128:192].reshape(384, 512)], axis=1)
    ukv = w_ukv.reshape(256, 8, 256)
    ukv = np.concatenate([ukv[:, :, 0:128].reshape(256, 1024), ukv[:, :, 128:256].reshape(256, 1024)], axis=1)
    return kmaj(wd), kmaj(uq), kmaj(ukv)


def lay_ffn(w_up, w_down, conv_w, conv_b):
    wa = w_up[:, :DFF].reshape(8, 128, NJ, 128)
    wg = w_up[:, DFF:].reshape(8, 128, NJ, 128)
    wup_l = np.ascontiguousarray(np.concatenate([wa, wg], axis=3).transpose(1, 2, 0, 3))
    wdn_l = np.ascontiguousarray(w_down.reshape(NJ, 128, 1024).transpose(1, 0, 2))
    cw_l = np.stack([conv_w[:, :DFF].reshape(3, NJ, 128), conv_w[:, DFF:].reshape(3, NJ, 128)], axis=-1)
    cw_l = cw_l.transpose(2, 0, 1, 3).reshape(128, 3 * NJ * 2)
    cb_l = np.stack([conv_b[:DFF].reshape(NJ, 128), conv_b[DFF:].reshape(NJ, 128)], axis=-1).transpose(1, 0, 2).reshape(128, NJ * 2)
    return wup_l, wdn_l, np.ascontiguousarray(cw_l), np.ascontiguousarray(cb_l)


def rope_tables():
    t = np.arange(L)
    row = (t // 64).astype(np.float32)
    col = (t % 64).astype(np.float32)
    inv = (10000.0 ** (-np.arange(16, dtype=np.float32) / 16)).astype(np.float32)
    ang = np.stack([row[:, None] * inv, col[:, None] * inv], axis=1).astype(np.float32)
    return np.cos(ang).reshape(L, 32).astype(np.float32), np.sin(ang).reshape(L, 32).astype(np.float32)


NCH = None


def phase_gla_proj(K, G, hT, win_d, w1_d, w2_d, vec, qkT, gT, v_d, sr_d):
    ph = Phase(K, "gp")
    win = ph.sb([128, 8, 3072], BF16)
    for q in range(4):
        ph.dma("sp", win[:, q * 2:(q + 1) * 2, :], win_d[:, q * 2:(q + 1) * 2, :], writes=[win])
    w1 = ph.sb([128, 8, 32], BF16)
    ph.dma("sp", w1[:], w1_d, writes=[w1])
    w2f = ph.sb([16, 2, 512], F32)
    w2 = ph.sb([16, 2, 512], BF16)
    ph.dma("sp", w2f[:], w2_d, writes=[w2f])
    cp(ph, "dve", w2[:], w2f[:], [w2f], [w2])
    negb = ph.sb([128, 8], F32)
    ts(ph, "dve", negb[:], vec[:, 192:200], -1.0, ALU.mult, [], [negb])
    hb = [ph.sb([128, 8, 512], BF16) for _ in range(2)]
    stg = [ph.sb([128, 512], F32) for _ in range(3)]
    hw1 = [ph.sb([16, 512], BF16) for _ in range(2)]
    e1 = [ph.sb([128, 512], F32) for _ in range(2)]
    vs = [ph.sb([128, 1024], BF16) for _ in range(2)]
    srs = [ph.sb([128, 1024], F32) for _ in range(2)]
    hv = hT.rearrange("(k p) t -> p k t", p=128)
    ps = ph.ps
    pi = 0
    si = 0
    for gi, g0 in enumerate(range(0, NT, 512)):
        gn = min(512, NT - g0)
        H = hb[gi % 2]
        ph.dma("sp", H[:, :, 0:gn], hv[:, :, g0:g0 + gn], writes=[H])
        for fc in range(8):
            P = ps[pi % 8]
            pi += 1
            ST = stg[si % 3]
            si += 1
            mm_group(ph, P, P[:, 0:gn], [(win[:, k, fc * 128:(fc + 1) * 128], H[:, k, 0:gn]) for k in range(8)], [win, H])
            if fc < 4:
                ts(ph, "dve", ST[:, 0:gn], P[:, 0:gn], 128.0 ** -0.5, ALU.mult, [P], [ST])
            else:
                cp(ph, "act", ST[:, 0:gn], P[:, 0:gn], [P], [ST])
            ph.dma("pool", qkT[fc * 128:(fc + 1) * 128, g0:g0 + gn], ST[:, 0:gn], reads=[ST])
        for d in range(2):
            P = ps[pi % 8]
            pi += 1
            HW = hw1[d]
            mm_group(ph, P, P[0:16, 0:gn], [(w1[:, k, d * 16:(d + 1) * 16], H[:, k, 0:gn]) for k in range(8)], [w1, H])
            cp(ph, "dve", HW[:, 0:gn], P[0:16, 0:gn], [P], [HW])
            for h in range(4):
                P2 = ps[pi % 8]
                pi += 1
                ST = stg[si % 3]
                si += 1
                E = e1[h % 2]
                mm_group(ph, P2, P2[:, 0:gn], [(w2[:, d, h * 128:(h + 1) * 128], HW[:, 0:gn])], [w2, HW])
                act(ph, E[:, 0:gn], P2[:, 0:gn], AF.Exp, [P2, negb], [E], bias=negb[:, d * 4 + h:d * 4 + h + 1], scale=-1.0)
                act(ph, E[:, 0:gn], E[:, 0:gn], AF.Ln, [E], [E], bias=G.one1[:, :], scale=1.0)
                ts(ph, "dve", ST[:, 0:gn], E[:, 0:gn], -1.0 / 16.0, ALU.mult, [E], [ST])
                ph.dma("pool", gT[d, h * 128:(h + 1) * 128, g0:g0 + gn], ST[:, 0:gn], reads=[ST])
        for m0 in range(0, gn, 128):
            t0 = g0 + m0
            VS, SR = vs[(t0 // 128) % 2], srs[(t0 // 128) % 2]
            for nb in range(4):
                P = ps[pi % 8]
                pi += 1
                mm_group(ph, P, P[:, :], [(H[:, k, m0:m0 + 128], win[:, k, 1024 + nb * 512:1024 + (nb + 1) * 512]) for k in range(8)],
                         [H, win])
                if nb < 2:
                    cp(ph, "dve", VS[:, nb * 512:(nb + 1) * 512], P[:, :], [P], [VS])
                else:
                    act(ph, SR[:, (nb - 2) * 512:(nb - 1) * 512], P[:, :], AF.Silu, [P], [SR])
            ph.dma("pool", v_d[t0:t0 + 128, :], VS[:, :], reads=[VS])
            ph.dma("pool", sr_d[t0:t0 + 128, :], SR[:, :], reads=[SR])
    ph.end()


def phase_gla_scan(K, G, qkT, gT, v_d, o_d):
    ph = Phase(K, "gs")
    nch = NT // 64
    ncc = LC // 64
    q = ph.sb([128, NT], F32)
    k = ph.sb([128, NT], F32)
    g = ph.sb([128, NT], F32)
    Pc = ph.sb([128, NT], F32)
    E = ph.sb([128, NT], F32, nsub=2)
    Eall = [E] + E.sub
    qd = ph.sb([128, NT], BF16)
    ki = ph.sb([128, NT], BF16)
    kitok = ph.sb([64, nch, 128], BF16)
    vh = ph.sb([64, nch, 256], BF16)
    dec = ph.sb([128, nch], F32)
    S = ph.sb([128, 256], F32)
    Sb = ph.sb([128, 256], BF16)
    SCall = ph.sb([64, nch, 64], BF16)
    ps = ph.ps
    it = 0
    odb = [Buf() for _ in range(nch)]
    for h in range(4):
        ph.dma("sp", q[:], qkT[h * 128:(h + 1) * 128, :], writes=[q])
        ph.dma("sp", k[:], qkT[512 + h * 128:512 + (h + 1) * 128, :], writes=[k])
        ph.dma("sp", vh[:], v_d[:, h * 256:(h + 1) * 256].rearrange("(c p) v -> p c v", p=64), writes=[vh])
        for d in range(2):
            ph.dma("sp", g[:], gT[d, h * 128:(h + 1) * 128, :], writes=[g])
            for c in range(nch):
                ph.op("dve", lambda e, c=c: e.tensor_tensor_scan(out=Pc[:, c * 64:(c + 1) * 64], data0=G.onesf[:, 0:64],
                                                                 data1=g[:, c * 64:(c + 1) * 64], initial=0.0,
                                                                 op0=ALU.mult, op1=ALU.add), reads=[g], writes=[Pc])
            P3 = Pc[:].rearrange("p (c t) -> p c t", t=64)
            tot = P3[:, :, 63:64]
            act(ph, dec[:].unsqueeze(2), tot, AF.Exp, [Pc], [dec])
            if d == 0:
                bq, bq_t = Pc, Pc
            else:
                g3 = g[:].rearrange("p (c t) -> p c t", t=64)
                tt(ph, "pool", g[:], g[:], Pc[:], ALU.subtract, [g, Pc], [g])
                tt(ph, "pool", g3, g3, tot.broadcast_to([128, nch, 64]), ALU.add, [g, Pc], [g])
                bq, bq_t = g, g
            act(ph, E[:], bq[:], AF.Exp, [bq_t], Eall)
            tt(ph, "dve", qd[:], q[:], E[:], ALU.mult, [q] + Eall, [qd])
            act(ph, E[:], bq[:], AF.Exp, [bq_t], Eall, scale=-1.0)
            tt(ph, "dve", ki[:], k[:], E[:], ALU.mult, [k] + Eall, [ki])
            for c0 in range(0, nch, 8):
                nb = min(8, nch - c0)
                P = ps[(c0 // 8) % 2]
                Pb = P[:, :].bitcast(BF16)
                transposes(ph, P, [(Pb[0:64, j * 128:(j + 1) * 128], ki[:, (c0 + j) * 64:(c0 + j + 1) * 64], 128) for j in range(nb)],
                           G.identb, [ki])
                cp(ph, "dve", kitok[:, c0:c0 + nb, :], Pb[0:64, 0:nb * 128].rearrange("p (j x) -> p j x", j=nb), [P], [kitok])
            memset(ph, "dve", S[:], 0.0, [S])
            memset(ph, "dve", Sb[:], 0.0, [Sb])
            order = list(range(nch)) if d == 0 else (list(range(ncc - 1, -1, -1)) + list(range(nch - 1, ncc - 1, -1)))
            mask = G.mask_f if d == 0 else G.mask_b
            for c0 in range(0, nch, 8):
                nb = min(8, nch - c0)
                P = ps[(c0 // 8) % 2]

                def fn(e, P=P, c0=c0, nb=nb):
                    ins = None
                    for j in range(nb):
                        cs_ = slice((c0 + j) * 64, (c0 + j + 1) * 64)
                        ins = e.matmul(P[0:64, j * 64:(j + 1) * 64], ki[:, cs_], qd[:, cs_], start=True, stop=True)
                    return ins
                ph.op("pe", fn, reads=[ki, qd], writes=[P])
                tt(ph, "dve", SCall[:, c0:c0 + nb, :], P[0:64, 0:nb * 64].rearrange("p (j t) -> p j t", j=nb),
                   mask[0:64, 0:64].unsqueeze(1).broadcast_to([64, nb, 64]), ALU.mult, [P], [SCall])
            c_first = order[0]
            Cf = ps[6 + it % 2]
            mm_group(ph, Cf, Cf[:, 0:256], [(kitok[:, c_first, :], vh[:, c_first, :])], [kitok, vh])
            prev_c = None
            pend = None
            if d == 0:
                groups = [list(range(0, ncc))] + [list(range(a, a + 8)) for a in range(ncc, nch, 8)]
            else:
                groups = [list(range(ncc - 1, -1, -1))] + [list(range(a + 7, a - 1, -1)) for a in range(nch - 8, ncc - 1, -8)]
            assert [c for g_ in groups for c in g_] == order
            gidx = {}
            for gi_, g_ in enumerate(groups):
                for c in g_:
                    gidx[c] = (gi_, c - min(g_), min(g_), len(g_))

            def flush(B, c, h=h, d=d):
                gi_, slot, cmin, glen = gidx[c]
                stage = E[0:64, (gi_ % 2) * 2048:(gi_ % 2) * 2048 + 2048].rearrange("p (j v) -> p j v", j=8)
                eb = E.sub[gi_ % 2]
                cp(ph, "dve", stage[:, slot, :], B[0:64, 0:256], [B], [eb])
                if c == groups[gi_][-1]:
                    dst = o_d[cmin * 64:(cmin + glen) * 64, h * 256:(h + 1) * 256].rearrange("(j p) v -> p j v", p=64)
                    dbufs = [odb[x] for x in groups[gi_]]
                    if d == 0:
                        ph.dma("sp", dst, stage[:, 0:glen, :], reads=[eb], writes=dbufs)
                    else:
                        ph.dma("pool", dst, stage[:, 0:glen, :], reads=[eb], writes=dbufs, accum_op=ALU.add)
            for i_, c in enumerate(order):
                cs_ = slice(c * 64, (c + 1) * 64)
                B, C = ps[2 + it % 3], ps[6 + it % 2]
                it += 1
                mm_group(ph, B, B[0:64, 0:256], [(SCall[:, c, :], vh[:, c, :]), (qd[:, cs_], Sb[:, :])], [SCall, vh, qd, Sb])
                if i_ + 1 < len(order):
                    cn = order[i_ + 1]
                    Cn = ps[6 + it % 2]
                    mm_group(ph, Cn, Cn[:, 0:256], [(kitok[:, cn, :], vh[:, cn, :])], [kitok, vh])
                cprev = c if prev_c is None else prev_c
                stt(ph, S[:], S[:], dec[:, cprev:cprev + 1], C[:, 0:256], ALU.mult, ALU.add, [S, dec, C], [S])
                act(ph, Sb[:], S[:], AF.Copy, [S, dec], [Sb], scale=dec[:, c:c + 1])
                prev_c = c
                if pend is not None:
                    flush(*pend)
                pend = (B, c)
            flush(*pend)
    ph.end()


def phase_gla_out(K, G, o_d, sr_d, vec, ogT):
    ph = Phase(K, "go")
    ob = [ph.sb([128, 1024], F32) for _ in range(2)]
    sb_ = [ph.sb([128, 1024], F32) for _ in range(2)]
    sq = ph.sb([128, 1024], F32)
    st = [ph.sb([128, 4], F32) for _ in range(2)]
    on = [ph.sb([128, 1024], F32) for _ in range(2)]
    og = [ph.sb([128, 8, 512], BF16) for _ in range(2)]
    ogv = ogT.rearrange("(k p) t -> p k t", p=128)
    onorm = vec[:, 200:202]
    ps = ph.ps
    ti = 0
    for gi, g0 in enumerate(range(0, NT, 512)):
        gn = min(512, NT - g0)
        OG = og[gi % 2]
        for m0 in range(0, gn, 128):
            t0 = g0 + m0
            O, SR, ST, ON = ob[ti % 2], sb_[ti % 2], st[ti % 2], on[ti % 2]
            ph.dma("sp", O[:], o_d[t0:t0 + 128, :], writes=[O])
            ph.dma("sp", SR[:], sr_d[t0:t0 + 128, :], writes=[SR])
            act(ph, sq[:], O[:], AF.Square, [O], [sq])
            ph.op("dve", lambda e, ST=ST: e.tensor_reduce(out=ST[:, 0:4], in_=sq[:, :].rearrange("p (h d) -> p h d", h=4),
                                                          axis=AX.X, op=ALU.add), reads=[sq], writes=[ST])
            rstd_inplace(ph, G, ST, ST[:, 0:4], 256)
            tt(ph, "dve", ON[:].rearrange("p (h d) -> p h d", h=4), O[:].rearrange("p (h d) -> p h d", h=4),
               ST[:, 0:4].unsqueeze(2).broadcast_to([128, 4, 256]), ALU.mult, [O, ST], [ON])
            tt(ph, "pool", ON[:], ON[:], SR[:], ALU.mult, [ON, SR], [ON])
            for half in range(2):
                P = ps[(ti * 2 + half) % 8]
                transposes(ph, P, [(P[:, q * 128:(q + 1) * 128], ON[:, (half * 4 + q) * 128:(half * 4 + q + 1) * 128], 128)
                                   for q in range(4)], G.ident, [ON])
                Pv = P[:, :].rearrange("p (h s t) -> p h s t", h=2, s=2)
                for s_ in range(2):
                    ts(ph, "dve", OG[:, half * 4:(half + 1) * 4, m0:m0 + 128].rearrange("p (h s) t -> p h s t", s=2)[:, :, s_, :],
                       Pv[:, :, s_, :], onorm[:, s_:s_ + 1], ALU.mult, [P], [OG])
            ti += 1
        ph.dma("pool", ogv[:, :, g0:g0 + gn], OG[:, :, 0:gn], reads=[OG])
    ph.end()


def lay_gla(w_in, w1, w2):
    w1c = np.concatenate([w1[0], w1[1]], axis=1)
    w2l = np.ascontiguousarray(w2.transpose(1, 0, 2))
    return kmaj(w_in), kmaj(w1c), w2l


def gla_masks():
    s_ = np.arange(64)[:, None]
    t_ = np.arange(64)[None, :]
    return np.stack([(t_ >= s_), (t_ <= s_)]).astype(np.float32)


def phase_ada(K, G, condT, wada_d, bada_d, m_dram, mT, modv, vec):
    ph = Phase(K, "ad")
    msb = ph.sb([4, 6144], F32)
    bt = ph.sb([4, 6144], F32)
    ph.dma("sp", bt[:], bada_d.partition_broadcast(4), writes=[bt])
    wt = [ph.sb([128, 8, 512], F32) for _ in range(2)]
    ps = ph.ps
    for n in range(12):
        W = wt[n % 2]
        ph.dma("sp", W[:], wada_d[:, :, n * 512:(n + 1) * 512], writes=[W])
        P = ps[n % 2]
        mm_group(ph, P, P[0:4, :], [(condT[:, k, :], W[:, k, :]) for k in range(8)], [W])
        tt(ph, "dve", msb[:, n * 512:(n + 1) * 512], P[0:4, :], bt[:, n * 512:(n + 1) * 512], ALU.add, [P, bt], [msb])
    ph.dma("sp", m_dram[:, :], msb[:], reads=[msb])
    PT = ps[2]
    transposes(ph, PT, [(PT[:, c * 4:(c + 1) * 4], msb[0:4, c * 128:(c + 1) * 128], 4) for c in range(48)], G.ident, [msb])
    MT = T(mT)
    cp(ph, "dve", mT[:].rearrange("p c j -> p (c j)"), PT[:, 0:192], [PT], [MT])
    MV = T(modv)
    for c in range(3):
        for (slot, jsc, jsh, goff) in ((0, 1, 0, 0), (2, 4, 3, 8)):
            ts(ph, "dve", modv[:, c, slot, :], mT[:, jsc * 8:(jsc + 1) * 8, c], 1.0, ALU.add, [MT], [MV])
            tt(ph, "dve", modv[:, c, slot, :], modv[:, c, slot, :], vec[:, goff:goff + 8], ALU.mult, [MV], [MV])
            cp(ph, "dve", modv[:, c, slot + 1, :], mT[:, jsh * 8:(jsh + 1) * 8, c], [MT], [MV])
    ph.end()


def cast_dram(ph, dst2, src2):
    F = src2.shape[1]
    for a in range(0, F, 8192):
        b = min(F, a + 8192)
        ph.dma("pool", dst2[:, a:b], src2[:, a:b], max_dma_last_dim=8192)


def build_program():
    nc = bass.Bass("TRN2", target_bir_lowering=False)

    def din(n, s, t=F32):
        return nc.dram_tensor(n, list(s), t, kind="ExternalInput").ap()

    def dsc(n, s, t):
        return nc.dram_tensor(n, list(s), t, kind="Internal").ap()

    x = din("x", [2, L, D])
    ctx = din("ctx", [2, LC, D])
    condT_d = din("condT", [128, 8, 4])
    wada = din("wada", [4, 128, 8, 6144])
    bada = din("bada", [4, 6144])
    vecs = din("vecs", [4, 128, 224])
    mrows = din("mrows", [2, 128])
    cos_d = din("cos", [L, 32])
    sin_d = din("sin", [L, 32])
    ident_d = din("ident", [128, 128])
    masks_d = din("masks", [2, 64, 64])
    gw2 = din("gw2", [2, 16, 2, 512])
    wspec = {"gwin": [2, 128, 8 * 3072], "gw1": [2, 128, 8 * 32], "gwout": [2, 128, 8 * 1024],
             "mwdn": [2, 128, 8 * 768], "mwuq": [2, 128, 3 * 1536], "mwukv": [2, 128, 2 * 2048], "mwout": [2, 128, 8 * 1024],
             "fwup": [4, 128, NJ * 8 * 256], "fwdn": [4, 128, NJ * 1024]}
    wf = {n: din(n, s) for n, s in wspec.items()}
    wb = {n: dsc(n + "_b", s, BF16) for n, s in wspec.items()}
    y = nc.dram_tensor("y", [2, L, D], F32, kind="ExternalOutput").ap()
    xc = dsc("xc", [2, LC, D], F32)
    m_dram = dsc("m_dram", [4, 6144], F32)
    hT = dsc("hT", [D, NT], BF16)
    oT = dsc("oT", [D, NT], BF16)
    qnT = dsc("qnT", [8, 128, NT], BF16)
    qrT = dsc("qrT", [8, 64, NT], BF16)
    knT = dsc("knT", [8, 128, NT], BF16)
    kpT = dsc("kpT", [64, NT], BF16)
    v_d = dsc("v_d", [NT, D], BF16)
    qkT = dsc("qkT", [D, NT], F32)
    gT = dsc("gT", [2, 512, NT], F32)
    sr_d = dsc("sr_d", [NT, D], F32)
    o_d = dsc("o_d", [NT, D], F32)

    K = Kern(nc)
    G = Glob(K, ident_d, masks_d)
    condT = nc.alloc_sbuf_tensor("condT_sb", [128, 8, 4], F32)
    vec = nc.alloc_sbuf_tensor("vec_sb", [128, 224], F32)
    mT = nc.alloc_sbuf_tensor("mT_sb", [128, 48, 4], F32)
    modv = nc.alloc_sbuf_tensor("modv_sb", [128, 3, 4, 8], F32)

    ph = Phase(K, "pro")
    CT = T(condT)
    ph.dma("sp", condT[:], condT_d, writes=[CT])
    act(ph, condT[:], condT[:], AF.Silu, [CT], [CT])
    for b in range(2):
        for r0 in range(0, L, 1024):
            ph.dma("sp", y[b, r0:r0 + 1024, :], x[b, r0:r0 + 1024, :])
        ph.dma("sp", xc[b], ctx[b])
    for n in wspec:
        for i in range(wspec[n][0]):
            cast_dram(ph, wb[n][i], wf[n][i])
    ph.end()

    for l in range(4):
        j = l // 2
        last = l == 3
        ph = Phase(K, "lv")
        ph.dma("sp", vec[:], vecs[l])
        ph.end()
        phase_ada(K, G, condT, wada[l], bada[l:l + 1, :], m_dram, mT, modv, vec)
        for b in range(2):
            phase_norm(K, G, xc[b], LC, hT, 0, modv[:, 2, 0, :], modv[:, 2, 1, :])
            phase_norm(K, G, y[b], L, hT, LC, modv[:, b, 0, :], modv[:, b, 1, :])
            if l % 2 == 0:
                phase_gla_proj(K, G, hT, wb["gwin"][j].rearrange("p (k n) -> p k n", k=8),
                               wb["gw1"][j].rearrange("p (k n) -> p k n", k=8), gw2[j], vec, qkT, gT, v_d, sr_d)
                phase_gla_scan(K, G, qkT, gT, v_d, o_d)
                phase_gla_out(K, G, o_d, sr_d, vec, oT)
                wo = wb["gwout"][j]
            else:
                phase_mla_qkv(K, G, l, hT, wb["mwdn"][j].rearrange("p (k n) -> p k n", k=8),
                              wb["mwuq"][j].rearrange("p (k n) -> p k n", k=3),
                              wb["mwukv"][j].rearrange("p (k n) -> p k n", k=2), vec, mrows[j:j + 1, :], cos_d, sin_d,
                              qnT, qrT, knT, kpT, v_d, not last)
                phase_mla_attn(K, G, qnT, qrT, knT, kpT, v_d, oT, not last)
                wo = wb["mwout"][j]
            jobs = [(oT[:, LC:NT], L, y[b], m_dram[b:b + 1, 2048:3072])]
            if not last:
                jobs.append((oT[:, 0:LC], LC, xc[b], m_dram[2:3, 2048:3072]))
            phase_outproj(K, G, wo.rearrange("p (k n) -> p k n", k=8), jobs)
            phase_norm(K, G, y[b], L, hT, LC, modv[:, b, 2, :], modv[:, b, 3, :])
            jobs = [(hT[:, LC:NT], L, y[b], m_dram[b:b + 1, 5120:6144])]
            if not last:
                phase_norm(K, G, xc[b], LC, hT, 0, modv[:, 2, 2, :], modv[:, 2, 3, :])
                jobs.append((hT[:, 0:LC], LC, xc[b], m_dram[2:3, 5120:6144]))
            cw = vec[:, 16:148].rearrange("p (a j c) -> p a j c", a=3, j=NJ)
            cb = vec[:, 148:192].rearrange("p (j c) -> p j c", j=NJ)
            phase_ffn(K, G, wb["fwup"][l].rearrange("p (j k c) -> p j k c", j=NJ, k=8),
                      wb["fwdn"][l].rearrange("p (j n) -> p j n", j=NJ), cw, cb, jobs)
    return nc


def host_inputs(inp):
    f = lambda a: np.ascontiguousarray(np.asarray(a, dtype=np.float32))
    g = {k: f(v) for k, v in inp.items()}
    shared = {}
    shared["wada"] = np.ascontiguousarray(g["w_ada"].reshape(4, 8, 128, 6144).transpose(0, 2, 1, 3))
    shared["bada"] = g["b_ada"]
    vecs = np.zeros((4, 128, 224), np.float32)
    fwup, fwdn = [], []
    for l in range(4):
        wup_l, wdn_l, cw_l, cb_l = lay_ffn(g["ffn_w_up"][l], g["ffn_w_down"][l], g["ffn_conv_w"][l], g["ffn_conv_b"][l])
        fwup.append(wup_l.reshape(128, -1))
        fwdn.append(wdn_l.reshape(128, -1))
        vecs[l, :, 0:8] = fmaj(g["norm_mix"][l])
        vecs[l, :, 8:16] = fmaj(g["norm_ffn"][l])
        vecs[l, :, 16:148] = cw_l
        vecs[l, :, 148:192] = cb_l
        j = l // 2
        if l % 2 == 0:
            vecs[l, :, 192:200] = g["gla_gate_b"][j].reshape(2, 4, 128).transpose(2, 0, 1).reshape(128, 8)
            vecs[l, :, 200:202] = fmaj(g["gla_out_norm"][j])
        else:
            vecs[l, :, 192:195] = fmaj(g["mla_q_lora_norm"][j])
            vecs[l, :, 195:197] = fmaj(g["mla_kv_lora_norm"][j])
            vecs[l, :, 197] = g["mla_q_norm"][j][:128]
            vecs[l, :, 198] = g["mla_k_norm"][j][:128]
    shared["vecs"] = vecs
    shared["fwup"] = np.stack(fwup)
    shared["fwdn"] = np.stack(fwdn)
    shared["mrows"] = np.ascontiguousarray(np.concatenate([g["mla_q_norm"][:, 128:], g["mla_k_norm"][:, 128:]], axis=1))
    gl = [lay_gla(g["gla_w_in"][j], g["gla_gate_w1"][j], g["gla_gate_w2"][j]) for j in range(2)]
    shared["gwin"] = np.stack([a[0].reshape(128, -1) for a in gl])
    shared["gw1"] = np.stack([a[1].reshape(128, -1) for a in gl])
    shared["gw2"] = np.stack([a[2] for a in gl])
    shared["gwout"] = np.stack([kmaj(g["gla_w_out"][j]).reshape(128, -1) for j in range(2)])
    ml = [lay_mla(g["mla_w_down"][j], g["mla_w_uq"][j], g["mla_w_ukv"][j]) for j in range(2)]
    shared["mwdn"] = np.stack([a[0].reshape(128, -1) for a in ml])
    shared["mwuq"] = np.stack([a[1].reshape(128, -1) for a in ml])
    shared["mwukv"] = np.stack([a[2].reshape(128, -1) for a in ml])
    shared["mwout"] = np.stack([kmaj(g["mla_w_out"][j]).reshape(128, -1) for j in range(2)])
    cos, sin = rope_tables()
    shared["cos"], shared["sin"] = cos, sin
    shared["ident"] = np.eye(128, dtype=np.float32)
    shared["masks"] = gla_masks()
    maps = []
    for c in range(8):
        m = dict(shared)
        m["x"] = g["x"][2 * c:2 * c + 2]
        m["ctx"] = g["ctx"][2 * c:2 * c + 2]
        cond = np.zeros((4, D), np.float32)
        cond[0:2] = g["c"][2 * c:2 * c + 2]
        cond[2] = g["c_ctx"]
        m["condT"] = np.ascontiguousarray(cond.reshape(4, 8, 128).transpose(2, 1, 0))
        maps.append(m)
    return maps


def kernel(**inputs):
    maps = host_inputs(inputs)
    nc = build_program()
    res = run_bass_kernel_spmd(nc, maps, core_ids=list(range(8)))
    return np.concatenate([np.asarray(r["y"], dtype=np.float32) for r in res.results], axis=0)
```
